# Optimizing a Trainium2 kernel written in Bass

```python
import math
import jax, jax.numpy as jnp
from jax import lax
import numpy as np

D_MODEL = 1024
BATCH = 8
SEQ = 2048
DEPTH = 1
DEC_BATCH = 16
DEC_SEQ = 2048
PAST_LEN = 128

GRID_W = 64
SSM_WIDTH = D_MODEL // 2
SSM_GROUP_CH = 16
N_SSM_GROUPS = SSM_WIDTH // SSM_GROUP_CH
SSM_STATE = 64
ATTN_WIDTH = D_MODEL - SSM_WIDTH
HEAD_DIM = 64
N_HEADS_ATTN = ATTN_WIDTH // HEAD_DIM
MIX_WIDTH = SSM_WIDTH + ATTN_WIDTH
IN_PROJ_WIDTH = SSM_WIDTH + 3 * ATTN_WIDTH
NA_WIN_ROWS = 8
NA_WIN_COLS = 16
D_FF = -(-8 * D_MODEL // (3 * 256)) * 256
N_MOD = 6
EPS = 1e-6

kernel_name = "hymba_s5_natten_adaln_encoder"


def rms_norm(x, gain):
    xf = x.astype(jnp.float32)
    y = xf * lax.rsqrt(jnp.mean(xf * xf, axis=-1, keepdims=True) + EPS)
    return (y * gain.astype(jnp.float32)).astype(x.dtype)


def modulate(h, shift, scale):
    return h * (1 + scale[:, None, :]) + shift[:, None, :]


def _linear_recurrence_op(left, right):
    a_l, b_l = left
    a_r, b_r = right
    return a_r * a_l, a_r * b_l + b_r


def s5_bidirectional(u, a_re, a_im, log_dt, b_re, b_im, c_re, c_im, d_skip):
    bsz, seq_len, _ = u.shape
    f32 = jnp.float32
    uf = u.astype(f32).reshape(bsz, seq_len, N_SSM_GROUPS, SSM_GROUP_CH)
    uc = uf.astype(jnp.complex64)
    y = d_skip.astype(f32).reshape(N_SSM_GROUPS, SSM_GROUP_CH) * uf
    for direction in range(2):
        lam = lax.complex(jnp.minimum(a_re[direction].astype(f32), -1e-4),
                          a_im[direction].astype(f32))
        dt = jnp.exp(log_dt[direction].astype(f32))[:, None]
        a_bar = jnp.exp(lam * dt)
        b = lax.complex(b_re[direction].astype(f32), b_im[direction].astype(f32))
        b_bar = ((a_bar - 1.0) / lam)[:, :, None] * b
        bu = jnp.einsum("gph,blgh->blgp", b_bar, uc)
        a_seq = jnp.broadcast_to(a_bar, bu.shape)
        _, states = lax.associative_scan(_linear_recurrence_op, (a_seq, bu),
                                         reverse=(direction == 1), axis=1)
        c = lax.complex(c_re[direction].astype(f32), c_im[direction].astype(f32))
        y = y + jnp.einsum("ghp,blgp->blgh", c, states).real
    return y.reshape(bsz, seq_len, SSM_WIDTH).astype(u.dtype)


def neighbourhood_attention(q, k, v, rpb):
    bsz, seq_len, _ = q.shape
    rows = seq_len // GRID_W
    wh = min(NA_WIN_ROWS, rows)
    grid = (bsz, rows, GRID_W, N_HEADS_ATTN, HEAD_DIM)
    qg, kg, vg = q.reshape(grid), k.reshape(grid), v.reshape(grid)
    cols = jnp.arange(GRID_W)
    col_start = jnp.clip(cols - NA_WIN_COLS // 2, 0, GRID_W - NA_WIN_COLS)
    col_idx = col_start[:, None] + jnp.arange(NA_WIN_COLS)[None, :]
    dc = col_idx - cols[:, None]
    scale = HEAD_DIM ** -0.5

    def one_row(r):
        rs = jnp.clip(r - wh // 2, 0, rows - wh)
        q_r = lax.dynamic_index_in_dim(qg, r, axis=1, keepdims=False)
        k_blk = lax.dynamic_slice_in_dim(kg, rs, wh, axis=1)
        v_blk = lax.dynamic_slice_in_dim(vg, rs, wh, axis=1)
        k_win = k_blk[:, :, col_idx]
        v_win = v_blk[:, :, col_idx]
        dr = rs + jnp.arange(wh) - r
        bias = rpb[:, dr[None, :, None] + NA_WIN_ROWS - 1,
                   dc[:, None, :] + NA_WIN_COLS - 1]
        s = jnp.einsum("bchd,bicjhd->bhcij", q_r, k_win).astype(jnp.float32) * scale
        s = s + bias.astype(jnp.float32)[None]
        p = jax.nn.softmax(s.reshape(s.shape[:3] + (wh * NA_WIN_COLS,)), axis=-1)
        p = p.reshape(s.shape).astype(v.dtype)
        return jnp.einsum("bhcij,bicjhd->bchd", p, v_win)

    out = lax.map(one_row, jnp.arange(rows))
    return jnp.transpose(out, (1, 0, 2, 3, 4)).reshape(bsz, seq_len, ATTN_WIDTH)


def encoder_trunk(x, c, w_ada, b_ada, norm_mix, w_in, ssm_a_re, ssm_a_im, ssm_log_dt,
                  ssm_b_re, ssm_b_im, ssm_c_re, ssm_c_im, ssm_d, w_glu, b_glu,
                  norm_ssm_out, na_rpb, norm_attn_out, w_out, norm_ffn, w_ffn_gate,
                  w_ffn_up, w_ffn_down, norm_final):
    for layer in range(DEPTH):
        mod = jax.nn.silu(c) @ w_ada[layer] + b_ada[layer]
        sh_mix, sc_mix, g_mix, sh_ffn, sc_ffn, g_ffn = jnp.split(mod, N_MOD, axis=-1)
        h = modulate(rms_norm(x, norm_mix[layer]), sh_mix, sc_mix)
        proj = h @ w_in[layer]
        u, q, k, v = jnp.split(proj, [SSM_WIDTH, SSM_WIDTH + ATTN_WIDTH,
                                      SSM_WIDTH + 2 * ATTN_WIDTH], axis=-1)
        y_ssm = s5_bidirectional(u, ssm_a_re[layer], ssm_a_im[layer], ssm_log_dt[layer],
                                 ssm_b_re[layer], ssm_b_im[layer], ssm_c_re[layer],
                                 ssm_c_im[layer], ssm_d[layer])
        y_ssm = jax.nn.gelu(y_ssm)
        y_ssm = y_ssm * jax.nn.sigmoid(y_ssm @ w_glu[layer] + b_glu[layer])
        y_att = neighbourhood_attention(q, k, v, na_rpb[layer])
        mixed = jnp.concatenate([rms_norm(y_ssm, norm_ssm_out[layer]),
                                 rms_norm(y_att, norm_attn_out[layer])], axis=-1)
        x = x + g_mix[:, None, :] * (mixed @ w_out[layer])
        h = modulate(rms_norm(x, norm_ffn[layer]), sh_ffn, sc_ffn)
        f = (jax.nn.silu(h @ w_ffn_gate[layer]) * (h @ w_ffn_up[layer])) @ w_ffn_down[layer]
        x = x + g_ffn[:, None, :] * f
    return rms_norm(x, norm_final)


def setup_inputs(seed: int = 0) -> dict:
    key = jax.random.key(seed)
    ks = jax.random.split(key, 32)
    f32 = jnp.float32
    nrm = lambda k, shape, s: jax.random.normal(k, shape, f32) * s
    gain = lambda k, shape: 1.0 + 0.01 * jax.random.normal(k, shape, f32)
    L2 = (DEPTH, 2)
    G, P, H = N_SSM_GROUPS, SSM_STATE, SSM_GROUP_CH
    a_im_init = math.pi * jnp.arange(P, dtype=f32)
    return {
        "x_prompt": nrm(ks[0], (BATCH, SEQ, D_MODEL), 1.0),
        "x_sample": nrm(ks[1], (DEC_BATCH, DEC_SEQ, D_MODEL), 1.0),
        "c_prompt": nrm(ks[2], (BATCH, D_MODEL), 1.0),
        "c_sample": nrm(ks[3], (DEC_BATCH, D_MODEL), 1.0),
        "w_ada": nrm(ks[4], (DEPTH, D_MODEL, N_MOD * D_MODEL), 0.5 * D_MODEL ** -0.5),
        "b_ada": nrm(ks[5], (DEPTH, N_MOD * D_MODEL), 0.01),
        "norm_mix": gain(ks[6], (DEPTH, D_MODEL)),
        "w_in": nrm(ks[7], (DEPTH, D_MODEL, IN_PROJ_WIDTH), D_MODEL ** -0.5),
        "ssm_a_re": -0.5 + nrm(ks[8], L2 + (G, P), 0.01),
        "ssm_a_im": a_im_init + nrm(ks[9], L2 + (G, P), 0.01),
        "ssm_log_dt": jax.random.uniform(ks[10], L2 + (G,), f32,
                                         math.log(1e-3), math.log(1e-1)),
        "ssm_b_re": nrm(ks[11], L2 + (G, P, H), (2 * H) ** -0.5),
        "ssm_b_im": nrm(ks[12], L2 + (G, P, H), (2 * H) ** -0.5),
        "ssm_c_re": nrm(ks[13], L2 + (G, H, P), (2 * P) ** -0.5),
        "ssm_c_im": nrm(ks[14], L2 + (G, H, P), (2 * P) ** -0.5),
        "ssm_d": nrm(ks[15], (DEPTH, SSM_WIDTH), 1.0),
        "w_glu": nrm(ks[16], (DEPTH, SSM_WIDTH, SSM_WIDTH), SSM_WIDTH ** -0.5),
        "b_glu": nrm(ks[17], (DEPTH, SSM_WIDTH), 0.01),
        "norm_ssm_out": gain(ks[18], (DEPTH, SSM_WIDTH)),
        "na_rpb": nrm(ks[19], (DEPTH, N_HEADS_ATTN, 2 * NA_WIN_ROWS - 1, 2 * NA_WIN_COLS - 1), 0.02),
        "norm_attn_out": gain(ks[20], (DEPTH, ATTN_WIDTH)),
        "w_out": nrm(ks[21], (DEPTH, MIX_WIDTH, D_MODEL), MIX_WIDTH ** -0.5),
        "norm_ffn": gain(ks[22], (DEPTH, D_MODEL)),
        "w_ffn_gate": nrm(ks[23], (DEPTH, D_MODEL, D_FF), D_MODEL ** -0.5),
        "w_ffn_up": nrm(ks[24], (DEPTH, D_MODEL, D_FF), D_MODEL ** -0.5),
        "w_ffn_down": nrm(ks[25], (DEPTH, D_FF, D_MODEL), D_FF ** -0.5),
        "norm_final": gain(ks[26], (D_MODEL,)),
    }


def reference(x_prompt, x_sample, c_prompt, c_sample, w_ada, b_ada, norm_mix, w_in,
              ssm_a_re, ssm_a_im, ssm_log_dt, ssm_b_re, ssm_b_im, ssm_c_re, ssm_c_im,
              ssm_d, w_glu, b_glu, norm_ssm_out, na_rpb, norm_attn_out, w_out,
              norm_ffn, w_ffn_gate, w_ffn_up, w_ffn_down, norm_final):
    y_prompt = encoder_trunk(x_prompt, c_prompt, w_ada, b_ada, norm_mix, w_in, ssm_a_re,
                             ssm_a_im, ssm_log_dt, ssm_b_re, ssm_b_im, ssm_c_re, ssm_c_im,
                             ssm_d, w_glu, b_glu, norm_ssm_out, na_rpb, norm_attn_out,
                             w_out, norm_ffn, w_ffn_gate, w_ffn_up, w_ffn_down, norm_final)
    y_sample = encoder_trunk(x_sample, c_sample, w_ada, b_ada, norm_mix, w_in, ssm_a_re,
                             ssm_a_im, ssm_log_dt, ssm_b_re, ssm_b_im, ssm_c_re, ssm_c_im,
                             ssm_d, w_glu, b_glu, norm_ssm_out, na_rpb, norm_attn_out,
                             w_out, norm_ffn, w_ffn_gate, w_ffn_up, w_ffn_down, norm_final)
    return (y_prompt, y_sample)
```

```python
import math
import numpy as np
import ml_dtypes
import concourse.bass as bass
import concourse.mybir as mybir
from concourse.bass_utils import run_bass_kernel_spmd
from concourse.ap import AP

F32 = mybir.dt.float32
BF16 = mybir.dt.bfloat16
ALU = mybir.AluOpType
AF = mybir.ActivationFunctionType

D = 1024
L = 2048
NSEQ = 3
DFF = 2816
NFC = 22
EPS = 1e-6
NEG = -30000.0


class Tok:
    __slots__ = ("w", "r")

    def __init__(self):
        self.w = None
        self.r = {}


class EngW:
    def __init__(self, eng, sem, same_wait):
        self.eng = eng
        self.sem = sem
        self.n = 0
        self.waited = {}
        self.same_wait = same_wait


class Builder:
    def __init__(self, nc, sems, dma_sems):
        self.nc = nc
        self.pe = EngW(nc.tensor, sems[0], False)
        self.act = EngW(nc.scalar, sems[1], True)
        self.dve = EngW(nc.vector, sems[2], True)
        self.pool = EngW(nc.gpsimd, sems[3], True)
        self.sp = EngW(nc.sync, sems[4], False)
        self.engs = [self.pe, self.act, self.dve, self.pool, self.sp]
        self.dma_sems = dma_sems
        self.dma_n = [0] * len(dma_sems)
        self.dma_rr = 0

    def _wait(self, E, deps):
        for (sem, val) in deps:
            if sem is E.sem and not E.same_wait:
                continue
            k = id(sem)
            if E.waited.get(k, 0) < val:
                E.eng.wait_ge(sem, val)
                E.waited[k] = val

    @staticmethod
    def _deps(reads, writes):
        deps = []
        for t in reads:
            if t.w is not None:
                deps.append(t.w)
        for t in writes:
            if t.w is not None:
                deps.append(t.w)
            for k, v in t.r.items():
                deps.append(v)
        return deps

    @staticmethod
    def _upd(me, reads, writes):
        for t in reads:
            k = id(me[0])
            if k not in t.r or t.r[k][1] < me[1]:
                t.r[k] = me
        for t in writes:
            t.w = me
            t.r = {}

    def op(self, E, fn, reads=(), writes=()):
        self._wait(E, self._deps(reads, writes))
        inst = fn(E.eng)
        E.n += 1
        inst.then_inc(E.sem, 1)
        self._upd((E.sem, E.n), reads, writes)

    def dma(self, Q, out, in_, reads=(), writes=(), **kw):
        k = self.dma_rr
        self.dma_rr = (self.dma_rr + 1) % len(self.dma_sems)
        sem = self.dma_sems[k]
        deps = self._deps(reads, writes)
        if self.dma_n[k] > 0:
            deps.append((sem, 16 * self.dma_n[k]))
        self._wait(Q, deps)
        inst = Q.eng.dma_start(out=out, in_=in_, **kw)
        self.dma_n[k] += 1
        inst.then_inc(sem, 16)
        self._upd((sem, 16 * self.dma_n[k]), reads, writes)

    def barrier(self):
        for E in self.engs:
            deps = [(F.sem, F.n) for F in self.engs if F is not E and F.n > 0]
            deps += [(s, 16 * n) for s, n in zip(self.dma_sems, self.dma_n) if n > 0]
            self._wait(E, deps)

    def finish(self):
        deps = [(s, 16 * n) for s, n in zip(self.dma_sems, self.dma_n) if n > 0]
        deps += [(F.sem, F.n) for F in self.engs if F.n > 0 and F is not self.sp]
        self._wait(self.sp, deps)


def bcast_last(ap, n):
    return AP(ap.tensor, ap.offset, [list(d) for d in ap.ap] + [[0, n]])


SKIP = set()


def build_nc(debug=False):
    nc = bass.Bass("TRN2", target_bir_lowering=False)
    dbg = {}
    def dout(name, shape, dt):
        dbg[name] = nc.dram_tensor(name, list(shape), dt, kind="ExternalOutput").ap()
        return dbg[name]
    if debug:
        dout('d_hT', [128, 8 * 2048], BF16); dout('d_QT', [128, 4 * 2048], BF16); dout('d_KT', [128, 4 * 2048], BF16)
        dout('d_Utm', [128, 8192], BF16); dout('d_yatt', [128, 8192], BF16); dout('d_mixT', [128, 8 * 2048], BF16)
        dout('d_yssm', [128, 8192], BF16); dout('d_Vev', [128, 16 * 8 * 65], BF16); dout('d_mod', [128, 192], F32)
        dout('d_U8', [128, 8192], BF16); dout('d_Xb', [128, 16384], BF16); dout('d_toep', [128, 4096], BF16)
        dout('d_Y8', [128, 8192], BF16); dout('d_T2', [128, 8192], BF16)

    def din(name, shape, dt=F32):
        return nc.dram_tensor(name, list(shape), dt, kind="ExternalInput").ap()

    def dscr(name, shape, dt):
        return nc.dram_tensor(name, list(shape), dt, kind="Internal").ap()

    x_d = din("x", [NSEQ, L, D])
    y_d = nc.dram_tensor("y", [NSEQ, L, D], F32, kind="ExternalOutput").ap()
    cT_d = din("cT", [128, 8, 4])
    wada_d = din("w_ada", [D, 6 * D])
    bada_d = din("b_ada", [1, 6 * D])
    nmix_d = din("nmix", [128, 8])
    nffn_d = din("nffn", [128, 8])
    nfin_d = din("nfin", [128, D])
    nssm_d = din("nssm", [128, 4])
    natt_d = din("natt", [128, 4])
    bglu_d = din("bglu", [128, 512])
    win_d = din("w_in", [D, 2048])
    wglu_d = din("w_glu", [512, 512])
    wout_d = din("w_out", [D, D])
    wg_d = din("w_g", [D, DFF])
    wu_d = din("w_u", [D, DFF])
    wd_d = din("w_d", [DFF, D])
    are_d = din("are", [128, 32])
    aim_d = din("aim", [128, 32])
    ldt_d = din("ldt", [128, 32])
    bre_d = din("bre", [128, 32, 16])
    bim_d = din("bim", [128, 32, 16])
    cre_d = din("cre", [128, 32, 16])
    cim_d = din("cim", [128, 32, 16])
    dd_d = din("dd", [16, 32, 16])
    t2raw_d = din("t2raw", [8, 128, 16 * 64])
    m2_d = din("m2", [128, 16 * 64])
    identf_d = din("identf", [128, 128])

    winb = dscr("winb", [128, 8, 2048], BF16)
    wglub = dscr("wglub", [128, 4, 512], BF16)
    woutb = dscr("woutb", [128, 8, 1024], BF16)
    wgb = dscr("wgb", [NFC, 128, 8, 128], BF16)
    wub = dscr("wub", [NFC, 128, 8, 128], BF16)
    wdb = dscr("wdb", [NFC, 128, 1024], BF16)
    toepb = dscr("toepb", [128, 32, 128], BF16)
    kallb = dscr("kallb", [16, 32, 240], BF16)
    gscr = dscr("gscr", [NSEQ, 2, D], F32)
    winTb = dscr("winTb", [128, 32 * 2 * 128], BF16)
    woutSb = dscr("woutSb", [128, 32 * 2 * 128], BF16)

    from contextlib import ExitStack

    with ExitStack() as es:
        def sb(name, shape, dt):
            return es.enter_context(nc.sbuf_tensor("sb_" + name, list(shape), dt))

        sems = [es.enter_context(nc.semaphore("e%d" % i)) for i in range(5)]
        dsems = [es.enter_context(nc.semaphore("d%d" % i)) for i in range(16)]
        B = Builder(nc, sems, dsems)
        PE, ACT, DVE, POOL, SP = B.pe, B.act, B.dve, B.pool, B.sp

        psum = [es.enter_context(nc.psum_tensor("ps%d" % i, [128, 512], F32)) for i in range(8)]
        ptok = [Tok() for _ in range(8)]
        pstate = {"i": 0, "n": 8}

        def next_bank():
            i = pstate["i"] % pstate["n"]
            pstate["i"] = (i + 1) % pstate["n"]
            return psum[i], ptok[i]

        def psbf(bank):
            return bank[:].bitcast(BF16)

        identf = sb("identf", [128, 128], F32)
        identb = sb("identb", [128, 128], BF16)
        ones1 = sb("ones1", [1, 8], F32)
        epst = sb("epst", [128, 1], F32)
        modfm = sb("modfm", [128, 48, 4], F32)
        sclmix = sb("sclmix", [128, 8, 4], F32)
        sclffn = sb("sclffn", [128, 8, 4], F32)
        nssm = sb("nssm", [128, 4], F32)
        natt = sb("natt", [128, 4], F32)
        A1 = sb("A1", [128, 64], F32)
        A2 = sb("A2", [128, 64], F32)
        T2 = sb("T2", [128, 8, 16 * 64], BF16)
        bglu_sb = sb("bglu_sb", [128, 512], F32)
        wglu = sb("wglu", [128, 4, 512], BF16)
        xstate = sb("xstate", [128, 64], F32)
        st1 = sb("st1", [128, 64], F32)
        st2 = sb("st2", [128, 64], F32)
        ssq = sb("ssq", [128, 4], F32)
        rstd = sb("rstd", [128, 4], F32)
        rec = sb("rec", [128, 8], F32)
        junk = sb("junk", [128, 1024], BF16)
        t_const = Tok()
        t_mod = Tok()
        t_ssmw = Tok()
        t_stat = Tok()
        t_junk = Tok()

        NA = 160 * 512
        arena = sb("arena", [128, NA], BF16)

        def av(off_kb, nbytes, dt=BF16):
            o = int(off_kb * 512)
            ap = arena[:, o:o + nbytes // 2]
            if dt is F32:
                ap = ap.bitcast(F32)
            return ap

        B.dma(SP, identf[:], identf_d[:, :], writes=[t_const])
        B.op(DVE, lambda e: e.tensor_copy(out=identb[:], in_=identf[:]), reads=[t_const], writes=[t_const])
        B.op(DVE, lambda e: e.memset(ones1[:], 1.0), writes=[t_const])
        B.op(DVE, lambda e: e.memset(epst[:], EPS), writes=[t_const])
        B.dma(SP, nssm[:], nssm_d[:, :], writes=[t_const])
        B.dma(SP, natt[:], natt_d[:, :], writes=[t_const])
        B.dma(SP, bglu_sb[:], bglu_d[:, :], writes=[t_const])
        winT = av(128, 16384).rearrange("p (g r c) -> p g r c", g=32, r=2)
        woutS = av(144, 16384).rearrange("p (g r c) -> p g r c", g=32, r=2)

        def f32tile(off_kb, shape):
            n = int(np.prod(shape))
            ap = av(off_kb, n * 4, F32)
            if len(shape) == 2:
                return ap.rearrange("p (a b) -> p a b", a=shape[0]) if False else ap
            return ap

        ts = Tok()
        off = [0.0]

        def alloc(ncols):
            o = off[0]
            off[0] += ncols * 4 / 1024.0
            return av(o, ncols * 4, F32)

        are = alloc(32); aim = alloc(32); ldt = alloc(32)
        lr = alloc(32); dtt = alloc(32); er = alloc(32); ph2 = alloc(64); k2 = alloc(64); sc2 = alloc(64)
        ar = alloc(32); ai = alloc(32); nr = alloc(32); den = alloc(32); zr = alloc(32); zi = alloc(32)
        tA = alloc(32); tB = alloc(32); a8r = alloc(32); a8i = alloc(32)
        B.dma(SP, are, are_d[:, :], writes=[ts])
        B.dma(SP, aim, aim_d[:, :], writes=[ts])
        B.dma(SP, ldt, ldt_d[:, :], writes=[ts])

        def V(fn):
            B.op(DVE, fn, reads=[ts], writes=[ts])

        def A(fn):
            B.op(ACT, fn, reads=[ts], writes=[ts])

        V(lambda e: e.tensor_scalar(out=lr, in0=are, scalar1=-1e-4, scalar2=None, op0=ALU.min))
        A(lambda e: e.activation(out=dtt, in_=ldt, func=AF.Exp))
        V(lambda e: e.tensor_tensor(out=er, in0=lr, in1=dtt, op=ALU.mult))
        A(lambda e: e.activation(out=er, in_=er, func=AF.Exp))
        V(lambda e: e.tensor_tensor(out=ph2[:, 0:32], in0=aim, in1=dtt, op=ALU.mult))
        V(lambda e: e.tensor_scalar(out=ph2[:, 32:64], in0=ph2[:, 0:32], scalar1=math.pi / 2, scalar2=None, op0=ALU.add))
        MAGIC = 12582912.0
        V(lambda e: e.tensor_scalar(out=k2, in0=ph2, scalar1=1.0 / (2 * math.pi), scalar2=MAGIC, op0=ALU.mult, op1=ALU.add))
        V(lambda e: e.tensor_scalar(out=k2, in0=k2, scalar1=-MAGIC, scalar2=None, op0=ALU.add))
        V(lambda e: e.scalar_tensor_tensor(out=ph2, in0=k2, scalar=-2 * math.pi, in1=ph2, op0=ALU.mult, op1=ALU.add))
        V(lambda e: e.tensor_scalar(out=ph2, in0=ph2, scalar1=3.14159, scalar2=-3.14159, op0=ALU.min, op1=ALU.max))
        A(lambda e: e.activation(out=sc2, in_=ph2, func=AF.Sin))
        V(lambda e: e.tensor_tensor(out=ai, in0=er, in1=sc2[:, 0:32], op=ALU.mult))
        V(lambda e: e.tensor_tensor(out=ar, in0=er, in1=sc2[:, 32:64], op=ALU.mult))
        V(lambda e: e.tensor_scalar(out=nr, in0=ar, scalar1=-1.0, scalar2=None, op0=ALU.add))
        V(lambda e: e.tensor_tensor(out=den, in0=lr, in1=lr, op=ALU.mult))
        V(lambda e: e.tensor_tensor(out=tA, in0=aim, in1=aim, op=ALU.mult))
        V(lambda e: e.tensor_tensor(out=den, in0=den, in1=tA, op=ALU.add))
        V(lambda e: e.reciprocal(out=den, in_=den))
        V(lambda e: e.tensor_tensor(out=tA, in0=nr, in1=lr, op=ALU.mult))
        V(lambda e: e.tensor_tensor(out=tB, in0=ai, in1=aim, op=ALU.mult))
        V(lambda e: e.tensor_tensor(out=tA, in0=tA, in1=tB, op=ALU.add))
        V(lambda e: e.tensor_tensor(out=zr, in0=tA, in1=den, op=ALU.mult))
        V(lambda e: e.tensor_tensor(out=tA, in0=ai, in1=lr, op=ALU.mult))
        V(lambda e: e.tensor_tensor(out=tB, in0=nr, in1=aim, op=ALU.mult))
        V(lambda e: e.tensor_tensor(out=tA, in0=tA, in1=tB, op=ALU.subtract))
        V(lambda e: e.tensor_tensor(out=zi, in0=tA, in1=den, op=ALU.mult))
        V(lambda e: e.tensor_copy(out=a8r, in_=ar))
        V(lambda e: e.tensor_copy(out=a8i, in_=ai))
        for _ in range(3):
            V(lambda e: e.tensor_tensor(out=tA, in0=a8r, in1=a8r, op=ALU.mult))
            V(lambda e: e.tensor_tensor(out=tB, in0=a8i, in1=a8i, op=ALU.mult))
            V(lambda e: e.tensor_tensor(out=a8i, in0=a8r, in1=a8i, op=ALU.mult))
            V(lambda e: e.tensor_tensor(out=a8r, in0=tA, in1=tB, op=ALU.subtract))
            V(lambda e: e.tensor_scalar(out=a8i, in0=a8i, scalar1=2.0, scalar2=None, op0=ALU.mult))
        B.op(DVE, lambda e: e.tensor_copy(out=A1[:, 0:32], in_=a8r), reads=[ts], writes=[t_ssmw])
        B.op(DVE, lambda e: e.tensor_copy(out=A1[:, 32:64], in_=a8r), reads=[ts], writes=[t_ssmw])
        B.op(DVE, lambda e: e.tensor_scalar(out=A2[:, 0:32], in0=a8i, scalar1=-1.0, scalar2=None, op0=ALU.mult), reads=[ts], writes=[t_ssmw])
        B.op(DVE, lambda e: e.tensor_copy(out=A2[:, 32:64], in_=a8i), reads=[ts], writes=[t_ssmw])

        def alloc3(nk):
            return alloc(nk * 512).rearrange("p (k g h) -> p k g h", k=nk, g=32)

        TR = alloc3(8); TI_off = off[0]; TI = alloc3(8); GR = alloc3(9); GI = alloc3(9)
        braw = alloc(512).rearrange("p (g h) -> p g h", g=32)
        biraw = alloc(512).rearrange("p (g h) -> p g h", g=32)
        tmp1 = alloc(512).rearrange("p (g h) -> p g h", g=32)
        tmp2 = alloc(512).rearrange("p (g h) -> p g h", g=32)
        B.dma(SP, braw, bre_d[:, :, :], writes=[ts])
        B.dma(SP, biraw, bim_d[:, :, :], writes=[ts])
        B.dma(SP, GR[:, 0], cre_d[:, :, :], writes=[ts])
        B.dma(SP, GI[:, 0], cim_d[:, :, :], writes=[ts])
        zrb = bcast_last(zr, 16); zib = bcast_last(zi, 16)
        arb = bcast_last(ar, 16); aib = bcast_last(ai, 16)

        def cmul(outR, outI, inR, inI, sR, sI):
            V(lambda e: e.tensor_tensor(out=tmp1, in0=inR, in1=sR, op=ALU.mult))
            V(lambda e: e.tensor_tensor(out=tmp2, in0=inI, in1=sI, op=ALU.mult))
            V(lambda e: e.tensor_tensor(out=outR, in0=tmp1, in1=tmp2, op=ALU.subtract))
            V(lambda e: e.tensor_tensor(out=tmp1, in0=inR, in1=sI, op=ALU.mult))
            V(lambda e: e.tensor_tensor(out=tmp2, in0=inI, in1=sR, op=ALU.mult))
            V(lambda e: e.tensor_tensor(out=outI, in0=tmp1, in1=tmp2, op=ALU.add))

        cmul(TR[:, 0], TI[:, 0], braw, biraw, zrb, zib)
        for k in range(1, 8):
            cmul(TR[:, k], TI[:, k], TR[:, k - 1], TI[:, k - 1], arb, aib)
        for k in range(1, 9):
            cmul(GR[:, k], GI[:, k], GR[:, k - 1], GI[:, k - 1], arb, aib)

        for t in range(8):
            for (lo, hi, k) in ((0, 64, t + 1), (64, 128, 8 - t)):
                B.op(DVE, lambda e, lo=lo, hi=hi, k=k, t=t: e.tensor_copy(
                    out=woutS[lo:hi, :, 0, t * 16:(t + 1) * 16], in_=GR[lo:hi, k]), reads=[ts], writes=[t_ssmw])
                B.op(DVE, lambda e, lo=lo, hi=hi, k=k, t=t: e.tensor_scalar(
                    out=woutS[lo:hi, :, 1, t * 16:(t + 1) * 16], in0=GI[lo:hi, k], scalar1=-1.0, scalar2=None,
                    op0=ALU.mult), reads=[ts], writes=[t_ssmw])
        ddt = alloc(512).rearrange("p (g h) -> p g h", g=32)
        winp_off = off[0]
        winp = alloc(32 * 2 * 128).rearrange("p (g r c) -> p g r c", g=32, r=2)
        for s in range(8):
            for (lo, hi, k) in ((0, 64, 7 - s), (64, 128, s)):
                V(lambda e, lo=lo, hi=hi, k=k, s=s: e.tensor_copy(out=winp[lo:hi, :, 0, s * 16:(s + 1) * 16], in_=TR[lo:hi, k]))
                V(lambda e, lo=lo, hi=hi, k=k, s=s: e.tensor_copy(out=winp[lo:hi, :, 1, s * 16:(s + 1) * 16], in_=TI[lo:hi, k]))
        for g in range(32):
            bank, bt = next_bank()
            for ri in range(2):
                B.op(PE, lambda e, g=g, ri=ri, bank=bank: e.transpose(
                    out=bank[:, ri * 128:(ri + 1) * 128], in_=winp[:, g, ri, :], identity=identf[:]),
                    reads=[ts, t_const], writes=[bt])
            B.op(ACT if g % 2 else DVE, (lambda e, g=g, bank=bank: e.tensor_copy(
                out=winT[:, g, :, :], in_=bank[:, 0:256].rearrange("p (r c) -> p r c", r=2))) if g % 2 == 0 else
                (lambda e, g=g, bank=bank: e.copy(
                    out=winT[:, g, :, :], in_=bank[:, 0:256].rearrange("p (r c) -> p r c", r=2))),
                reads=[bt], writes=[t_ssmw, bt])

        V(lambda e: e.tensor_scalar(out=tmp1, in0=TI[:, 0], scalar1=-1.0, scalar2=None, op0=ALU.mult))
        B.dma(SP, ddt[0:16], dd_d[:, :, :], writes=[ts])
        gbase = winp_off
        GallR = av(gbase, 16 * 240 * 4, F32).rearrange("p (g t h) -> p g t h", g=16, t=15)
        GallI = av(gbase + 15, 16 * 240 * 4, F32).rearrange("p (g t h) -> p g t h", g=16, t=15)
        kall = av(gbase + 30, 16 * 240 * 4, F32).rearrange("p (g c) -> p g c", g=16)
        kallbf = av(TI_off, 16 * 240 * 2).rearrange("p (g c) -> p g c", g=16)
        for gh in range(2):
            gs = slice(gh * 16, (gh + 1) * 16)
            V(lambda e: e.memset(GallR, 0.0))
            V(lambda e: e.memset(GallI, 0.0))
            for k in range(8):
                V(lambda e, k=k: e.tensor_copy(out=GallR[0:64, :, 7 + k, :], in_=GR[0:64, k, gs, :]))
                V(lambda e, k=k: e.tensor_copy(out=GallI[0:64, :, 7 + k, :], in_=GI[0:64, k, gs, :]))
                V(lambda e, k=k: e.tensor_copy(out=GallR[64:128, :, 7 - k, :], in_=GR[64:128, k, gs, :]))
                V(lambda e, k=k: e.tensor_copy(out=GallI[64:128, :, 7 - k, :], in_=GI[64:128, k, gs, :]))
            for g2 in range(8):
                bank, bt = next_bank()
                for gi in range(2):
                    gl = g2 * 2 + gi
                    g = gh * 16 + gl
                    B.op(PE, lambda e, g=g, gl=gl, gi=gi, bank=bank: e.matmul(
                        bank[0:16, gi * 240:(gi + 1) * 240], lhsT=TR[:, 0, g, :],
                        rhs=GallR[:, gl].rearrange("p t h -> p (t h)"), start=True, stop=False),
                        reads=[ts], writes=[bt])
                    B.op(PE, lambda e, g=g, gl=gl, gi=gi, bank=bank: e.matmul(
                        bank[0:16, gi * 240:(gi + 1) * 240], lhsT=tmp1[:, g, :],
                        rhs=GallI[:, gl].rearrange("p t h -> p (t h)"), start=False, stop=True),
                        reads=[ts], writes=[bt])
                B.op(DVE, lambda e, g2=g2, bank=bank: e.tensor_copy(
                    out=kall[0:16, g2 * 2:g2 * 2 + 2, :], in_=bank[0:16, 0:480].rearrange("p (g c) -> p g c", g=2)),
                    reads=[bt], writes=[ts, bt])
            V(lambda e: e.tensor_tensor(out=kall[0:16, :, 112:128], in0=kall[0:16, :, 112:128], in1=ddt[0:16, gs, :], op=ALU.add))
            V(lambda e: e.tensor_copy(out=kallbf[0:16], in_=kall[0:16]))
            B.dma(SP, kallb[:, gs, :], kallbf[0:16], reads=[ts], writes=[ts])
        for s in range(8):
            B.dma(SP, toepb[s * 16:(s + 1) * 16, :, :], kallb[:, :, (7 - s) * 16:(7 - s) * 16 + 128],
                  reads=[ts], writes=[ts])
        B.dma(SP, winTb[:, :], winT.rearrange("p g r c -> p (g r c)"), reads=[t_ssmw], writes=[ts])
        B.dma(SP, woutSb[:, :], woutS.rearrange("p g r c -> p (g r c)"), reads=[t_ssmw], writes=[ts])

        B.barrier()
        cT = av(0, 128, F32).rearrange("p (k n) -> p k n", k=8)
        sil = av(1, 128, F32).rearrange("p (k n) -> p k n", k=8)
        badar = av(2, 6 * D * 4, F32)
        wblk = [av(32 + 16 * i, 8 * 512 * 4, F32).rearrange("p (k c) -> p k c", k=8) for i in range(2)]
        wtok = [Tok(), Tok()]
        tm = Tok()
        B.dma(SP, cT, cT_d[:, :, :], writes=[tm])
        B.dma(SP, badar[0:1], bada_d[:, :], writes=[tm])
        B.op(ACT, lambda e: e.activation(out=sil, in_=cT, func=AF.Silu), reads=[tm], writes=[tm])
        wada_v = wada_d.rearrange("(k p) c -> p k c", p=128)
        mbank, mtok = next_bank()
        for blk in range(12):
            wb = wblk[blk % 2]
            B.dma(SP, wb, wada_v[:, :, blk * 512:(blk + 1) * 512], writes=[wtok[blk % 2]])
            for j in range(4):
                ct = blk * 4 + j
                o = (ct % 48) * 4
                B.op(PE, lambda e, ct=ct, o=o: e.matmul(
                    mbank[:, o:o + 4], lhsT=badar[0:1, ct * 128:(ct + 1) * 128], rhs=ones1[0:1, 0:4],
                    start=True, stop=False), reads=[tm, t_const], writes=[mtok])
                for kt in range(8):
                    B.op(PE, lambda e, wb=wb, j=j, kt=kt, o=o: e.matmul(
                        mbank[:, o:o + 4], lhsT=wb[:, kt, j * 128:(j + 1) * 128], rhs=sil[:, kt, :],
                        start=False, stop=(kt == 7)), reads=[tm, wtok[blk % 2]], writes=[mtok])
        B.op(DVE, lambda e: e.tensor_copy(out=modfm[:], in_=mbank[:, 0:192].rearrange("p (c n) -> p c n", n=4)),
             reads=[mtok], writes=[t_mod, mtok])
        nm = av(3, 32, F32); nf = av(3.5, 32, F32)
        B.dma(SP, nm, nmix_d[:, :], writes=[tm])
        B.dma(SP, nf, nffn_d[:, :], writes=[tm])
        for (dst, nrm, base) in ((sclmix, nm, 8), (sclffn, nf, 32)):
            B.op(DVE, lambda e, dst=dst, base=base: e.tensor_scalar(
                out=dst[:], in0=modfm[:, base:base + 8, :], scalar1=1.0, scalar2=None, op0=ALU.add),
                reads=[t_mod, tm], writes=[t_mod])
            B.op(DVE, lambda e, dst=dst, nrm=nrm: e.tensor_tensor(
                out=dst[:], in0=dst[:], in1=bcast_last(nrm, 4), op=ALU.mult), reads=[t_mod, tm], writes=[t_mod])
        for w, base in ((0, 16), (1, 40)):
            for n in range(NSEQ):
                B.dma(SP, gscr[n, w, :].rearrange("(k p) -> p k", p=128), modfm[:, base:base + 8, n],
                      reads=[t_mod], writes=[tm], allow_slow_non_contiguous=True)

        B.barrier()
        stg = [av(0 + 8 * i, 2048 * 4, F32) for i in range(3)]
        stgb = [av(32 + 4 * i, 2048 * 2) for i in range(3)]
        stok = [Tok() for _ in range(3)]
        sbtok = [Tok() for _ in range(3)]
        cnt = [0]

        def conv(src, dst, ncol):
            i = cnt[0] % 3
            cnt[0] += 1
            B.dma(SP, stg[i][:, 0:ncol], src, writes=[stok[i]])
            E = (ACT, DVE, POOL)[i]
            if E is ACT:
                B.op(E, lambda e: e.copy(out=stgb[i][:, 0:ncol], in_=stg[i][:, 0:ncol]), reads=[stok[i]], writes=[sbtok[i]])
            else:
                B.op(E, lambda e: e.tensor_copy(out=stgb[i][:, 0:ncol], in_=stg[i][:, 0:ncol]), reads=[stok[i]], writes=[sbtok[i]])
            return i

        for kt in range(8):
            i = conv(win_d[kt * 128:(kt + 1) * 128, :], None, 2048)
            B.dma(SP, winb[:, kt, :], stgb[i][:, 0:2048], reads=[sbtok[i]])
        for kt in range(4):
            i = conv(wglu_d[kt * 128:(kt + 1) * 128, :], None, 512)
            B.dma(SP, wglub[:, kt, :], stgb[i][:, 0:512], reads=[sbtok[i]])
        for kt in range(8):
            i = conv(wout_d[kt * 128:(kt + 1) * 128, :], None, 1024)
            B.dma(SP, woutb[:, kt, :], stgb[i][:, 0:1024], reads=[sbtok[i]])
        for (wsrc, wdst) in ((wg_d, wgb), (wu_d, wub)):
            for kt in range(8):
                for hf in range(2):
                    i = conv(wsrc[kt * 128:(kt + 1) * 128, hf * 1408:(hf + 1) * 1408], None, 1408)
                    B.dma(SP, wdst[hf * 11:(hf + 1) * 11, :, kt, :].rearrange("f p j -> p f j"),
                          stgb[i][:, 0:1408].rearrange("p (f j) -> p f j", j=128), reads=[sbtok[i]])
        for fc in range(NFC):
            i = conv(wd_d[fc * 128:(fc + 1) * 128, :], None, 1024)
            B.dma(SP, wdb[fc, :, :], stgb[i][:, 0:1024], reads=[sbtok[i]])

        B.barrier()
        m2t = av(64, 1024 * 4, F32)
        t2r = [av(72 + 4 * i, 1024 * 4, F32) for i in range(2)]
        t2tok = [Tok(), Tok()]
        tm2 = Tok()
        t_T2 = Tok()
        B.dma(SP, m2t, m2_d[:, :], writes=[tm2])
        for h in range(8):
            B.dma(SP, t2r[h % 2], t2raw_d[h, :, :], writes=[t2tok[h % 2]])
            B.op(DVE, lambda e, h=h: e.tensor_tensor(out=T2[:, h, :], in0=t2r[h % 2], in1=m2t, op=ALU.add),
                 reads=[t2tok[h % 2], tm2], writes=[t_T2])
        B.barrier()
        B.dma(SP, wglu[:], wglub[:, :, :], writes=[t_ssmw])

        hT = av(0, 32768).rearrange("p (k t) -> p k t", k=8)
        Xb = av(0, 32768).rearrange("p (c r g) -> p c r g", c=256, r=2)
        w_in = av(32, 32768).rearrange("p (k c) -> p k c", k=8)
        mixT = av(32, 32768).rearrange("p (k t) -> p k t", k=8)
        QT = av(64, 16384).rearrange("p (h t) -> p h t", h=4)
        KT = av(80, 16384).rearrange("p (h t) -> p h t", h=4)
        Vev = av(96, 16 * 8 * 65 * 2).rearrange("p (i h d) -> p i h d", i=16, h=8)
        Vod = av(112.5, 15 * 8 * 65 * 2).rearrange("p (i h d) -> p i h d", i=15, h=8)
        Utm = av(128, 16384).rearrange("p (a g s h) -> p a g s h", a=2, g=32, s=8)
        ytm = av(144, 16384).rearrange("p (i c) -> p i c", i=16)
        U8 = av(64, 16384).rearrange("p (g c) -> p g c", g=32)
        Sblk = av(80, 16384, F32).rearrange("p (c r g) -> p c r g", c=64, r=2)
        Y8 = av(96, 16384).rearrange("p (g c) -> p g c", g=32)
        toep = av(112, 8192).rearrange("p (g c) -> p g c", g=32)
        w_out = av(0, 16384).rearrange("p (k c) -> p k c", k=8)
        h2T = av(16, 8192).rearrange("p (k t) -> p k t", k=8)
        hid = av(64, NFC * 512 * 2).rearrange("p (f t) -> p f t", f=NFC)
        x1 = av(86, 4 * 1024 * 4, F32).rearrange("p (t c) -> p t c", t=4)
        NWB = 4
        wgs = [av(102 + 4 * i, 2048).rearrange("p (k j) -> p k j", k=8) for i in range(NWB)]
        wus = [av(102 + 4 * i + 2, 2048).rearrange("p (k j) -> p k j", k=8) for i in range(NWB)]
        wds = [av(118 + 2 * i, 2048) for i in range(NWB)]
        gmb = av(24, 4096, F32)
        gfb = av(28, 4096, F32)
        nfb = av(156, 4096, F32)

        x_tiles = x_d

        def rms_rstd(E, src_ap, ncol, col, reads, tokw):
            B.op(ACT, lambda e: e.activation(out=junk[:, 0:ncol], in_=src_ap, func=AF.Square,
                                             accum_out=ssq[:, col:col + 1]),
                 reads=list(reads) + [], writes=[t_stat, t_junk])
            B.op(ACT, lambda e: e.activation(out=rstd[:, col:col + 1], in_=ssq[:, col:col + 1], func=AF.Sqrt,
                                             bias=epst[:], scale=1.0 / ncol), reads=[t_stat, t_const], writes=[t_stat])
            B.op(DVE, lambda e: e.reciprocal(out=rstd[:, col:col + 1], in_=rstd[:, col:col + 1]),
                 reads=[t_stat], writes=[t_stat])

        for n in range(NSEQ):
            B.barrier()
            xtile = [av(144 + 4 * i, 4096, F32) for i in range(2)]
            xnb = [av(152 + 2 * i, 2048) for i in range(2)]
            xtok = [Tok(), Tok()]
            xntok = [Tok(), Tok()]
            t_hT = [Tok() for _ in range(16)]
            t_win = Tok()
            B.dma(SP, w_in, winb[:, :, :], writes=[t_win])
            for tt in range(16):
                i = tt % 2
                B.dma(SP, xtile[i], x_tiles[n, tt * 128:(tt + 1) * 128, :], writes=[xtok[i]])
                rms_rstd(ACT, xtile[i], 1024, 0, [xtok[i]], None)
                B.op(DVE, lambda e, i=i: e.tensor_scalar(out=xnb[i], in0=xtile[i], scalar1=rstd[:, 0:1], scalar2=None,
                                                       op0=ALU.mult), reads=[xtok[i], t_stat], writes=[xntok[i]])
                bank, bt = next_bank()
                pb = psbf(bank)
                for kt in range(8):
                    B.op(PE, lambda e, i=i, kt=kt, pb=pb: e.transpose(
                        out=pb[:, kt * 128:(kt + 1) * 128], in_=xnb[i][:, kt * 128:(kt + 1) * 128], identity=identb[:]),
                        reads=[xntok[i], t_const], writes=[bt])
                for kt in range(8):
                    if kt % 2 == 0:
                        B.op(DVE, lambda e, kt=kt, pb=pb, tt=tt: e.tensor_scalar(
                            out=hT[:, kt, tt * 128:(tt + 1) * 128], in0=pb[:, kt * 128:(kt + 1) * 128],
                            scalar1=sclmix[:, kt, n:n + 1], scalar2=modfm[:, kt, n:n + 1], op0=ALU.mult, op1=ALU.add),
                            reads=[bt, t_mod], writes=[t_hT[tt], bt])
                    else:
                        B.op(ACT, lambda e, kt=kt, pb=pb, tt=tt: e.activation(
                            out=hT[:, kt, tt * 128:(tt + 1) * 128], in_=pb[:, kt * 128:(kt + 1) * 128],
                            func=AF.Identity, scale=sclmix[:, kt, n:n + 1], bias=modfm[:, kt, n:n + 1]),
                            reads=[bt, t_mod], writes=[t_hT[tt], bt])
            t_Q = Tok(); t_K = Tok(); t_V = Tok(); t_U = Tok()
            flip = [0]

            def evac(out_ap, in_ap, reads, writes, scale=None):
                flip[0] ^= 1
                if flip[0]:
                    if scale is None:
                        B.op(DVE, lambda e: e.tensor_copy(out=out_ap, in_=in_ap), reads=reads, writes=writes)
                    else:
                        B.op(DVE, lambda e: e.tensor_scalar(out=out_ap, in0=in_ap, scalar1=scale, scalar2=None,
                                                          op0=ALU.mult), reads=reads, writes=writes)
                else:
                    if scale is None:
                        B.op(ACT, lambda e: e.copy(out=out_ap, in_=in_ap), reads=reads, writes=writes)
                    else:
                        B.op(ACT, lambda e: e.mul(out=out_ap, in_=in_ap, mul=scale), reads=reads, writes=writes)

            for (dst, cbase, tk, scl) in ((QT, 512, t_Q, 0.125), (KT, 1024, t_K, None)):
                for hp in range(4):
                    for tb in range(4):
                        bank, bt = next_bank()
                        for kt in range(8):
                            B.op(PE, lambda e, kt=kt, hp=hp, tb=tb, bank=bank, cbase=cbase: e.matmul(
                                bank[:, :], lhsT=w_in[:, kt, cbase + hp * 128:cbase + (hp + 1) * 128],
                                rhs=hT[:, kt, tb * 512:(tb + 1) * 512], start=(kt == 0), stop=(kt == 7)),
                                reads=[t_win] + t_hT[tb * 4:tb * 4 + 4], writes=[bt])
                        evac(dst[:, hp, tb * 512:(tb + 1) * 512], bank[:, :], [bt], [tk, bt], scale=scl)
            B.op(POOL, lambda e: e.memset(Vev[:, :, :, 64:65], 1.0), writes=[t_V])
            B.op(POOL, lambda e: e.memset(Vod[:, :, :, 64:65], 1.0), writes=[t_V])
            for (dst, ntile, tok0) in ((Vev, 16, 0), (Vod, 15, 64)):
                for i in range(ntile):
                    t0 = tok0 + i * 128
                    bank, bt = next_bank()
                    for kt in range(8):
                        B.op(PE, lambda e, kt=kt, t0=t0, bank=bank: e.matmul(
                            bank[:, :], lhsT=hT[:, kt, t0:t0 + 128], rhs=w_in[:, kt, 1536:2048],
                            start=(kt == 0), stop=(kt == 7)),
                            reads=[t_win] + t_hT[t0 // 128:(t0 + 127) // 128 + 1], writes=[bt])
                    evac(dst[:, i, :, 0:64], bank[:, :].rearrange("p (h d) -> p h d", h=8), [bt], [t_V, bt])
            for a in range(2):
                for s in range(8):
                    bank, bt = next_bank()
                    for kt in range(8):
                        hsl = hT[:, kt, a * 1024:(a + 1) * 1024].rearrange("p (c s) -> p c s", s=8)[:, :, s]
                        B.op(PE, lambda e, kt=kt, hsl=hsl, bank=bank: e.matmul(
                            bank[:, :], lhsT=hsl, rhs=w_in[:, kt, 0:512], start=(kt == 0), stop=(kt == 7)),
                            reads=[t_win] + t_hT[a * 8:a * 8 + 8], writes=[bt])
                    evac(Utm[:, a, :, s, :], bank[:, :].rearrange("p (g h) -> p g h", g=32), [bt], [t_U, bt])

            B.barrier()
            if debug and n == 0:
                B.dma(SP, dbg['d_hT'][:, :], av(0, 32768)); B.dma(SP, dbg['d_QT'][:, :], av(64, 16384)); B.dma(SP, dbg['d_KT'][:, :], av(80, 16384))
                B.dma(SP, dbg['d_Utm'][:, :], av(128, 16384)); B.dma(SP, dbg['d_Vev'][:, :], av(96, 16 * 8 * 65 * 2))
                B.dma(SP, dbg['d_mod'][:, :], modfm[:].rearrange("p c n -> p (c n)")); B.dma(SP, dbg['d_T2'][:, :], T2[:].rearrange("p h c -> p (h c)"))
                B.barrier()
            P3ON = 'P3' not in SKIP
            sct = [av(0 + 1 * i, 1024, F32).rearrange("p (t q) -> p t q", t=4) for i in range(4)]
            ptt = [av(4 + 0.5 * i, 512).rearrange("p (t q) -> p t q", t=4) for i in range(4)]
            sctok = [Tok() for _ in range(4)]
            pttok = [Tok() for _ in range(4)]
            ynb = av(8, 1024)
            t_yn = Tok()
            t_ytm = [Tok() for _ in range(16)]
            t_mix = [Tok() for _ in range(16)]
            pstate["n"] = 4
            pstate["i"] = 0
            NJ = 4
            LA = 2
            iters = []
            for i in range(16 if P3ON else 0):
                for hg in range(2):
                    for hl in range(4):
                        for a in range(2):
                            iters.append((i, hg, hl, a))
            sc_state = {}

            def stage_scores(idx):
                i, hg, hl, a = iters[idx]
                h = hg * 4 + hl
                hp, hb = h // 2, (h % 2) * 64
                r = 2 * i + a
                rs = min(max(r - 4, 0), 24)
                e0 = rs - r + 7
                j = idx % NJ
                sbank, sbt = next_bank()
                for t in range(4):
                    B.op(PE, lambda e, t=t: e.matmul(
                        sbank[:, t * 64:(t + 1) * 64],
                        lhsT=KT[hb:hb + 64, hp, (rs + 2 * t) * 64:(rs + 2 * t) * 64 + 128],
                        rhs=QT[hb:hb + 64, hp, r * 64:(r + 1) * 64], start=True, stop=True),
                        reads=[t_Q, t_K], writes=[sbt])
                t2v = T2[:, h, :].rearrange("p (e q) -> p e q", e=16)[:, e0:e0 + 7:2, :]
                B.op(DVE, lambda e: e.tensor_tensor(
                    out=sct[j], in0=sbank[:, 0:256].rearrange("p (t q) -> p t q", t=4), in1=t2v, op=ALU.add),
                    reads=[sbt, t_T2], writes=[sctok[j], sbt])
                B.op(ACT, lambda e: e.activation(out=ptt[j], in_=sct[j], func=AF.Exp),
                     reads=[sctok[j]], writes=[pttok[j]])

            def stage_pv(idx):
                i, hg, hl, a = iters[idx]
                h = hg * 4 + hl
                r = 2 * i + a
                rs = min(max(r - 4, 0), 24)
                Vt, vb = (Vev, rs // 2) if rs % 2 == 0 else (Vod, (rs - 1) // 2)
                j = idx % NJ
                bi = 4 + (i % 2) * 2 + hg
                pbank, pbt = psum[bi], ptok[bi]
                for t in range(4):
                    B.op(PE, lambda e, t=t: e.matmul(
                        pbank[a * 64:(a + 1) * 64, hl * 65:(hl + 1) * 65], lhsT=ptt[j][:, t, :],
                        rhs=Vt[:, vb + t, h, :], start=(t == 0), stop=(t == 3)),
                        reads=[pttok[j], t_V], writes=[pbt])
                if hg == 1 and hl == 3 and a == 1:
                    finish_rowpair(i)

            def finish_rowpair(i):
                for hg in range(2):
                    bi = 4 + (i % 2) * 2 + hg
                    pbank, pbt = psum[bi], ptok[bi]
                    pv = pbank[:, 0:260].rearrange("p (h d) -> p h d", h=4)
                    B.op(DVE, lambda e: e.reciprocal(out=rec[:, hg * 4:(hg + 1) * 4], in_=pv[:, :, 64]),
                         reads=[pbt], writes=[t_stat])
                    B.op(DVE, lambda e: e.tensor_tensor(
                        out=ytm[:, i, hg * 256:(hg + 1) * 256].rearrange("p (h d) -> p h d", h=4), in0=pv[:, :, 0:64],
                        in1=bcast_last(rec[:, hg * 4:(hg + 1) * 4], 64), op=ALU.mult),
                        reads=[pbt, t_stat], writes=[t_ytm[i], pbt])
                rms_rstd(ACT, ytm[:, i, :], 512, 1, [t_ytm[i]], None)
                B.op(DVE, lambda e: e.tensor_scalar(out=ynb[:, 0:512], in0=ytm[:, i, :], scalar1=rstd[:, 1:2],
                                                  scalar2=None, op0=ALU.mult),
                     reads=[t_ytm[i], t_stat], writes=[t_yn])
                bank, bt = next_bank()
                pb = psbf(bank)
                for jj in range(4):
                    B.op(PE, lambda e, jj=jj: e.transpose(out=pb[:, jj * 128:(jj + 1) * 128],
                                                          in_=ynb[:, jj * 128:(jj + 1) * 128], identity=identb[:]),
                         reads=[t_yn, t_const], writes=[bt])
                for jj in range(4):
                    B.op(ACT, lambda e, jj=jj: e.activation(
                        out=mixT[:, 4 + jj, i * 128:(i + 1) * 128], in_=pb[:, jj * 128:(jj + 1) * 128],
                        func=AF.Identity, scale=natt[:, jj:jj + 1]), reads=[bt, t_const], writes=[t_mix[i], bt])

            for idx in range(len(iters) + LA):
                if idx < len(iters):
                    stage_scores(idx)
                if idx >= LA and idx - LA < len(iters):
                    stage_pv(idx - LA)

            pstate["n"] = 8
            B.barrier()
            if debug and n == 0:
                B.dma(SP, dbg['d_yatt'][:, :], av(144, 16384))
                B.barrier()
            t_U8 = Tok(); t_S = [Tok(), Tok()]; t_X = Tok(); t_Y8 = Tok(); t_toep = Tok(); t_xs = [Tok(), Tok()]
            B.dma(SP, toep, toepb[:, :, :], writes=[t_toep])
            winT = av(128, 16384).rearrange("p (g r c) -> p g r c", g=32, r=2)
            woutS = av(80, 16384).rearrange("p (g r c) -> p g r c", g=32, r=2)
            t_wi = Tok(); t_wo2 = Tok()
            for g in range(32):
                bank, bt = next_bank()
                pb = psbf(bank)
                for a in range(2):
                    B.op(PE, lambda e, g=g, a=a, pb=pb: e.transpose(
                        out=pb[:, a * 128:(a + 1) * 128], in_=Utm[:, a, g, :, :].rearrange("p s h -> p (s h)"), identity=identb[:]),
                        reads=[t_U, t_const], writes=[bt])
                evac(U8[:, g, :], pb[:, 0:256], [bt], [t_U8, bt])
            B.dma(SP, winT.rearrange("p g r c -> p (g r c)"), winTb[:, :], writes=[t_wi, t_U])
            B.op(DVE, lambda e: e.memset(xstate[:], 0.0), writes=t_xs)
            B.op(DVE, lambda e: e.memset(Xb[0:64, 0], 0.0), writes=[t_X])
            B.op(POOL, lambda e: e.memset(Xb[64:128, 255], 0.0), writes=[t_X])
            halves = ((0, 64, DVE, 0), (64, 128, POOL, 1))
            for jb in range(4):
                for gq in range(8):
                    bank, bt = next_bank()
                    for gl in range(4):
                        g = gq * 4 + gl
                        for ri in range(2):
                            col = (gl * 2 + ri) * 64
                            B.op(PE, lambda e, g=g, ri=ri, col=col, bank=bank: e.matmul(
                                bank[0:64, col:col + 64], lhsT=winT[:, g, ri, 0:64],
                                rhs=U8[:, g, jb * 64:(jb + 1) * 64], start=True, stop=True),
                                reads=[t_U8, t_wi], writes=[bt])
                            B.op(PE, lambda e, g=g, ri=ri, col=col, bank=bank: e.matmul(
                                bank[64:128, col:col + 64], lhsT=winT[:, g, ri, 64:128],
                                rhs=U8[:, g, (3 - jb) * 64:(4 - jb) * 64], start=True, stop=True),
                                reads=[t_U8, t_wi], writes=[bt])
                    src = bank[:, :].rearrange("p (g r c) -> p g r c", g=4, r=2)
                    dst = Sblk[:, :, :, gq * 4:(gq + 1) * 4].rearrange("p c r g -> p g r c")
                    B.op(ACT, lambda e, src=src, dst=dst: e.copy(out=dst, in_=src), reads=[bt], writes=[t_S[0], t_S[1], bt])
                for stp in range(64 if 'SCAN' not in SKIP else 0):
                    for (lo, hi, E, d) in halves:
                        c = stp if d == 0 else 63 - stp
                        if stp == 0:
                            prev = xstate[lo:hi, :]
                        else:
                            cp = c - 1 if d == 0 else c + 1
                            prev = Sblk[lo:hi, cp].rearrange("p r g -> p (r g)")
                        pv = prev
                        psw = AP(pv.tensor, pv.offset + 32, [list(pv.ap[0]), [-32, 2], [1, 32]])
                        cur = Sblk[lo:hi, c].rearrange("p r g -> p (r g)")
                        rd = [t_S[d], t_xs[d], t_ssmw]
                        B.op(E, lambda e, lo=lo, hi=hi, pv=pv: e.tensor_tensor(
                            out=st1[lo:hi, :], in0=A1[lo:hi, :], in1=pv, op=ALU.mult), reads=rd, writes=[t_xs[d]])
                        B.op(E, lambda e, lo=lo, hi=hi, psw=psw: e.tensor_tensor(
                            out=st2[lo:hi, :].rearrange("p (r g) -> p r g", r=2),
                            in0=A2[lo:hi, :].rearrange("p (r g) -> p r g", r=2), in1=psw, op=ALU.mult),
                            reads=rd, writes=[t_xs[d]])
                        B.op(E, lambda e, lo=lo, hi=hi: e.tensor_tensor(
                            out=st1[lo:hi, :], in0=st1[lo:hi, :], in1=st2[lo:hi, :], op=ALU.add),
                            reads=[t_xs[d]], writes=[t_xs[d]])
                        B.op(E, lambda e, lo=lo, hi=hi, cur=cur: e.tensor_tensor(
                            out=cur, in0=st1[lo:hi, :], in1=cur, op=ALU.add), reads=[t_xs[d]], writes=[t_S[d], t_xs[d]])
                for (lo, hi, E, d) in halves:
                    cb = jb if d == 0 else 3 - jb
                    last = 63 if d == 0 else 0
                    if d == 0:
                        nn = 64 if cb < 3 else 63
                        B.op(E, lambda e, lo=lo, hi=hi, cb=cb, nn=nn: e.tensor_copy(
                            out=Xb[lo:hi, cb * 64 + 1:cb * 64 + 1 + nn], in_=Sblk[lo:hi, 0:nn]), reads=[t_S[d]], writes=[t_X])
                    else:
                        c0 = 0 if cb > 0 else 1
                        B.op(E, lambda e, lo=lo, hi=hi, cb=cb, c0=c0: e.tensor_copy(
                            out=Xb[lo:hi, cb * 64 + c0 - 1:cb * 64 + 63], in_=Sblk[lo:hi, c0:64]), reads=[t_S[d]], writes=[t_X])
                    B.op(E, lambda e, lo=lo, hi=hi, last=last: e.tensor_copy(
                        out=xstate[lo:hi, :], in_=Sblk[lo:hi, last].rearrange("p r g -> p (r g)")),
                        reads=[t_S[d]], writes=[t_xs[d]])
            B.dma(SP, woutS.rearrange("p g r c -> p (g r c)"), woutSb[:, :], writes=[t_wo2, t_S[0], t_S[1]])
            for g in range(32):
                bank, bt = next_bank()
                B.op(PE, lambda e, g=g, bank=bank: e.matmul(bank[:, 0:256], lhsT=toep[:, g, :], rhs=U8[:, g, :],
                                                            start=True, stop=False), reads=[t_toep, t_U8], writes=[bt])
                for ri in range(2):
                    B.op(PE, lambda e, g=g, ri=ri, bank=bank: e.matmul(
                        bank[:, 0:256], lhsT=woutS[:, g, ri, :], rhs=Xb[:, :, ri, g], start=False, stop=(ri == 1)),
                        reads=[t_wo2, t_X], writes=[bt])
                evac(Y8[:, g, :], bank[:, 0:256], [bt], [t_Y8, bt])
            ytv = ytm.rearrange("p (a s) c -> p a s c", a=2)
            t_ysm = Tok()
            for g in range(32):
                bank, bt = next_bank()
                pb = psbf(bank)
                for a in range(2):
                    B.op(PE, lambda e, g=g, a=a, pb=pb: e.transpose(
                        out=pb[:, a * 128:(a + 1) * 128], in_=Y8[:, g, a * 128:(a + 1) * 128], identity=identb[:]),
                        reads=[t_Y8, t_const], writes=[bt])
                evac(ytv[:, :, :, g * 16:(g + 1) * 16], pb[:, 0:256].rearrange("p (a s h) -> p a s h", a=2, s=8),
                     [bt], [t_ysm, bt])
            B.barrier()
            if debug and n == 0:
                B.dma(SP, dbg['d_yssm'][:, :], av(144, 16384)); B.dma(SP, dbg['d_U8'][:, :], av(64, 16384))
                B.dma(SP, dbg['d_Xb'][:, :], av(0, 32768)); B.dma(SP, dbg['d_toep'][:, :], av(112, 8192)); B.dma(SP, dbg['d_Y8'][:, :], av(96, 16384))
                B.barrier()
            uu = av(64, 2048, F32); ww = av(66, 2048, F32); ygb = av(68, 1024); ygT = av(69, 1024).rearrange("p (j t) -> p j t", j=4)
            zz = av(70, 2048, F32); y2 = av(72, 2048, F32); ynb2 = av(74, 1024)
            tg = Tok()
            C0 = 1.5957691216057308
            for ti in range(16):
                a, s = ti // 8, ti % 8
                yv = ytm[:, ti, :]
                B.op(ACT, lambda e, yv=yv: e.activation(out=uu, in_=yv, func=AF.Square), reads=[t_ysm, t_toep], writes=[tg])
                B.op(DVE, lambda e: e.tensor_scalar(out=ww, in0=uu, scalar1=0.044715, scalar2=1.0, op0=ALU.mult, op1=ALU.add),
                     reads=[tg], writes=[tg])
                B.op(DVE, lambda e, yv=yv: e.tensor_tensor(out=ww, in0=ww, in1=yv, op=ALU.mult), reads=[tg, t_ysm], writes=[tg])
                B.op(ACT, lambda e: e.activation(out=uu, in_=ww, func=AF.Sigmoid, scale=C0), reads=[tg], writes=[tg])
                B.op(DVE, lambda e, yv=yv: e.tensor_tensor(out=ygb, in0=uu, in1=yv, op=ALU.mult), reads=[tg, t_ysm], writes=[tg])
                bank, bt = next_bank()
                pb = psbf(bank)
                for jj in range(4):
                    B.op(PE, lambda e, jj=jj, pb=pb: e.transpose(out=pb[:, jj * 128:(jj + 1) * 128],
                                                                 in_=ygb[:, jj * 128:(jj + 1) * 128], identity=identb[:]),
                         reads=[tg, t_const], writes=[bt])
                B.op(ACT, lambda e, pb=pb: e.copy(out=ygT, in_=pb[:, 0:512].rearrange("p (j t) -> p j t", j=4)),
                     reads=[bt], writes=[tg, bt])
                zb, zt = next_bank()
                for jj in range(4):
                    B.op(PE, lambda e, jj=jj, zb=zb: e.matmul(zb[:, :], lhsT=ygT[:, jj, :], rhs=wglu[:, jj, :],
                                                              start=(jj == 0), stop=(jj == 3)), reads=[tg, t_ssmw], writes=[zt])
                B.op(DVE, lambda e, zb=zb: e.tensor_tensor(out=zz, in0=zb[:, :], in1=bglu_sb[:], op=ALU.add),
                     reads=[zt, t_const], writes=[tg, zt])
                B.op(ACT, lambda e: e.activation(out=zz, in_=zz, func=AF.Sigmoid), reads=[tg], writes=[tg])
                B.op(DVE, lambda e: e.tensor_tensor(out=y2, in0=zz, in1=ygb, op=ALU.mult), reads=[tg], writes=[tg])
                rms_rstd(ACT, y2, 512, 2, [tg], None)
                B.op(DVE, lambda e: e.tensor_scalar(out=ynb2[:, 0:512], in0=y2, scalar1=rstd[:, 2:3], scalar2=None,
                                                  op0=ALU.mult), reads=[tg, t_stat], writes=[tg])
                bank, bt = next_bank()
                pb = psbf(bank)
                for jj in range(4):
                    B.op(PE, lambda e, jj=jj, pb=pb: e.transpose(out=pb[:, jj * 128:(jj + 1) * 128],
                                                                 in_=ynb2[:, jj * 128:(jj + 1) * 128], identity=identb[:]),
                         reads=[tg, t_const], writes=[bt])
                for jj in range(4):
                    dstv = mixT[:, jj, a * 1024:(a + 1) * 1024].rearrange("p (c s) -> p c s", s=8)[:, :, s]
                    B.op(ACT, lambda e, jj=jj, pb=pb, dstv=dstv: e.activation(
                        out=dstv, in_=pb[:, jj * 128:(jj + 1) * 128], func=AF.Identity, scale=nssm[:, jj:jj + 1]),
                        reads=[bt, t_const], writes=[t_mix[0], bt])

            B.barrier()
            if debug and n == 0:
                B.dma(SP, dbg['d_mixT'][:, :], av(32, 32768))
                B.barrier()
            t_wo = Tok(); t_g = Tok()
            B.dma(SP, w_out, woutb[:, :, :], writes=[t_wo])
            for (dst, w) in ((gmb, 0), (gfb, 1)):
                src = gscr[n, w:w + 1, :]
                srcb = AP(src.tensor, src.offset, [[0, 128], [1, D]])
                B.dma(SP, dst, srcb, writes=[t_g])
            B.dma(SP, nfb, nfin_d[:, :], writes=[t_g])
            xl = [av(128 + 4 * i, 4096, F32) for i in range(2)]
            xltok = [Tok(), Tok()]
            xn5 = [av(136 + 2 * i, 2048) for i in range(2)]
            xn5tok = [Tok(), Tok()]
            sgt = [av(140 + 2 * i, 2048, F32) for i in range(2)]
            sgtok = [Tok(), Tok()]
            ot = [av(144 + 4 * i, 4096, F32) for i in range(2)]
            ottok = [Tok(), Tok()]
            tmpv = av(152, 4096, F32)
            t_tmp = Tok()
            wgtok = [Tok() for _ in range(NWB)]; wdtok = [Tok() for _ in range(NWB)]
            for tb in range(4 if 'P5' not in SKIP else 0):
                t_x1 = [Tok() for _ in range(4)]
                t_h2 = [Tok() for _ in range(4)]
                t_hid = Tok()
                for tt in range(4):
                    T = tb * 4 + tt
                    i = T % 2
                    B.dma(SP, xl[i], x_tiles[n, T * 128:(T + 1) * 128, :], writes=[xltok[i]])
                    for ob in range(2):
                        bank, bt = next_bank()
                        for kt in range(8):
                            B.op(PE, lambda e, kt=kt, T=T, ob=ob, bank=bank: e.matmul(
                                bank[:, :], lhsT=mixT[:, kt, T * 128:(T + 1) * 128],
                                rhs=w_out[:, kt, ob * 512:(ob + 1) * 512], start=(kt == 0), stop=(kt == 7)),
                                reads=[t_wo, t_mix[0]] + t_mix, writes=[bt])
                        B.op(DVE, lambda e, ob=ob, bank=bank: e.tensor_tensor(
                            out=tmpv[:, 0:512], in0=bank[:, :], in1=gmb[:, ob * 512:(ob + 1) * 512], op=ALU.mult),
                            reads=[bt, t_g], writes=[t_tmp, bt])
                        B.op(DVE, lambda e, ob=ob, tt=tt, i=i: e.tensor_tensor(
                            out=x1[:, tt, ob * 512:(ob + 1) * 512], in0=tmpv[:, 0:512],
                            in1=xl[i][:, ob * 512:(ob + 1) * 512], op=ALU.add),
                            reads=[t_tmp, xltok[i]], writes=[t_x1[tt]])
                    rms_rstd(ACT, x1[:, tt, :], 1024, 0, [t_x1[tt]], None)
                    B.op(DVE, lambda e, tt=tt, i=i: e.tensor_scalar(out=xn5[i], in0=x1[:, tt, :], scalar1=rstd[:, 0:1],
                                                                  scalar2=None, op0=ALU.mult),
                         reads=[t_x1[tt], t_stat], writes=[xn5tok[i]])
                    bank, bt = next_bank()
                    pb = psbf(bank)
                    for kt in range(8):
                        B.op(PE, lambda e, i=i, kt=kt, pb=pb: e.transpose(
                            out=pb[:, kt * 128:(kt + 1) * 128], in_=xn5[i][:, kt * 128:(kt + 1) * 128], identity=identb[:]),
                            reads=[xn5tok[i], t_const], writes=[bt])
                    for kt in range(8):
                        if kt % 2 == 0:
                            B.op(DVE, lambda e, kt=kt, pb=pb, tt=tt: e.tensor_scalar(
                                out=h2T[:, kt, tt * 128:(tt + 1) * 128], in0=pb[:, kt * 128:(kt + 1) * 128],
                                scalar1=sclffn[:, kt, n:n + 1], scalar2=modfm[:, 24 + kt, n:n + 1],
                                op0=ALU.mult, op1=ALU.add), reads=[bt, t_mod], writes=[t_h2[tt], bt])
                        else:
                            B.op(ACT, lambda e, kt=kt, pb=pb, tt=tt: e.activation(
                                out=h2T[:, kt, tt * 128:(tt + 1) * 128], in_=pb[:, kt * 128:(kt + 1) * 128],
                                func=AF.Identity, scale=sclffn[:, kt, n:n + 1], bias=modfm[:, 24 + kt, n:n + 1]),
                                reads=[bt, t_mod], writes=[t_h2[tt], bt])
                for fc in range(NFC):
                    i = fc % NWB
                    B.dma(SP, wgs[i], wgb[fc, :, :, :], writes=[wgtok[i]])
                    B.dma(SP, wus[i], wub[fc, :, :, :], writes=[wgtok[i]])
                    gb, gt = next_bank()
                    for kt in range(8):
                        B.op(PE, lambda e, i=i, kt=kt, gb=gb: e.matmul(gb[:, :], lhsT=wgs[i][:, kt, :], rhs=h2T[:, kt, :],
                                                                        start=(kt == 0), stop=(kt == 7)),
                             reads=[wgtok[i]] + t_h2, writes=[gt])
                    ub, ut = next_bank()
                    for kt in range(8):
                        B.op(PE, lambda e, i=i, kt=kt, ub=ub: e.matmul(ub[:, :], lhsT=wus[i][:, kt, :], rhs=h2T[:, kt, :],
                                                                        start=(kt == 0), stop=(kt == 7)),
                             reads=[wgtok[i]] + t_h2, writes=[ut])
                    si = fc % 2
                    B.op(ACT, lambda e, si=si, gb=gb: e.activation(out=sgt[si], in_=gb[:, :], func=AF.Silu),
                         reads=[gt], writes=[sgtok[si], gt])
                    B.op(DVE, lambda e, i=i, ub=ub, fc=fc: e.tensor_tensor(out=hid[:, fc, :], in0=sgt[si], in1=ub[:, :],
                                                                         op=ALU.mult),
                         reads=[sgtok[si], ut], writes=[t_hid, ut])
                for fc in range(NFC):
                    i = fc % NWB
                    B.dma(SP, wds[i], wdb[fc, :, :], writes=[wdtok[i]])
                    for tt in range(4):
                        for ob in range(2):
                            bi = tt * 2 + ob
                            B.op(PE, lambda e, i=i, fc=fc, tt=tt, ob=ob, bi=bi: e.matmul(
                                psum[bi][:, :], lhsT=hid[:, fc, tt * 128:(tt + 1) * 128],
                                rhs=wds[i][:, ob * 512:(ob + 1) * 512], start=(fc == 0), stop=(fc == NFC - 1)),
                                reads=[wdtok[i], t_hid], writes=[ptok[bi]])
                for tt in range(4):
                    T = tb * 4 + tt
                    for ob in range(2):
                        bi = tt * 2 + ob
                        B.op(DVE, lambda e, ob=ob, bi=bi: e.tensor_tensor(
                            out=tmpv[:, 0:512], in0=psum[bi][:, :], in1=gfb[:, ob * 512:(ob + 1) * 512], op=ALU.mult),
                            reads=[ptok[bi], t_g], writes=[t_tmp, ptok[bi]])
                        B.op(DVE, lambda e, ob=ob, tt=tt: e.tensor_tensor(
                            out=x1[:, tt, ob * 512:(ob + 1) * 512], in0=tmpv[:, 0:512],
                            in1=x1[:, tt, ob * 512:(ob + 1) * 512], op=ALU.add),
                            reads=[t_tmp], writes=[t_x1[tt]])
                    rms_rstd(ACT, x1[:, tt, :], 1024, 3, [t_x1[tt]], None)
                    i = T % 2
                    B.op(DVE, lambda e, tt=tt, i=i: e.scalar_tensor_tensor(
                        out=ot[i], in0=x1[:, tt, :], scalar=rstd[:, 3:4], in1=nfb, op0=ALU.mult, op1=ALU.mult),
                        reads=[t_x1[tt], t_stat, t_g], writes=[ottok[i]])
                    B.dma(SP, y_d[n, T * 128:(T + 1) * 128, :], ot[i], reads=[ottok[i]])
        B.finish()
    return nc


_bglu_holder = {}


def _layout_inputs(inp):
    f = np.float32
    out = {}
    xs = np.concatenate([inp["x_prompt"], inp["x_sample"]], axis=0)
    cs = np.concatenate([inp["c_prompt"], inp["c_sample"]], axis=0)
    shared = {}
    shared["w_ada"] = np.ascontiguousarray(inp["w_ada"][0], dtype=f)
    shared["b_ada"] = np.ascontiguousarray(inp["b_ada"][0:1], dtype=f)
    shared["nmix"] = np.ascontiguousarray(inp["norm_mix"][0].reshape(8, 128).T, dtype=f)
    shared["nffn"] = np.ascontiguousarray(inp["norm_ffn"][0].reshape(8, 128).T, dtype=f)
    shared["nfin"] = np.ascontiguousarray(np.broadcast_to(inp["norm_final"][None, :], (128, D)), dtype=f)
    shared["nssm"] = np.ascontiguousarray(inp["norm_ssm_out"][0].reshape(4, 128).T, dtype=f)
    shared["natt"] = np.ascontiguousarray(inp["norm_attn_out"][0].reshape(4, 128).T, dtype=f)
    shared["bglu"] = np.ascontiguousarray(np.broadcast_to(inp["b_glu"][0][None, :], (128, 512)), dtype=f)
    shared["w_in"] = np.ascontiguousarray(inp["w_in"][0], dtype=f)
    shared["w_glu"] = np.ascontiguousarray(inp["w_glu"][0], dtype=f)
    shared["w_out"] = np.ascontiguousarray(inp["w_out"][0], dtype=f)
    shared["w_g"] = np.ascontiguousarray(inp["w_ffn_gate"][0], dtype=f)
    shared["w_u"] = np.ascontiguousarray(inp["w_ffn_up"][0], dtype=f)
    shared["w_d"] = np.ascontiguousarray(inp["w_ffn_down"][0], dtype=f)

    def dpg(a):
        return np.ascontiguousarray(a.transpose(0, 2, 1).reshape(128, 32), dtype=f)

    shared["are"] = dpg(inp["ssm_a_re"][0])
    shared["aim"] = dpg(inp["ssm_a_im"][0])
    shared["ldt"] = dpg(np.broadcast_to(inp["ssm_log_dt"][0][:, :, None], (2, 32, 64)))
    shared["bre"] = np.ascontiguousarray(inp["ssm_b_re"][0].transpose(0, 2, 1, 3).reshape(128, 32, 16), dtype=f)
    shared["bim"] = np.ascontiguousarray(inp["ssm_b_im"][0].transpose(0, 2, 1, 3).reshape(128, 32, 16), dtype=f)
    shared["cre"] = np.ascontiguousarray(inp["ssm_c_re"][0].transpose(0, 3, 1, 2).reshape(128, 32, 16), dtype=f)
    shared["cim"] = np.ascontiguousarray(inp["ssm_c_im"][0].transpose(0, 3, 1, 2).reshape(128, 32, 16), dtype=f)
    dvec = inp["ssm_d"][0].reshape(32, 16)
    dd = np.zeros((16, 32, 16), f)
    for h in range(16):
        dd[h, :, h] = dvec[:, h]
    shared["dd"] = dd
    rpb = inp["na_rpb"][0]
    b = np.arange(2)[:, None, None, None]
    kc = np.arange(64)[None, :, None, None]
    e = np.arange(16)[None, None, :, None]
    qc = np.arange(64)[None, None, None, :]
    ri = np.clip(e + b, 0, 14) + 0 * kc + 0 * qc
    ci = np.clip(kc - qc + 15, 0, 30) + 0 * e + 0 * b
    t2raw = rpb[:, ri, ci].reshape(8, 128, 16 * 64)
    shared["t2raw"] = np.ascontiguousarray(t2raw, dtype=f)
    cstart = np.clip(qc - 8, 0, 48)
    valid = (kc >= cstart) & (kc < cstart + 16) & ((e + b) <= 14)
    m2 = np.where(valid, 0.0, NEG).astype(f) + np.zeros((2, 64, 16, 64), f)
    shared["m2"] = np.ascontiguousarray(m2.reshape(128, 16 * 64))
    shared["identf"] = np.eye(128, dtype=f)
    in_maps = []
    for core in range(8):
        m = dict(shared)
        m["x"] = np.ascontiguousarray(xs[core * 3:(core + 1) * 3], dtype=f)
        cc = cs[core * 3:(core + 1) * 3]
        cT = np.zeros((128, 8, 4), f)
        cT[:, :, 0:3] = cc.reshape(3, 8, 128).transpose(2, 1, 0)
        m["cT"] = cT
        in_maps.append(m)
    return in_maps


def kernel(**inputs):
    inp = {k: np.asarray(v) for k, v in inputs.items()}
    in_maps = _layout_inputs(inp)
    nc = build_nc()
    res = run_bass_kernel_spmd(nc, in_maps, core_ids=list(range(8)))
    ys = np.concatenate([np.asarray(r["y"]).reshape(NSEQ, L, D) for r in res.results], axis=0).astype(np.float32)
    return ys[0:8], ys[8:24]
```

```python
import math
import numpy as np
import ml_dtypes
import concourse.bass as bass
import concourse.mybir as mybir
from concourse.bass_utils import run_bass_kernel_spmd
from concourse.ap import AP

F32 = mybir.dt.float32
BF16 = mybir.dt.bfloat16
ALU = mybir.AluOpType
AF = mybir.ActivationFunctionType

D = 1024
L = 2048
NSEQ = 3
DFF = 2816
NFC = 22
EPS = 1e-6
NEG = -30000.0


class Tok:
    __slots__ = ("w", "r")

    def __init__(self):
        self.w = None
        self.r = {}


class EngW:
    def __init__(self, eng, sem, same_wait):
        self.eng = eng
        self.sem = sem
        self.n = 0
        self.waited = {}
        self.same_wait = same_wait


class Builder:
    def __init__(self, nc, sems, dma_sems):
        self.nc = nc
        self.pe = EngW(nc.tensor, sems[0], False)
        self.act = EngW(nc.scalar, sems[1], True)
        self.dve = EngW(nc.vector, sems[2], True)
        self.pool = EngW(nc.gpsimd, sems[3], True)
        self.sp = EngW(nc.sync, sems[4], False)
        self.engs = [self.pe, self.act, self.dve, self.pool, self.sp]
        self.dma_sems = dma_sems
        self.dma_n = [0] * len(dma_sems)
        self.dma_rr = 0

    def _wait(self, E, deps):
        for (sem, val) in deps:
            if sem is E.sem and not E.same_wait:
                continue
            k = id(sem)
            if E.waited.get(k, 0) < val:
                E.eng.wait_ge(sem, val)
                E.waited[k] = val

    @staticmethod
    def _deps(reads, writes):
        deps = []
        for t in reads:
            if t.w is not None:
                deps.append(t.w)
        for t in writes:
            if t.w is not None:
                deps.append(t.w)
            for k, v in t.r.items():
                deps.append(v)
        return deps

    @staticmethod
    def _upd(me, reads, writes):
        for t in reads:
            k = id(me[0])
            if k not in t.r or t.r[k][1] < me[1]:
                t.r[k] = me
        for t in writes:
            t.w = me
            t.r = {}

    def op(self, E, fn, reads=(), writes=()):
        self._wait(E, self._deps(reads, writes))
        inst = fn(E.eng)
        E.n += 1
        inst.then_inc(E.sem, 1)
        self._upd((E.sem, E.n), reads, writes)

    def dma(self, Q, out, in_, reads=(), writes=(), **kw):
        k = self.dma_rr
        self.dma_rr = (self.dma_rr + 1) % len(self.dma_sems)
        sem = self.dma_sems[k]
        deps = self._deps(reads, writes)
        if self.dma_n[k] > 0:
            deps.append((sem, 16 * self.dma_n[k]))
        self._wait(Q, deps)
        inst = Q.eng.dma_start(out=out, in_=in_, **kw)
        self.dma_n[k] += 1
        inst.then_inc(sem, 16)
        self._upd((sem, 16 * self.dma_n[k]), reads, writes)

    def barrier(self):
        for E in self.engs:
            deps = [(F.sem, F.n) for F in self.engs if F is not E and F.n > 0]
            deps += [(s, 16 * n) for s, n in zip(self.dma_sems, self.dma_n) if n > 0]
            self._wait(E, deps)

    def finish(self):
        deps = [(s, 16 * n) for s, n in zip(self.dma_sems, self.dma_n) if n > 0]
        deps += [(F.sem, F.n) for F in self.engs if F.n > 0 and F is not self.sp]
        self._wait(self.sp, deps)


def bcast_last(ap, n):
    return AP(ap.tensor, ap.offset, [list(d) for d in ap.ap] + [[0, n]])


SKIP = set()


def build_nc(debug=False):
    nc = bass.Bass("TRN2", target_bir_lowering=False)
    dbg = {}
    def dout(name, shape, dt):
        dbg[name] = nc.dram_tensor(name, list(shape), dt, kind="ExternalOutput").ap()
        return dbg[name]
    if debug:
        dout('d_hT', [128, 8 * 2048], BF16); dout('d_QT', [128, 4 * 2048], BF16); dout('d_KT', [128, 4 * 2048], BF16)
        dout('d_Utm', [128, 8192], BF16); dout('d_yatt', [128, 8192], BF16); dout('d_mixT', [128, 8 * 2048], BF16)
        dout('d_yssm', [128, 8192], BF16); dout('d_Vev', [128, 16 * 8 * 65], BF16); dout('d_mod', [128, 192], F32)
        dout('d_U8', [128, 8192], BF16); dout('d_Xb', [128, 16384], BF16); dout('d_toep', [128, 4096], BF16)
        dout('d_Y8', [128, 8192], BF16); dout('d_T2', [128, 8192], BF16)

    def din(name, shape, dt=F32):
        return nc.dram_tensor(name, list(shape), dt, kind="ExternalInput").ap()

    def dscr(name, shape, dt):
        return nc.dram_tensor(name, list(shape), dt, kind="Internal").ap()

    x_d = din("x", [NSEQ, L, D])
    y_d = nc.dram_tensor("y", [NSEQ, L, D], F32, kind="ExternalOutput").ap()
    cT_d = din("cT", [128, 8, 4])
    wada_d = din("w_ada", [D, 6 * D])
    bada_d = din("b_ada", [1, 6 * D])
    nmix_d = din("nmix", [128, 8])
    nffn_d = din("nffn", [128, 8])
    nfin_d = din("nfin", [128, D])
    nssm_d = din("nssm", [128, 4])
    natt_d = din("natt", [128, 4])
    bglu_d = din("bglu", [128, 512])
    win_d = din("w_in", [D, 2048])
    wglu_d = din("w_glu", [512, 512])
    wout_d = din("w_out", [D, D])
    wg_d = din("w_g", [D, DFF])
    wu_d = din("w_u", [D, DFF])
    wd_d = din("w_d", [DFF, D])
    are_d = din("are", [128, 32])
    aim_d = din("aim", [128, 32])
    ldt_d = din("ldt", [128, 32])
    bre_d = din("bre", [128, 32, 16])
    bim_d = din("bim", [128, 32, 16])
    cre_d = din("cre", [128, 32, 16])
    cim_d = din("cim", [128, 32, 16])
    dd_d = din("dd", [16, 32, 16])
    t2raw_d = din("t2raw", [8, 128, 16 * 64])
    m2_d = din("m2", [128, 16 * 64])
    identf_d = din("identf", [128, 128])

    winb = dscr("winb", [128, 8, 2048], BF16)
    wglub = dscr("wglub", [128, 4, 512], BF16)
    woutb = dscr("woutb", [128, 8, 1024], BF16)
    wgb = dscr("wgb", [NFC, 128, 8, 128], BF16)
    wub = dscr("wub", [NFC, 128, 8, 128], BF16)
    wdb = dscr("wdb", [NFC, 128, 1024], BF16)
    toepb = dscr("toepb", [128, 32, 128], BF16)
    kallb = dscr("kallb", [16, 32, 240], BF16)
    gscr = dscr("gscr", [NSEQ, 2, D], F32)
    winTb = dscr("winTb", [128, 32 * 2 * 128], BF16)
    woutSb = dscr("woutSb", [128, 32 * 2 * 128], BF16)

    from contextlib import ExitStack

    with ExitStack() as es:
        def sb(name, shape, dt):
            return es.enter_context(nc.sbuf_tensor("sb_" + name, list(shape), dt))

        sems = [es.enter_context(nc.semaphore("e%d" % i)) for i in range(5)]
        dsems = [es.enter_context(nc.semaphore("d%d" % i)) for i in range(16)]
        B = Builder(nc, sems, dsems)
        PE, ACT, DVE, POOL, SP = B.pe, B.act, B.dve, B.pool, B.sp

        psum = [es.enter_context(nc.psum_tensor("ps%d" % i, [128, 512], F32)) for i in range(8)]
        ptok = [Tok() for _ in range(8)]
        pstate = {"i": 0, "n": 8}

        def next_bank():
            i = pstate["i"] % pstate["n"]
            pstate["i"] = (i + 1) % pstate["n"]
            return psum[i], ptok[i]

        def psbf(bank):
            return bank[:].bitcast(BF16)

        identf = sb("identf", [128, 128], F32)
        identb = sb("identb", [128, 128], BF16)
        ones1 = sb("ones1", [1, 8], F32)
        epst = sb("epst", [128, 1], F32)
        modfm = sb("modfm", [128, 48, 4], F32)
        sclmix = sb("sclmix", [128, 8, 4], F32)
        sclffn = sb("sclffn", [128, 8, 4], F32)
        nssm = sb("nssm", [128, 4], F32)
        natt = sb("natt", [128, 4], F32)
        A1 = sb("A1", [128, 64], F32)
        A2 = sb("A2", [128, 64], F32)
        T2 = sb("T2", [128, 8, 16 * 64], BF16)
        bglu_sb = sb("bglu_sb", [128, 512], F32)
        wglu = sb("wglu", [128, 4, 512], BF16)
        xstate = sb("xstate", [128, 64], F32)
        st1 = sb("st1", [128, 64], F32)
        st2 = sb("st2", [128, 64], F32)
        ssq = sb("ssq", [128, 4], F32)
        rstd = sb("rstd", [128, 4], F32)
        rec = sb("rec", [128, 8], F32)
        junk = sb("junk", [128, 1024], BF16)
        t_const = Tok()
        t_mod = Tok()
        t_ssmw = Tok()
        t_stat = Tok()
        t_junk = Tok()

        NA = 160 * 512
        arena = sb("arena", [128, NA], BF16)

        def av(off_kb, nbytes, dt=BF16):
            o = int(off_kb * 512)
            ap = arena[:, o:o + nbytes // 2]
            if dt is F32:
                ap = ap.bitcast(F32)
            return ap

        B.dma(SP, identf[:], identf_d[:, :], writes=[t_const])
        B.op(DVE, lambda e: e.tensor_copy(out=identb[:], in_=identf[:]), reads=[t_const], writes=[t_const])
        B.op(DVE, lambda e: e.memset(ones1[:], 1.0), writes=[t_const])
        B.op(DVE, lambda e: e.memset(epst[:], EPS), writes=[t_const])
        B.dma(SP, nssm[:], nssm_d[:, :], writes=[t_const])
        B.dma(SP, natt[:], natt_d[:, :], writes=[t_const])
        B.dma(SP, bglu_sb[:], bglu_d[:, :], writes=[t_const])
        winT = av(128, 16384).rearrange("p (g r c) -> p g r c", g=32, r=2)
        woutS = av(144, 16384).rearrange("p (g r c) -> p g r c", g=32, r=2)

        def f32tile(off_kb, shape):
            n = int(np.prod(shape))
            ap = av(off_kb, n * 4, F32)
            if len(shape) == 2:
                return ap.rearrange("p (a b) -> p a b", a=shape[0]) if False else ap
            return ap

        ts = Tok()
        off = [0.0]

        def alloc(ncols):
            o = off[0]
            off[0] += ncols * 4 / 1024.0
            return av(o, ncols * 4, F32)

        are = alloc(32); aim = alloc(32); ldt = alloc(32)
        lr = alloc(32); dtt = alloc(32); er = alloc(32); ph2 = alloc(64); k2 = alloc(64); sc2 = alloc(64)
        ar = alloc(32); ai = alloc(32); nr = alloc(32); den = alloc(32); zr = alloc(32); zi = alloc(32)
        tA = alloc(32); tB = alloc(32); a8r = alloc(32); a8i = alloc(32)
        B.dma(SP, are, are_d[:, :], writes=[ts])
        B.dma(SP, aim, aim_d[:, :], writes=[ts])
        B.dma(SP, ldt, ldt_d[:, :], writes=[ts])

        def V(fn):
            B.op(DVE, fn, reads=[ts], writes=[ts])

        def A(fn):
            B.op(ACT, fn, reads=[ts], writes=[ts])

        V(lambda e: e.tensor_scalar(out=lr, in0=are, scalar1=-1e-4, scalar2=None, op0=ALU.min))
        A(lambda e: e.activation(out=dtt, in_=ldt, func=AF.Exp))
        V(lambda e: e.tensor_tensor(out=er, in0=lr, in1=dtt, op=ALU.mult))
        A(lambda e: e.activation(out=er, in_=er, func=AF.Exp))
        V(lambda e: e.tensor_tensor(out=ph2[:, 0:32], in0=aim, in1=dtt, op=ALU.mult))
        V(lambda e: e.tensor_scalar(out=ph2[:, 32:64], in0=ph2[:, 0:32], scalar1=math.pi / 2, scalar2=None, op0=ALU.add))
        MAGIC = 12582912.0
        V(lambda e: e.tensor_scalar(out=k2, in0=ph2, scalar1=1.0 / (2 * math.pi), scalar2=MAGIC, op0=ALU.mult, op1=ALU.add))
        V(lambda e: e.tensor_scalar(out=k2, in0=k2, scalar1=-MAGIC, scalar2=None, op0=ALU.add))
        V(lambda e: e.scalar_tensor_tensor(out=ph2, in0=k2, scalar=-2 * math.pi, in1=ph2, op0=ALU.mult, op1=ALU.add))
        V(lambda e: e.tensor_scalar(out=ph2, in0=ph2, scalar1=3.14159, scalar2=-3.14159, op0=ALU.min, op1=ALU.max))
        A(lambda e: e.activation(out=sc2, in_=ph2, func=AF.Sin))
        V(lambda e: e.tensor_tensor(out=ai, in0=er, in1=sc2[:, 0:32], op=ALU.mult))
        V(lambda e: e.tensor_tensor(out=ar, in0=er, in1=sc2[:, 32:64], op=ALU.mult))
        V(lambda e: e.tensor_scalar(out=nr, in0=ar, scalar1=-1.0, scalar2=None, op0=ALU.add))
        V(lambda e: e.tensor_tensor(out=den, in0=lr, in1=lr, op=ALU.mult))
        V(lambda e: e.tensor_tensor(out=tA, in0=aim, in1=aim, op=ALU.mult))
        V(lambda e: e.tensor_tensor(out=den, in0=den, in1=tA, op=ALU.add))
        V(lambda e: e.reciprocal(out=den, in_=den))
        V(lambda e: e.tensor_tensor(out=tA, in0=nr, in1=lr, op=ALU.mult))
        V(lambda e: e.tensor_tensor(out=tB, in0=ai, in1=aim, op=ALU.mult))
        V(lambda e: e.tensor_tensor(out=tA, in0=tA, in1=tB, op=ALU.add))
        V(lambda e: e.tensor_tensor(out=zr, in0=tA, in1=den, op=ALU.mult))
        V(lambda e: e.tensor_tensor(out=tA, in0=ai, in1=lr, op=ALU.mult))
        V(lambda e: e.tensor_tensor(out=tB, in0=nr, in1=aim, op=ALU.mult))
        V(lambda e: e.tensor_tensor(out=tA, in0=tA, in1=tB, op=ALU.subtract))
        V(lambda e: e.tensor_tensor(out=zi, in0=tA, in1=den, op=ALU.mult))
        V(lambda e: e.tensor_copy(out=a8r, in_=ar))
        V(lambda e: e.tensor_copy(out=a8i, in_=ai))
        for _ in range(3):
            V(lambda e: e.tensor_tensor(out=tA, in0=a8r, in1=a8r, op=ALU.mult))
            V(lambda e: e.tensor_tensor(out=tB, in0=a8i, in1=a8i, op=ALU.mult))
            V(lambda e: e.tensor_tensor(out=a8i, in0=a8r, in1=a8i, op=ALU.mult))
            V(lambda e: e.tensor_tensor(out=a8r, in0=tA, in1=tB, op=ALU.subtract))
            V(lambda e: e.tensor_scalar(out=a8i, in0=a8i, scalar1=2.0, scalar2=None, op0=ALU.mult))
        B.op(DVE, lambda e: e.tensor_copy(out=A1[:, 0:32], in_=a8r), reads=[ts], writes=[t_ssmw])
        B.op(DVE, lambda e: e.tensor_copy(out=A1[:, 32:64], in_=a8r), reads=[ts], writes=[t_ssmw])
        B.op(DVE, lambda e: e.tensor_scalar(out=A2[:, 0:32], in0=a8i, scalar1=-1.0, scalar2=None, op0=ALU.mult), reads=[ts], writes=[t_ssmw])
        B.op(DVE, lambda e: e.tensor_copy(out=A2[:, 32:64], in_=a8i), reads=[ts], writes=[t_ssmw])

        def alloc3(nk):
            return alloc(nk * 512).rearrange("p (k g h) -> p k g h", k=nk, g=32)

        TR = alloc3(8); TI_off = off[0]; TI = alloc3(8); GR = alloc3(9); GI = alloc3(9)
        braw = alloc(512).rearrange("p (g h) -> p g h", g=32)
        biraw = alloc(512).rearrange("p (g h) -> p g h", g=32)
        tmp1 = alloc(512).rearrange("p (g h) -> p g h", g=32)
        tmp2 = alloc(512).rearrange("p (g h) -> p g h", g=32)
        B.dma(SP, braw, bre_d[:, :, :], writes=[ts])
        B.dma(SP, biraw, bim_d[:, :, :], writes=[ts])
        B.dma(SP, GR[:, 0], cre_d[:, :, :], writes=[ts])
        B.dma(SP, GI[:, 0], cim_d[:, :, :], writes=[ts])
        zrb = bcast_last(zr, 16); zib = bcast_last(zi, 16)
        arb = bcast_last(ar, 16); aib = bcast_last(ai, 16)

        def cmul(outR, outI, inR, inI, sR, sI):
            V(lambda e: e.tensor_tensor(out=tmp1, in0=inR, in1=sR, op=ALU.mult))
            V(lambda e: e.tensor_tensor(out=tmp2, in0=inI, in1=sI, op=ALU.mult))
            V(lambda e: e.tensor_tensor(out=outR, in0=tmp1, in1=tmp2, op=ALU.subtract))
            V(lambda e: e.tensor_tensor(out=tmp1, in0=inR, in1=sI, op=ALU.mult))
            V(lambda e: e.tensor_tensor(out=tmp2, in0=inI, in1=sR, op=ALU.mult))
            V(lambda e: e.tensor_tensor(out=outI, in0=tmp1, in1=tmp2, op=ALU.add))

        cmul(TR[:, 0], TI[:, 0], braw, biraw, zrb, zib)
        for k in range(1, 8):
            cmul(TR[:, k], TI[:, k], TR[:, k - 1], TI[:, k - 1], arb, aib)
        for k in range(1, 9):
            cmul(GR[:, k], GI[:, k], GR[:, k - 1], GI[:, k - 1], arb, aib)

        for t in range(8):
            for (lo, hi, k) in ((0, 64, t + 1), (64, 128, 8 - t)):
                B.op(DVE, lambda e, lo=lo, hi=hi, k=k, t=t: e.tensor_copy(
                    out=woutS[lo:hi, :, 0, t * 16:(t + 1) * 16], in_=GR[lo:hi, k]), reads=[ts], writes=[t_ssmw])
                B.op(DVE, lambda e, lo=lo, hi=hi, k=k, t=t: e.tensor_scalar(
                    out=woutS[lo:hi, :, 1, t * 16:(t + 1) * 16], in0=GI[lo:hi, k], scalar1=-1.0, scalar2=None,
                    op0=ALU.mult), reads=[ts], writes=[t_ssmw])
        ddt = alloc(512).rearrange("p (g h) -> p g h", g=32)
        winp_off = off[0]
        winp = alloc(32 * 2 * 128).rearrange("p (g r c) -> p g r c", g=32, r=2)
        for s in range(8):
            for (lo, hi, k) in ((0, 64, 7 - s), (64, 128, s)):
                V(lambda e, lo=lo, hi=hi, k=k, s=s: e.tensor_copy(out=winp[lo:hi, :, 0, s * 16:(s + 1) * 16], in_=TR[lo:hi, k]))
                V(lambda e, lo=lo, hi=hi, k=k, s=s: e.tensor_copy(out=winp[lo:hi, :, 1, s * 16:(s + 1) * 16], in_=TI[lo:hi, k]))
        for g in range(32):
            bank, bt = next_bank()
            for ri in range(2):
                B.op(PE, lambda e, g=g, ri=ri, bank=bank: e.transpose(
                    out=bank[:, ri * 128:(ri + 1) * 128], in_=winp[:, g, ri, :], identity=identf[:]),
                    reads=[ts, t_const], writes=[bt])
            B.op(ACT if g % 2 else DVE, (lambda e, g=g, bank=bank: e.tensor_copy(
                out=winT[:, g, :, :], in_=bank[:, 0:256].rearrange("p (r c) -> p r c", r=2))) if g % 2 == 0 else
                (lambda e, g=g, bank=bank: e.copy(
                    out=winT[:, g, :, :], in_=bank[:, 0:256].rearrange("p (r c) -> p r c", r=2))),
                reads=[bt], writes=[t_ssmw, bt])

        V(lambda e: e.tensor_scalar(out=tmp1, in0=TI[:, 0], scalar1=-1.0, scalar2=None, op0=ALU.mult))
        B.dma(SP, ddt[0:16], dd_d[:, :, :], writes=[ts])
        gbase = winp_off
        GallR = av(gbase, 16 * 240 * 4, F32).rearrange("p (g t h) -> p g t h", g=16, t=15)
        GallI = av(gbase + 15, 16 * 240 * 4, F32).rearrange("p (g t h) -> p g t h", g=16, t=15)
        kall = av(gbase + 30, 16 * 240 * 4, F32).rearrange("p (g c) -> p g c", g=16)
        kallbf = av(TI_off, 16 * 240 * 2).rearrange("p (g c) -> p g c", g=16)
        for gh in range(2):
            gs = slice(gh * 16, (gh + 1) * 16)
            V(lambda e: e.memset(GallR, 0.0))
            V(lambda e: e.memset(GallI, 0.0))
            for k in range(8):
                V(lambda e, k=k: e.tensor_copy(out=GallR[0:64, :, 7 + k, :], in_=GR[0:64, k, gs, :]))
                V(lambda e, k=k: e.tensor_copy(out=GallI[0:64, :, 7 + k, :], in_=GI[0:64, k, gs, :]))
                V(lambda e, k=k: e.tensor_copy(out=GallR[64:128, :, 7 - k, :], in_=GR[64:128, k, gs, :]))
                V(lambda e, k=k: e.tensor_copy(out=GallI[64:128, :, 7 - k, :], in_=GI[64:128, k, gs, :]))
            for g2 in range(8):
                bank, bt = next_bank()
                for gi in range(2):
                    gl = g2 * 2 + gi
                    g = gh * 16 + gl
                    B.op(PE, lambda e, g=g, gl=gl, gi=gi, bank=bank: e.matmul(
                        bank[0:16, gi * 240:(gi + 1) * 240], lhsT=TR[:, 0, g, :],
                        rhs=GallR[:, gl].rearrange("p t h -> p (t h)"), start=True, stop=False),
                        reads=[ts], writes=[bt])
                    B.op(PE, lambda e, g=g, gl=gl, gi=gi, bank=bank: e.matmul(
                        bank[0:16, gi * 240:(gi + 1) * 240], lhsT=tmp1[:, g, :],
                        rhs=GallI[:, gl].rearrange("p t h -> p (t h)"), start=False, stop=True),
                        reads=[ts], writes=[bt])
                B.op(DVE, lambda e, g2=g2, bank=bank: e.tensor_copy(
                    out=kall[0:16, g2 * 2:g2 * 2 + 2, :], in_=bank[0:16, 0:480].rearrange("p (g c) -> p g c", g=2)),
                    reads=[bt], writes=[ts, bt])
            V(lambda e: e.tensor_tensor(out=kall[0:16, :, 112:128], in0=kall[0:16, :, 112:128], in1=ddt[0:16, gs, :], op=ALU.add))
            V(lambda e: e.tensor_copy(out=kallbf[0:16], in_=kall[0:16]))
            B.dma(SP, kallb[:, gs, :], kallbf[0:16], reads=[ts], writes=[ts])
        for s in range(8):
            B.dma(SP, toepb[s * 16:(s + 1) * 16, :, :], kallb[:, :, (7 - s) * 16:(7 - s) * 16 + 128],
                  reads=[ts], writes=[ts])
        B.dma(SP, winTb[:, :], winT.rearrange("p g r c -> p (g r c)"), reads=[t_ssmw], writes=[ts])
        B.dma(SP, woutSb[:, :], woutS.rearrange("p g r c -> p (g r c)"), reads=[t_ssmw], writes=[ts])

        B.barrier()
        cT = av(0, 128, F32).rearrange("p (k n) -> p k n", k=8)
        sil = av(1, 128, F32).rearrange("p (k n) -> p k n", k=8)
        badar = av(2, 6 * D * 4, F32)
        wblk = [av(32 + 16 * i, 8 * 512 * 4, F32).rearrange("p (k c) -> p k c", k=8) for i in range(2)]
        wtok = [Tok(), Tok()]
        tm = Tok()
        B.dma(SP, cT, cT_d[:, :, :], writes=[tm])
        B.dma(SP, badar[0:1], bada_d[:, :], writes=[tm])
        B.op(ACT, lambda e: e.activation(out=sil, in_=cT, func=AF.Silu), reads=[tm], writes=[tm])
        wada_v = wada_d.rearrange("(k p) c -> p k c", p=128)
        mbank, mtok = next_bank()
        for blk in range(12):
            wb = wblk[blk % 2]
            B.dma(SP, wb, wada_v[:, :, blk * 512:(blk + 1) * 512], writes=[wtok[blk % 2]])
            for j in range(4):
                ct = blk * 4 + j
                o = (ct % 48) * 4
                B.op(PE, lambda e, ct=ct, o=o: e.matmul(
                    mbank[:, o:o + 4], lhsT=badar[0:1, ct * 128:(ct + 1) * 128], rhs=ones1[0:1, 0:4],
                    start=True, stop=False), reads=[tm, t_const], writes=[mtok])
                for kt in range(8):
                    B.op(PE, lambda e, wb=wb, j=j, kt=kt, o=o: e.matmul(
                        mbank[:, o:o + 4], lhsT=wb[:, kt, j * 128:(j + 1) * 128], rhs=sil[:, kt, :],
                        start=False, stop=(kt == 7)), reads=[tm, wtok[blk % 2]], writes=[mtok])
        B.op(DVE, lambda e: e.tensor_copy(out=modfm[:], in_=mbank[:, 0:192].rearrange("p (c n) -> p c n", n=4)),
             reads=[mtok], writes=[t_mod, mtok])
        nm = av(3, 32, F32); nf = av(3.5, 32, F32)
        B.dma(SP, nm, nmix_d[:, :], writes=[tm])
        B.dma(SP, nf, nffn_d[:, :], writes=[tm])
        for (dst, nrm, base) in ((sclmix, nm, 8), (sclffn, nf, 32)):
            B.op(DVE, lambda e, dst=dst, base=base: e.tensor_scalar(
                out=dst[:], in0=modfm[:, base:base + 8, :], scalar1=1.0, scalar2=None, op0=ALU.add),
                reads=[t_mod, tm], writes=[t_mod])
            B.op(DVE, lambda e, dst=dst, nrm=nrm: e.tensor_tensor(
                out=dst[:], in0=dst[:], in1=bcast_last(nrm, 4), op=ALU.mult), reads=[t_mod, tm], writes=[t_mod])
        for w, base in ((0, 16), (1, 40)):
            for n in range(NSEQ):
                B.dma(SP, gscr[n, w, :].rearrange("(k p) -> p k", p=128), modfm[:, base:base + 8, n],
                      reads=[t_mod], writes=[tm], allow_slow_non_contiguous=True)

        B.barrier()
        stg = [av(0 + 8 * i, 2048 * 4, F32) for i in range(3)]
        stgb = [av(32 + 4 * i, 2048 * 2) for i in range(3)]
        stok = [Tok() for _ in range(3)]
        sbtok = [Tok() for _ in range(3)]
        cnt = [0]

        def conv(src, dst, ncol):
            i = cnt[0] % 3
            cnt[0] += 1
            B.dma(SP, stg[i][:, 0:ncol], src, writes=[stok[i]])
            E = (ACT, DVE, POOL)[i]
            if E is ACT:
                B.op(E, lambda e: e.copy(out=stgb[i][:, 0:ncol], in_=stg[i][:, 0:ncol]), reads=[stok[i]], writes=[sbtok[i]])
            else:
                B.op(E, lambda e: e.tensor_copy(out=stgb[i][:, 0:ncol], in_=stg[i][:, 0:ncol]), reads=[stok[i]], writes=[sbtok[i]])
            return i

        for kt in range(8):
            i = conv(win_d[kt * 128:(kt + 1) * 128, :], None, 2048)
            B.dma(SP, winb[:, kt, :], stgb[i][:, 0:2048], reads=[sbtok[i]])
        for kt in range(4):
            i = conv(wglu_d[kt * 128:(kt + 1) * 128, :], None, 512)
            B.dma(SP, wglub[:, kt, :], stgb[i][:, 0:512], reads=[sbtok[i]])
        for kt in range(8):
            i = conv(wout_d[kt * 128:(kt + 1) * 128, :], None, 1024)
            B.dma(SP, woutb[:, kt, :], stgb[i][:, 0:1024], reads=[sbtok[i]])
        for (wsrc, wdst) in ((wg_d, wgb), (wu_d, wub)):
            for kt in range(8):
                for hf in range(2):
                    i = conv(wsrc[kt * 128:(kt + 1) * 128, hf * 1408:(hf + 1) * 1408], None, 1408)
                    B.dma(SP, wdst[hf * 11:(hf + 1) * 11, :, kt, :].rearrange("f p j -> p f j"),
                          stgb[i][:, 0:1408].rearrange("p (f j) -> p f j", j=128), reads=[sbtok[i]])
        for fc in range(NFC):
            i = conv(wd_d[fc * 128:(fc + 1) * 128, :], None, 1024)
            B.dma(SP, wdb[fc, :, :], stgb[i][:, 0:1024], reads=[sbtok[i]])

        B.barrier()
        m2t = av(64, 1024 * 4, F32)
        t2r = [av(72 + 4 * i, 1024 * 4, F32) for i in range(2)]
        t2tok = [Tok(), Tok()]
        tm2 = Tok()
        t_T2 = Tok()
        B.dma(SP, m2t, m2_d[:, :], writes=[tm2])
        for h in range(8):
            B.dma(SP, t2r[h % 2], t2raw_d[h, :, :], writes=[t2tok[h % 2]])
            B.op(DVE, lambda e, h=h: e.tensor_tensor(out=T2[:, h, :], in0=t2r[h % 2], in1=m2t, op=ALU.add),
                 reads=[t2tok[h % 2], tm2], writes=[t_T2])
        B.barrier()
        B.dma(SP, wglu[:], wglub[:, :, :], writes=[t_ssmw])

        hT = av(0, 32768).rearrange("p (k t) -> p k t", k=8)
        Xb = av(0, 32768).rearrange("p (c r g) -> p c r g", c=256, r=2)
        w_in = av(32, 32768).rearrange("p (k c) -> p k c", k=8)
        mixT = av(32, 32768).rearrange("p (k t) -> p k t", k=8)
        QT = av(64, 16384).rearrange("p (h t) -> p h t", h=4)
        KT = av(80, 16384).rearrange("p (h t) -> p h t", h=4)
        Vev = av(96, 16 * 8 * 65 * 2).rearrange("p (i h d) -> p i h d", i=16, h=8)
        Vod = av(112.5, 15 * 8 * 65 * 2).rearrange("p (i h d) -> p i h d", i=15, h=8)
        Utm = av(128, 16384).rearrange("p (a g s h) -> p a g s h", a=2, g=32, s=8)
        ytm = av(144, 16384).rearrange("p (i c) -> p i c", i=16)
        U8 = av(64, 16384).rearrange("p (g c) -> p g c", g=32)
        Sblk = av(80, 16384, F32).rearrange("p (c r g) -> p c r g", c=64, r=2)
        Y8 = av(96, 16384).rearrange("p (g c) -> p g c", g=32)
        toep = av(112, 8192).rearrange("p (g c) -> p g c", g=32)
        w_out = av(0, 16384).rearrange("p (k c) -> p k c", k=8)
        h2T = av(16, 8192).rearrange("p (k t) -> p k t", k=8)
        hid = av(64, NFC * 512 * 2).rearrange("p (f t) -> p f t", f=NFC)
        x1 = av(86, 4 * 1024 * 4, F32).rearrange("p (t c) -> p t c", t=4)
        NWB = 4
        wgs = [av(102 + 4 * i, 2048).rearrange("p (k j) -> p k j", k=8) for i in range(NWB)]
        wus = [av(102 + 4 * i + 2, 2048).rearrange("p (k j) -> p k j", k=8) for i in range(NWB)]
        wds = [av(118 + 2 * i, 2048) for i in range(NWB)]
        gmb = av(24, 4096, F32)
        gfb = av(28, 4096, F32)
        nfb = av(156, 4096, F32)

        x_tiles = x_d

        def rms_rstd(E, src_ap, ncol, col, reads, tokw):
            B.op(ACT, lambda e: e.activation(out=junk[:, 0:ncol], in_=src_ap, func=AF.Square,
                                             accum_out=ssq[:, col:col + 1]),
                 reads=list(reads) + [], writes=[t_stat, t_junk])
            B.op(ACT, lambda e: e.activation(out=rstd[:, col:col + 1], in_=ssq[:, col:col + 1], func=AF.Sqrt,
                                             bias=epst[:], scale=1.0 / ncol), reads=[t_stat, t_const], writes=[t_stat])
            B.op(DVE, lambda e: e.reciprocal(out=rstd[:, col:col + 1], in_=rstd[:, col:col + 1]),
                 reads=[t_stat], writes=[t_stat])

        for n in range(NSEQ):
            B.barrier()
            xtile = [av(144 + 4 * i, 4096, F32) for i in range(2)]
            xnb = [av(152 + 2 * i, 2048) for i in range(2)]
            xtok = [Tok(), Tok()]
            xntok = [Tok(), Tok()]
            t_hT = [Tok() for _ in range(16)]
            t_win = Tok()
            B.dma(SP, w_in, winb[:, :, :], writes=[t_win])
            for tt in range(16):
                i = tt % 2
                B.dma(SP, xtile[i], x_tiles[n, tt * 128:(tt + 1) * 128, :], writes=[xtok[i]])
                rms_rstd(ACT, xtile[i], 1024, 0, [xtok[i]], None)
                B.op(DVE, lambda e, i=i: e.tensor_scalar(out=xnb[i], in0=xtile[i], scalar1=rstd[:, 0:1], scalar2=None,
                                                       op0=ALU.mult), reads=[xtok[i], t_stat], writes=[xntok[i]])
                bank, bt = next_bank()
                pb = psbf(bank)
                for kt in range(8):
                    B.op(PE, lambda e, i=i, kt=kt, pb=pb: e.transpose(
                        out=pb[:, kt * 128:(kt + 1) * 128], in_=xnb[i][:, kt * 128:(kt + 1) * 128], identity=identb[:]),
                        reads=[xntok[i], t_const], writes=[bt])
                for kt in range(8):
                    if kt % 2 == 0:
                        B.op(DVE, lambda e, kt=kt, pb=pb, tt=tt: e.tensor_scalar(
                            out=hT[:, kt, tt * 128:(tt + 1) * 128], in0=pb[:, kt * 128:(kt + 1) * 128],
                            scalar1=sclmix[:, kt, n:n + 1], scalar2=modfm[:, kt, n:n + 1], op0=ALU.mult, op1=ALU.add),
                            reads=[bt, t_mod], writes=[t_hT[tt], bt])
                    else:
                        B.op(ACT, lambda e, kt=kt, pb=pb, tt=tt: e.activation(
                            out=hT[:, kt, tt * 128:(tt + 1) * 128], in_=pb[:, kt * 128:(kt + 1) * 128],
                            func=AF.Identity, scale=sclmix[:, kt, n:n + 1], bias=modfm[:, kt, n:n + 1]),
                            reads=[bt, t_mod], writes=[t_hT[tt], bt])
            t_Q = Tok(); t_K = Tok(); t_V = Tok(); t_U = Tok()
            flip = [0]

            def evac(out_ap, in_ap, reads, writes, scale=None):
                flip[0] ^= 1
                if flip[0]:
                    if scale is None:
                        B.op(DVE, lambda e: e.tensor_copy(out=out_ap, in_=in_ap), reads=reads, writes=writes)
                    else:
                        B.op(DVE, lambda e: e.tensor_scalar(out=out_ap, in0=in_ap, scalar1=scale, scalar2=None,
                                                          op0=ALU.mult), reads=reads, writes=writes)
                else:
                    if scale is None:
                        B.op(ACT, lambda e: e.copy(out=out_ap, in_=in_ap), reads=reads, writes=writes)
                    else:
                        B.op(ACT, lambda e: e.mul(out=out_ap, in_=in_ap, mul=scale), reads=reads, writes=writes)

            for (dst, cbase, tk, scl) in ((QT, 512, t_Q, 0.125), (KT, 1024, t_K, None)):
                for hp in range(4):
                    for tb in range(4):
                        bank, bt = next_bank()
                        for kt in range(8):
                            B.op(PE, lambda e, kt=kt, hp=hp, tb=tb, bank=bank, cbase=cbase: e.matmul(
                                bank[:, :], lhsT=w_in[:, kt, cbase + hp * 128:cbase + (hp + 1) * 128],
                                rhs=hT[:, kt, tb * 512:(tb + 1) * 512], start=(kt == 0), stop=(kt == 7)),
                                reads=[t_win] + t_hT[tb * 4:tb * 4 + 4], writes=[bt])
                        evac(dst[:, hp, tb * 512:(tb + 1) * 512], bank[:, :], [bt], [tk, bt], scale=scl)
            B.op(POOL, lambda e: e.memset(Vev[:, :, :, 64:65], 1.0), writes=[t_V])
            B.op(POOL, lambda e: e.memset(Vod[:, :, :, 64:65], 1.0), writes=[t_V])
            for (dst, ntile, tok0) in ((Vev, 16, 0), (Vod, 15, 64)):
                for i in range(ntile):
                    t0 = tok0 + i * 128
                    bank, bt = next_bank()
                    for kt in range(8):
                        B.op(PE, lambda e, kt=kt, t0=t0, bank=bank: e.matmul(
                            bank[:, :], lhsT=hT[:, kt, t0:t0 + 128], rhs=w_in[:, kt, 1536:2048],
                            start=(kt == 0), stop=(kt == 7)),
                            reads=[t_win] + t_hT[t0 // 128:(t0 + 127) // 128 + 1], writes=[bt])
                    evac(dst[:, i, :, 0:64], bank[:, :].rearrange("p (h d) -> p h d", h=8), [bt], [t_V, bt])
            for a in range(2):
                for s in range(8):
                    bank, bt = next_bank()
                    for kt in range(8):
                        hsl = hT[:, kt, a * 1024:(a + 1) * 1024].rearrange("p (c s) -> p c s", s=8)[:, :, s]
                        B.op(PE, lambda e, kt=kt, hsl=hsl, bank=bank: e.matmul(
                            bank[:, :], lhsT=hsl, rhs=w_in[:, kt, 0:512], start=(kt == 0), stop=(kt == 7)),
                            reads=[t_win] + t_hT[a * 8:a * 8 + 8], writes=[bt])
                    evac(Utm[:, a, :, s, :], bank[:, :].rearrange("p (g h) -> p g h", g=32), [bt], [t_U, bt])

            B.barrier()
            if debug and n == 0:
                B.dma(SP, dbg['d_hT'][:, :], av(0, 32768)); B.dma(SP, dbg['d_QT'][:, :], av(64, 16384)); B.dma(SP, dbg['d_KT'][:, :], av(80, 16384))
                B.dma(SP, dbg['d_Utm'][:, :], av(128, 16384)); B.dma(SP, dbg['d_Vev'][:, :], av(96, 16 * 8 * 65 * 2))
                B.dma(SP, dbg['d_mod'][:, :], modfm[:].rearrange("p c n -> p (c n)")); B.dma(SP, dbg['d_T2'][:, :], T2[:].rearrange("p h c -> p (h c)"))
                B.barrier()
            P3ON = 'P3' not in SKIP
            sct = [av(0 + 1 * i, 1024, F32).rearrange("p (t q) -> p t q", t=4) for i in range(4)]
            ptt = [av(4 + 0.5 * i, 512).rearrange("p (t q) -> p t q", t=4) for i in range(4)]
            sctok = [Tok() for _ in range(4)]
            pttok = [Tok() for _ in range(4)]
            ynb = av(8, 1024)
            t_yn = Tok()
            t_ytm = [Tok() for _ in range(16)]
            t_mix = [Tok() for _ in range(16)]
            pstate["n"] = 4
            pstate["i"] = 0
            NJ = 4
            LA = 2
            iters = []
            for i in range(16 if P3ON else 0):
                for hg in range(2):
                    for hl in range(4):
                        for a in range(2):
                            iters.append((i, hg, hl, a))
            sc_state = {}

            def stage_scores(idx):
                i, hg, hl, a = iters[idx]
                h = hg * 4 + hl
                hp, hb = h // 2, (h % 2) * 64
                r = 2 * i + a
                rs = min(max(r - 4, 0), 24)
                e0 = rs - r + 7
                j = idx % NJ
                sbank, sbt = next_bank()
                for t in range(4):
                    B.op(PE, lambda e, t=t: e.matmul(
                        sbank[:, t * 64:(t + 1) * 64],
                        lhsT=KT[hb:hb + 64, hp, (rs + 2 * t) * 64:(rs + 2 * t) * 64 + 128],
                        rhs=QT[hb:hb + 64, hp, r * 64:(r + 1) * 64], start=True, stop=True),
                        reads=[t_Q, t_K], writes=[sbt])
                t2v = T2[:, h, :].rearrange("p (e q) -> p e q", e=16)[:, e0:e0 + 7:2, :]
                B.op(DVE, lambda e: e.tensor_tensor(
                    out=sct[j], in0=sbank[:, 0:256].rearrange("p (t q) -> p t q", t=4), in1=t2v, op=ALU.add),
                    reads=[sbt, t_T2], writes=[sctok[j], sbt])
                B.op(ACT, lambda e: e.activation(out=ptt[j], in_=sct[j], func=AF.Exp),
                     reads=[sctok[j]], writes=[pttok[j]])

            def stage_pv(idx):
                i, hg, hl, a = iters[idx]
                h = hg * 4 + hl
                r = 2 * i + a
                rs = min(max(r - 4, 0), 24)
                Vt, vb = (Vev, rs // 2) if rs % 2 == 0 else (Vod, (rs - 1) // 2)
                j = idx % NJ
                bi = 4 + (i % 2) * 2 + hg
                pbank, pbt = psum[bi], ptok[bi]
                for t in range(4):
                    B.op(PE, lambda e, t=t: e.matmul(
                        pbank[a * 64:(a + 1) * 64, hl * 65:(hl + 1) * 65], lhsT=ptt[j][:, t, :],
                        rhs=Vt[:, vb + t, h, :], start=(t == 0), stop=(t == 3)),
                        reads=[pttok[j], t_V], writes=[pbt])
                if hg == 1 and hl == 3 and a == 1:
                    finish_rowpair(i)

            def finish_rowpair(i):
                for hg in range(2):
                    bi = 4 + (i % 2) * 2 + hg
                    pbank, pbt = psum[bi], ptok[bi]
                    pv = pbank[:, 0:260].rearrange("p (h d) -> p h d", h=4)
                    B.op(DVE, lambda e: e.reciprocal(out=rec[:, hg * 4:(hg + 1) * 4], in_=pv[:, :, 64]),
                         reads=[pbt], writes=[t_stat])
                    B.op(DVE, lambda e: e.tensor_tensor(
                        out=ytm[:, i, hg * 256:(hg + 1) * 256].rearrange("p (h d) -> p h d", h=4), in0=pv[:, :, 0:64],
                        in1=bcast_last(rec[:, hg * 4:(hg + 1) * 4], 64), op=ALU.mult),
                        reads=[pbt, t_stat], writes=[t_ytm[i], pbt])
                rms_rstd(ACT, ytm[:, i, :], 512, 1, [t_ytm[i]], None)
                B.op(DVE, lambda e: e.tensor_scalar(out=ynb[:, 0:512], in0=ytm[:, i, :], scalar1=rstd[:, 1:2],
                                                  scalar2=None, op0=ALU.mult),
                     reads=[t_ytm[i], t_stat], writes=[t_yn])
                bank, bt = next_bank()
                pb = psbf(bank)
                for jj in range(4):
                    B.op(PE, lambda e, jj=jj: e.transpose(out=pb[:, jj * 128:(jj + 1) * 128],
                                                          in_=ynb[:, jj * 128:(jj + 1) * 128], identity=identb[:]),
                         reads=[t_yn, t_const], writes=[bt])
                for jj in range(4):
                    B.op(ACT, lambda e, jj=jj: e.activation(
                        out=mixT[:, 4 + jj, i * 128:(i + 1) * 128], in_=pb[:, jj * 128:(jj + 1) * 128],
                        func=AF.Identity, scale=natt[:, jj:jj + 1]), reads=[bt, t_const], writes=[t_mix[i], bt])

            for idx in range(len(iters) + LA):
                if idx < len(iters):
                    stage_scores(idx)
                if idx >= LA and idx - LA < len(iters):
                    stage_pv(idx - LA)

            pstate["n"] = 8
            B.barrier()
            if debug and n == 0:
                B.dma(SP, dbg['d_yatt'][:, :], av(144, 16384))
                B.barrier()
            t_U8 = Tok(); t_S = [Tok(), Tok()]; t_X = Tok(); t_Y8 = Tok(); t_toep = Tok(); t_xs = [Tok(), Tok()]
            B.dma(SP, toep, toepb[:, :, :], writes=[t_toep])
            winT = av(128, 16384).rearrange("p (g r c) -> p g r c", g=32, r=2)
            woutS = av(80, 16384).rearrange("p (g r c) -> p g r c", g=32, r=2)
            t_wi = Tok(); t_wo2 = Tok()
            for g in range(32):
                bank, bt = next_bank()
                pb = psbf(bank)
                for a in range(2):
                    B.op(PE, lambda e, g=g, a=a, pb=pb: e.transpose(
                        out=pb[:, a * 128:(a + 1) * 128], in_=Utm[:, a, g, :, :].rearrange("p s h -> p (s h)"), identity=identb[:]),
                        reads=[t_U, t_const], writes=[bt])
                evac(U8[:, g, :], pb[:, 0:256], [bt], [t_U8, bt])
            B.dma(SP, winT.rearrange("p g r c -> p (g r c)"), winTb[:, :], writes=[t_wi, t_U])
            B.op(DVE, lambda e: e.memset(xstate[:], 0.0), writes=t_xs)
            B.op(DVE, lambda e: e.memset(Xb[0:64, 0], 0.0), writes=[t_X])
            B.op(POOL, lambda e: e.memset(Xb[64:128, 255], 0.0), writes=[t_X])
            halves = ((0, 128, DVE, 0),)

            def u8rev(g, cb):
                t = U8[:, g, cb * 64:(cb + 1) * 64]
                return AP(t.tensor, t.offset + 63, [list(t.ap[0]), [-1, 64]])
            for jb in range(4):
                for gq in range(8):
                    bank, bt = next_bank()
                    for gl in range(4):
                        g = gq * 4 + gl
                        for ri in range(2):
                            col = (gl * 2 + ri) * 64
                            B.op(PE, lambda e, g=g, ri=ri, col=col, bank=bank: e.matmul(
                                bank[0:64, col:col + 64], lhsT=winT[:, g, ri, 0:64],
                                rhs=U8[:, g, jb * 64:(jb + 1) * 64], start=True, stop=True),
                                reads=[t_U8, t_wi], writes=[bt])
                            B.op(PE, lambda e, g=g, ri=ri, col=col, bank=bank: e.matmul(
                                bank[64:128, col:col + 64], lhsT=winT[:, g, ri, 64:128],
                                rhs=u8rev(g, 3 - jb), start=True, stop=True),
                                reads=[t_U8, t_wi], writes=[bt])
                    src = bank[:, :].rearrange("p (g r c) -> p g r c", g=4, r=2)
                    dst = Sblk[:, :, :, gq * 4:(gq + 1) * 4].rearrange("p c r g -> p g r c")
                    B.op(ACT, lambda e, src=src, dst=dst: e.copy(out=dst, in_=src), reads=[bt], writes=[t_S[0], t_S[1], bt])
                for stp in range(64 if 'SCAN' not in SKIP else 0):
                    for (lo, hi, E, d) in halves:
                        c = stp
                        if stp == 0:
                            prev = xstate[lo:hi, :]
                        else:
                            prev = Sblk[lo:hi, c - 1].rearrange("p r g -> p (r g)")
                        pv = prev
                        psw = AP(pv.tensor, pv.offset + 32, [list(pv.ap[0]), [-32, 2], [1, 32]])
                        cur = Sblk[lo:hi, c].rearrange("p r g -> p (r g)")
                        rd = [t_S[d], t_xs[d], t_ssmw]
                        B.op(E, lambda e, lo=lo, hi=hi, pv=pv: e.tensor_tensor(
                            out=st1[lo:hi, :], in0=A1[lo:hi, :], in1=pv, op=ALU.mult), reads=rd, writes=[t_xs[d]])
                        B.op(E, lambda e, lo=lo, hi=hi, psw=psw: e.tensor_tensor(
                            out=st2[lo:hi, :].rearrange("p (r g) -> p r g", r=2),
                            in0=A2[lo:hi, :].rearrange("p (r g) -> p r g", r=2), in1=psw, op=ALU.mult),
                            reads=rd, writes=[t_xs[d]])
                        B.op(E, lambda e, lo=lo, hi=hi: e.tensor_tensor(
                            out=st1[lo:hi, :], in0=st1[lo:hi, :], in1=st2[lo:hi, :], op=ALU.add),
                            reads=[t_xs[d]], writes=[t_xs[d]])
                        B.op(E, lambda e, lo=lo, hi=hi, cur=cur: e.tensor_tensor(
                            out=cur, in0=st1[lo:hi, :], in1=cur, op=ALU.add), reads=[t_xs[d]], writes=[t_S[d], t_xs[d]])
                cb = jb
                nn = 64 if cb < 3 else 63
                B.op(ACT, lambda e: e.copy(out=Xb[0:64, cb * 64 + 1:cb * 64 + 1 + nn], in_=Sblk[0:64, 0:nn]),
                     reads=[t_S[0]], writes=[t_X])
                cbw = 3 - jb
                nb = 64 if cbw > 0 else 63
                xb2 = av(0, 32768)[64:128, :]
                xrev = AP(xb2.tensor, xb2.offset + (cbw * 64 + 62) * 64, [list(xb2.ap[0]), [-64, nb], [1, 64]])
                B.op(DVE, lambda e: e.tensor_copy(out=xrev, in_=Sblk[64:128, 0:nb].rearrange("p c r g -> p c (r g)")),
                     reads=[t_S[0]], writes=[t_X])
                B.op(DVE, lambda e: e.tensor_copy(out=xstate[:, :], in_=Sblk[:, 63].rearrange("p r g -> p (r g)")),
                     reads=[t_S[0]], writes=[t_xs[0]])
            B.dma(SP, woutS.rearrange("p g r c -> p (g r c)"), woutSb[:, :], writes=[t_wo2, t_S[0], t_S[1]])
            for g in range(32):
                bank, bt = next_bank()
                B.op(PE, lambda e, g=g, bank=bank: e.matmul(bank[:, 0:256], lhsT=toep[:, g, :], rhs=U8[:, g, :],
                                                            start=True, stop=False), reads=[t_toep, t_U8], writes=[bt])
                for ri in range(2):
                    B.op(PE, lambda e, g=g, ri=ri, bank=bank: e.matmul(
                        bank[:, 0:256], lhsT=woutS[:, g, ri, :], rhs=Xb[:, :, ri, g], start=False, stop=(ri == 1)),
                        reads=[t_wo2, t_X], writes=[bt])
                evac(Y8[:, g, :], bank[:, 0:256], [bt], [t_Y8, bt])
            ytv = ytm.rearrange("p (a s) c -> p a s c", a=2)
            t_ysm = Tok()
            for g in range(32):
                bank, bt = next_bank()
                pb = psbf(bank)
                for a in range(2):
                    B.op(PE, lambda e, g=g, a=a, pb=pb: e.transpose(
                        out=pb[:, a * 128:(a + 1) * 128], in_=Y8[:, g, a * 128:(a + 1) * 128], identity=identb[:]),
                        reads=[t_Y8, t_const], writes=[bt])
                evac(ytv[:, :, :, g * 16:(g + 1) * 16], pb[:, 0:256].rearrange("p (a s h) -> p a s h", a=2, s=8),
                     [bt], [t_ysm, bt])
            B.barrier()
            if debug and n == 0:
                B.dma(SP, dbg['d_yssm'][:, :], av(144, 16384)); B.dma(SP, dbg['d_U8'][:, :], av(64, 16384))
                B.dma(SP, dbg['d_Xb'][:, :], av(0, 32768)); B.dma(SP, dbg['d_toep'][:, :], av(112, 8192)); B.dma(SP, dbg['d_Y8'][:, :], av(96, 16384))
                B.barrier()
            gsets = []
            for q in range(2):
                o = 64 + 11 * q
                gsets.append((av(o, 2048, F32), av(o + 2, 2048, F32), av(o + 4, 1024),
                              av(o + 5, 1024).rearrange("p (j t) -> p j t", j=4), av(o + 6, 2048, F32),
                              av(o + 8, 2048, F32), av(o + 10, 1024), Tok()))
            C0 = 1.5957691216057308
            for ti in range(16):
                a, s = ti // 8, ti % 8
                uu, ww, ygb, ygT, zz, y2, ynb2, tg = gsets[ti % 2]
                yv = ytm[:, ti, :]
                B.op(ACT, lambda e, yv=yv: e.activation(out=uu, in_=yv, func=AF.Square), reads=[t_ysm, t_toep], writes=[tg])
                B.op(DVE, lambda e: e.tensor_scalar(out=ww, in0=uu, scalar1=0.044715, scalar2=1.0, op0=ALU.mult, op1=ALU.add),
                     reads=[tg], writes=[tg])
                B.op(DVE, lambda e, yv=yv: e.tensor_tensor(out=ww, in0=ww, in1=yv, op=ALU.mult), reads=[tg, t_ysm], writes=[tg])
                B.op(ACT, lambda e: e.activation(out=uu, in_=ww, func=AF.Sigmoid, scale=C0), reads=[tg], writes=[tg])
                B.op(DVE, lambda e, yv=yv: e.tensor_tensor(out=ygb, in0=uu, in1=yv, op=ALU.mult), reads=[tg, t_ysm], writes=[tg])
                bank, bt = next_bank()
                pb = psbf(bank)
                for jj in range(4):
                    B.op(PE, lambda e, jj=jj, pb=pb: e.transpose(out=pb[:, jj * 128:(jj + 1) * 128],
                                                                 in_=ygb[:, jj * 128:(jj + 1) * 128], identity=identb[:]),
                         reads=[tg, t_const], writes=[bt])
                B.op(ACT, lambda e, pb=pb: e.copy(out=ygT, in_=pb[:, 0:512].rearrange("p (j t) -> p j t", j=4)),
                     reads=[bt], writes=[tg, bt])
                zb, zt = next_bank()
                for jj in range(4):
                    B.op(PE, lambda e, jj=jj, zb=zb: e.matmul(zb[:, :], lhsT=ygT[:, jj, :], rhs=wglu[:, jj, :],
                                                              start=(jj == 0), stop=(jj == 3)), reads=[tg, t_ssmw], writes=[zt])
                B.op(DVE, lambda e, zb=zb: e.tensor_tensor(out=zz, in0=zb[:, :], in1=bglu_sb[:], op=ALU.add),
                     reads=[zt, t_const], writes=[tg, zt])
                B.op(ACT, lambda e: e.activation(out=zz, in_=zz, func=AF.Sigmoid), reads=[tg], writes=[tg])
                B.op(DVE, lambda e: e.tensor_tensor(out=y2, in0=zz, in1=ygb, op=ALU.mult), reads=[tg], writes=[tg])
                rms_rstd(ACT, y2, 512, 2, [tg], None)
                B.op(DVE, lambda e: e.tensor_scalar(out=ynb2[:, 0:512], in0=y2, scalar1=rstd[:, 2:3], scalar2=None,
                                                  op0=ALU.mult), reads=[tg, t_stat], writes=[tg])
                bank, bt = next_bank()
                pb = psbf(bank)
                for jj in range(4):
                    B.op(PE, lambda e, jj=jj, pb=pb: e.transpose(out=pb[:, jj * 128:(jj + 1) * 128],
                                                                 in_=ynb2[:, jj * 128:(jj + 1) * 128], identity=identb[:]),
                         reads=[tg, t_const], writes=[bt])
                for jj in range(4):
                    dstv = mixT[:, jj, a * 1024:(a + 1) * 1024].rearrange("p (c s) -> p c s", s=8)[:, :, s]
                    B.op(ACT, lambda e, jj=jj, pb=pb, dstv=dstv: e.activation(
                        out=dstv, in_=pb[:, jj * 128:(jj + 1) * 128], func=AF.Identity, scale=nssm[:, jj:jj + 1]),
                        reads=[bt, t_const], writes=[t_mix[0], bt])

            B.barrier()
            if debug and n == 0:
                B.dma(SP, dbg['d_mixT'][:, :], av(32, 32768))
                B.barrier()
            t_wo = Tok(); t_g = Tok()
            B.dma(SP, w_out, woutb[:, :, :], writes=[t_wo])
            for (dst, w) in ((gmb, 0), (gfb, 1)):
                src = gscr[n, w:w + 1, :]
                srcb = AP(src.tensor, src.offset, [[0, 128], [1, D]])
                B.dma(SP, dst, srcb, writes=[t_g])
            B.dma(SP, nfb, nfin_d[:, :], writes=[t_g])
            xl = [av(128 + 4 * i, 4096, F32) for i in range(2)]
            xltok = [Tok(), Tok()]
            xn5 = [av(136 + 2 * i, 2048) for i in range(2)]
            xn5tok = [Tok(), Tok()]
            sgt = [av(140 + 2 * i, 2048, F32) for i in range(2)]
            sgtok = [Tok(), Tok()]
            ot = [av(144 + 4 * i, 4096, F32) for i in range(2)]
            ottok = [Tok(), Tok()]
            tmpv = av(152, 4096, F32)
            t_tmp = Tok()
            wgtok = [Tok() for _ in range(NWB)]; wdtok = [Tok() for _ in range(NWB)]
            for tb in range(4 if 'P5' not in SKIP else 0):
                t_x1 = [Tok() for _ in range(4)]
                t_h2 = [Tok() for _ in range(4)]
                t_hid = Tok()
                for tt in range(4):
                    T = tb * 4 + tt
                    i = T % 2
                    B.dma(SP, xl[i], x_tiles[n, T * 128:(T + 1) * 128, :], writes=[xltok[i]])
                    for ob in range(2):
                        bank, bt = next_bank()
                        for kt in range(8):
                            B.op(PE, lambda e, kt=kt, T=T, ob=ob, bank=bank: e.matmul(
                                bank[:, :], lhsT=mixT[:, kt, T * 128:(T + 1) * 128],
                                rhs=w_out[:, kt, ob * 512:(ob + 1) * 512], start=(kt == 0), stop=(kt == 7)),
                                reads=[t_wo, t_mix[0]] + t_mix, writes=[bt])
                        B.op(DVE, lambda e, ob=ob, bank=bank: e.tensor_tensor(
                            out=tmpv[:, 0:512], in0=bank[:, :], in1=gmb[:, ob * 512:(ob + 1) * 512], op=ALU.mult),
                            reads=[bt, t_g], writes=[t_tmp, bt])
                        B.op(DVE, lambda e, ob=ob, tt=tt, i=i: e.tensor_tensor(
                            out=x1[:, tt, ob * 512:(ob + 1) * 512], in0=tmpv[:, 0:512],
                            in1=xl[i][:, ob * 512:(ob + 1) * 512], op=ALU.add),
                            reads=[t_tmp, xltok[i]], writes=[t_x1[tt]])
                    rms_rstd(ACT, x1[:, tt, :], 1024, 0, [t_x1[tt]], None)
                    B.op(DVE, lambda e, tt=tt, i=i: e.tensor_scalar(out=xn5[i], in0=x1[:, tt, :], scalar1=rstd[:, 0:1],
                                                                  scalar2=None, op0=ALU.mult),
                         reads=[t_x1[tt], t_stat], writes=[xn5tok[i]])
                    bank, bt = next_bank()
                    pb = psbf(bank)
                    for kt in range(8):
                        B.op(PE, lambda e, i=i, kt=kt, pb=pb: e.transpose(
                            out=pb[:, kt * 128:(kt + 1) * 128], in_=xn5[i][:, kt * 128:(kt + 1) * 128], identity=identb[:]),
                            reads=[xn5tok[i], t_const], writes=[bt])
                    for kt in range(8):
                        if kt % 2 == 0:
                            B.op(DVE, lambda e, kt=kt, pb=pb, tt=tt: e.tensor_scalar(
                                out=h2T[:, kt, tt * 128:(tt + 1) * 128], in0=pb[:, kt * 128:(kt + 1) * 128],
                                scalar1=sclffn[:, kt, n:n + 1], scalar2=modfm[:, 24 + kt, n:n + 1],
                                op0=ALU.mult, op1=ALU.add), reads=[bt, t_mod], writes=[t_h2[tt], bt])
                        else:
                            B.op(ACT, lambda e, kt=kt, pb=pb, tt=tt: e.activation(
                                out=h2T[:, kt, tt * 128:(tt + 1) * 128], in_=pb[:, kt * 128:(kt + 1) * 128],
                                func=AF.Identity, scale=sclffn[:, kt, n:n + 1], bias=modfm[:, 24 + kt, n:n + 1]),
                                reads=[bt, t_mod], writes=[t_h2[tt], bt])
                for fc in range(NFC):
                    i = fc % NWB
                    B.dma(SP, wgs[i], wgb[fc, :, :, :], writes=[wgtok[i]])
                    B.dma(SP, wus[i], wub[fc, :, :, :], writes=[wgtok[i]])
                    gb, gt = next_bank()
                    for kt in range(8):
                        B.op(PE, lambda e, i=i, kt=kt, gb=gb: e.matmul(gb[:, :], lhsT=wgs[i][:, kt, :], rhs=h2T[:, kt, :],
                                                                        start=(kt == 0), stop=(kt == 7)),
                             reads=[wgtok[i]] + t_h2, writes=[gt])
                    ub, ut = next_bank()
                    for kt in range(8):
                        B.op(PE, lambda e, i=i, kt=kt, ub=ub: e.matmul(ub[:, :], lhsT=wus[i][:, kt, :], rhs=h2T[:, kt, :],
                                                                        start=(kt == 0), stop=(kt == 7)),
                             reads=[wgtok[i]] + t_h2, writes=[ut])
                    si = fc % 2
                    B.op(ACT, lambda e, si=si, gb=gb: e.activation(out=sgt[si], in_=gb[:, :], func=AF.Silu),
                         reads=[gt], writes=[sgtok[si], gt])
                    B.op(DVE, lambda e, i=i, ub=ub, fc=fc: e.tensor_tensor(out=hid[:, fc, :], in0=sgt[si], in1=ub[:, :],
                                                                         op=ALU.mult),
                         reads=[sgtok[si], ut], writes=[t_hid, ut])
                for fc in range(NFC):
                    i = fc % NWB
                    B.dma(SP, wds[i], wdb[fc, :, :], writes=[wdtok[i]])
                    for tt in range(4):
                        for ob in range(2):
                            bi = tt * 2 + ob
                            B.op(PE, lambda e, i=i, fc=fc, tt=tt, ob=ob, bi=bi: e.matmul(
                                psum[bi][:, :], lhsT=hid[:, fc, tt * 128:(tt + 1) * 128],
                                rhs=wds[i][:, ob * 512:(ob + 1) * 512], start=(fc == 0), stop=(fc == NFC - 1)),
                                reads=[wdtok[i], t_hid], writes=[ptok[bi]])
                for tt in range(4):
                    T = tb * 4 + tt
                    for ob in range(2):
                        bi = tt * 2 + ob
                        B.op(DVE, lambda e, ob=ob, bi=bi: e.tensor_tensor(
                            out=tmpv[:, 0:512], in0=psum[bi][:, :], in1=gfb[:, ob * 512:(ob + 1) * 512], op=ALU.mult),
                            reads=[ptok[bi], t_g], writes=[t_tmp, ptok[bi]])
                        B.op(DVE, lambda e, ob=ob, tt=tt: e.tensor_tensor(
                            out=x1[:, tt, ob * 512:(ob + 1) * 512], in0=tmpv[:, 0:512],
                            in1=x1[:, tt, ob * 512:(ob + 1) * 512], op=ALU.add),
                            reads=[t_tmp], writes=[t_x1[tt]])
                    rms_rstd(ACT, x1[:, tt, :], 1024, 3, [t_x1[tt]], None)
                    i = T % 2
                    B.op(DVE, lambda e, tt=tt, i=i: e.scalar_tensor_tensor(
                        out=ot[i], in0=x1[:, tt, :], scalar=rstd[:, 3:4], in1=nfb, op0=ALU.mult, op1=ALU.mult),
                        reads=[t_x1[tt], t_stat, t_g], writes=[ottok[i]])
                    B.dma(SP, y_d[n, T * 128:(T + 1) * 128, :], ot[i], reads=[ottok[i]])
        B.finish()
    return nc


_bglu_holder = {}


def _layout_inputs(inp):
    f = np.float32
    out = {}
    xs = np.concatenate([inp["x_prompt"], inp["x_sample"]], axis=0)
    cs = np.concatenate([inp["c_prompt"], inp["c_sample"]], axis=0)
    shared = {}
    shared["w_ada"] = np.ascontiguousarray(inp["w_ada"][0], dtype=f)
    shared["b_ada"] = np.ascontiguousarray(inp["b_ada"][0:1], dtype=f)
    shared["nmix"] = np.ascontiguousarray(inp["norm_mix"][0].reshape(8, 128).T, dtype=f)
    shared["nffn"] = np.ascontiguousarray(inp["norm_ffn"][0].reshape(8, 128).T, dtype=f)
    shared["nfin"] = np.ascontiguousarray(np.broadcast_to(inp["norm_final"][None, :], (128, D)), dtype=f)
    shared["nssm"] = np.ascontiguousarray(inp["norm_ssm_out"][0].reshape(4, 128).T, dtype=f)
    shared["natt"] = np.ascontiguousarray(inp["norm_attn_out"][0].reshape(4, 128).T, dtype=f)
    shared["bglu"] = np.ascontiguousarray(np.broadcast_to(inp["b_glu"][0][None, :], (128, 512)), dtype=f)
    shared["w_in"] = np.ascontiguousarray(inp["w_in"][0], dtype=f)
    shared["w_glu"] = np.ascontiguousarray(inp["w_glu"][0], dtype=f)
    shared["w_out"] = np.ascontiguousarray(inp["w_out"][0], dtype=f)
    shared["w_g"] = np.ascontiguousarray(inp["w_ffn_gate"][0], dtype=f)
    shared["w_u"] = np.ascontiguousarray(inp["w_ffn_up"][0], dtype=f)
    shared["w_d"] = np.ascontiguousarray(inp["w_ffn_down"][0], dtype=f)

    def dpg(a):
        return np.ascontiguousarray(a.transpose(0, 2, 1).reshape(128, 32), dtype=f)

    shared["are"] = dpg(inp["ssm_a_re"][0])
    shared["aim"] = dpg(inp["ssm_a_im"][0])
    shared["ldt"] = dpg(np.broadcast_to(inp["ssm_log_dt"][0][:, :, None], (2, 32, 64)))
    shared["bre"] = np.ascontiguousarray(inp["ssm_b_re"][0].transpose(0, 2, 1, 3).reshape(128, 32, 16), dtype=f)
    shared["bim"] = np.ascontiguousarray(inp["ssm_b_im"][0].transpose(0, 2, 1, 3).reshape(128, 32, 16), dtype=f)
    shared["cre"] = np.ascontiguousarray(inp["ssm_c_re"][0].transpose(0, 3, 1, 2).reshape(128, 32, 16), dtype=f)
    shared["cim"] = np.ascontiguousarray(inp["ssm_c_im"][0].transpose(0, 3, 1, 2).reshape(128, 32, 16), dtype=f)
    dvec = inp["ssm_d"][0].reshape(32, 16)
    dd = np.zeros((16, 32, 16), f)
    for h in range(16):
        dd[h, :, h] = dvec[:, h]
    shared["dd"] = dd
    rpb = inp["na_rpb"][0]
    b = np.arange(2)[:, None, None, None]
    kc = np.arange(64)[None, :, None, None]
    e = np.arange(16)[None, None, :, None]
    qc = np.arange(64)[None, None, None, :]
    ri = np.clip(e + b, 0, 14) + 0 * kc + 0 * qc
    ci = np.clip(kc - qc + 15, 0, 30) + 0 * e + 0 * b
    t2raw = rpb[:, ri, ci].reshape(8, 128, 16 * 64)
    shared["t2raw"] = np.ascontiguousarray(t2raw, dtype=f)
    cstart = np.clip(qc - 8, 0, 48)
    valid = (kc >= cstart) & (kc < cstart + 16) & ((e + b) <= 14)
    m2 = np.where(valid, 0.0, NEG).astype(f) + np.zeros((2, 64, 16, 64), f)
    shared["m2"] = np.ascontiguousarray(m2.reshape(128, 16 * 64))
    shared["identf"] = np.eye(128, dtype=f)
    in_maps = []
    for core in range(8):
        m = dict(shared)
        m["x"] = np.ascontiguousarray(xs[core * 3:(core + 1) * 3], dtype=f)
        cc = cs[core * 3:(core + 1) * 3]
        cT = np.zeros((128, 8, 4), f)
        cT[:, :, 0:3] = cc.reshape(3, 8, 128).transpose(2, 1, 0)
        m["cT"] = cT
        in_maps.append(m)
    return in_maps


def kernel(**inputs):
    inp = {k: np.asarray(v) for k, v in inputs.items()}
    in_maps = _layout_inputs(inp)
    nc = build_nc()
    res = run_bass_kernel_spmd(nc, in_maps, core_ids=list(range(8)))
    ys = np.concatenate([np.asarray(r["y"]).reshape(NSEQ, L, D) for r in res.results], axis=0).astype(np.float32)
    return ys[0:8], ys[8:24]
```

```python
import math
import numpy as np
import ml_dtypes
import concourse.bass as bass
import concourse.mybir as mybir
from concourse.bass_utils import run_bass_kernel_spmd
from concourse.ap import AP

F32 = mybir.dt.float32
BF16 = mybir.dt.bfloat16
ALU = mybir.AluOpType
AF = mybir.ActivationFunctionType

D = 1024
L = 2048
NSEQ = 3
DFF = 2816
NFC = 22
EPS = 1e-6
NEG = -30000.0


class Tok:
    __slots__ = ("w", "r")

    def __init__(self):
        self.w = None
        self.r = {}


class EngW:
    def __init__(self, eng, sem, same_wait):
        self.eng = eng
        self.sem = sem
        self.n = 0
        self.waited = {}
        self.same_wait = same_wait


SAME_WAIT = True


class Builder:
    def __init__(self, nc, sems, dma_sems):
        self.nc = nc
        self.pe = EngW(nc.tensor, sems[0], False)
        self.act = EngW(nc.scalar, sems[1], SAME_WAIT)
        self.dve = EngW(nc.vector, sems[2], SAME_WAIT)
        self.pool = EngW(nc.gpsimd, sems[3], True)
        self.sp = EngW(nc.sync, sems[4], False)
        self.engs = [self.pe, self.act, self.dve, self.pool, self.sp]
        self.dma_sems = dma_sems
        self.dma_n = [0] * len(dma_sems)
        self.dma_rr = 0

    def _wait(self, E, deps):
        for (sem, val) in deps:
            if sem is E.sem and not E.same_wait:
                continue
            k = id(sem)
            if E.waited.get(k, 0) < val:
                E.eng.wait_ge(sem, val)
                E.waited[k] = val

    @staticmethod
    def _deps(reads, writes):
        deps = []
        for t in reads:
            if t.w is not None:
                deps.append(t.w)
        for t in writes:
            if t.w is not None:
                deps.append(t.w)
            for k, v in t.r.items():
                deps.append(v)
        return deps

    @staticmethod
    def _upd(me, reads, writes):
        for t in reads:
            k = id(me[0])
            if k not in t.r or t.r[k][1] < me[1]:
                t.r[k] = me
        for t in writes:
            t.w = me
            t.r = {}

    def op(self, E, fn, reads=(), writes=()):
        self._wait(E, self._deps(reads, writes))
        inst = fn(E.eng)
        E.n += 1
        inst.then_inc(E.sem, 1)
        self._upd((E.sem, E.n), reads, writes)

    def dma(self, Q, out, in_, reads=(), writes=(), **kw):
        k = self.dma_rr
        self.dma_rr = (self.dma_rr + 1) % len(self.dma_sems)
        sem = self.dma_sems[k]
        deps = self._deps(reads, writes)
        if self.dma_n[k] > 0:
            deps.append((sem, 16 * self.dma_n[k]))
        self._wait(Q, deps)
        inst = Q.eng.dma_start(out=out, in_=in_, **kw)
        self.dma_n[k] += 1
        inst.then_inc(sem, 16)
        self._upd((sem, 16 * self.dma_n[k]), reads, writes)

    def barrier(self):
        for E in self.engs:
            deps = [(F.sem, F.n) for F in self.engs if F is not E and F.n > 0]
            deps += [(s, 16 * n) for s, n in zip(self.dma_sems, self.dma_n) if n > 0]
            self._wait(E, deps)

    def finish(self):
        deps = [(s, 16 * n) for s, n in zip(self.dma_sems, self.dma_n) if n > 0]
        deps += [(F.sem, F.n) for F in self.engs if F.n > 0 and F is not self.sp]
        self._wait(self.sp, deps)


def bcast_last(ap, n):
    return AP(ap.tensor, ap.offset, [list(d) for d in ap.ap] + [[0, n]])


SKIP = set()


def build_nc(debug=False):
    nc = bass.Bass("TRN2", target_bir_lowering=False)
    dbg = {}
    def dout(name, shape, dt):
        dbg[name] = nc.dram_tensor(name, list(shape), dt, kind="ExternalOutput").ap()
        return dbg[name]
    if debug:
        dout('d_hT', [128, 8 * 2048], BF16); dout('d_QT', [128, 4 * 2048], BF16); dout('d_KT', [128, 4 * 2048], BF16)
        dout('d_Utm', [128, 8192], BF16); dout('d_yatt', [128, 8192], BF16); dout('d_mixT', [128, 8 * 2048], BF16)
        dout('d_yssm', [128, 8192], BF16); dout('d_Vev', [128, 16 * 8 * 65], BF16); dout('d_mod', [128, 192], F32)
        dout('d_U8', [128, 8192], BF16); dout('d_Xb', [128, 16384], BF16); dout('d_toep', [128, 4096], BF16)
        dout('d_Y8', [128, 8192], BF16); dout('d_T2', [128, 8192], BF16)

    def din(name, shape, dt=F32):
        return nc.dram_tensor(name, list(shape), dt, kind="ExternalInput").ap()

    def dscr(name, shape, dt):
        return nc.dram_tensor(name, list(shape), dt, kind="Internal").ap()

    x_d = din("x", [NSEQ, L, D])
    y_d = nc.dram_tensor("y", [NSEQ, L, D], F32, kind="ExternalOutput").ap()
    cT_d = din("cT", [128, 8, 4])
    wada_d = din("w_ada", [D, 6 * D])
    bada_d = din("b_ada", [1, 6 * D])
    nmix_d = din("nmix", [128, 8])
    nffn_d = din("nffn", [128, 8])
    nfin_d = din("nfin", [128, D])
    nssm_d = din("nssm", [128, 4])
    natt_d = din("natt", [128, 4])
    bglu_d = din("bglu", [128, 512])
    win_d = din("w_in", [D, 2048])
    wglu_d = din("w_glu", [512, 512])
    wout_d = din("w_out", [D, D])
    wg_d = din("w_g", [D, DFF])
    wu_d = din("w_u", [D, DFF])
    wd_d = din("w_d", [DFF, D])
    are_d = din("are", [128, 32])
    aim_d = din("aim", [128, 32])
    ldt_d = din("ldt", [128, 32])
    bre_d = din("bre", [128, 32, 16])
    bim_d = din("bim", [128, 32, 16])
    cre_d = din("cre", [128, 32, 16])
    cim_d = din("cim", [128, 32, 16])
    dd_d = din("dd", [16, 32, 16])
    t2raw_d = din("t2raw", [8, 128, 16 * 64])
    m2_d = din("m2", [128, 16 * 64])
    identf_d = din("identf", [128, 128])

    winb = dscr("winb", [128, 8, 2048], BF16)
    wglub = dscr("wglub", [128, 4, 512], BF16)
    woutb = dscr("woutb", [128, 8, 1024], BF16)
    wgb = dscr("wgb", [NFC, 128, 8, 128], BF16)
    wub = dscr("wub", [NFC, 128, 8, 128], BF16)
    wdb = dscr("wdb", [NFC, 128, 1024], BF16)
    toepb = dscr("toepb", [128, 32, 128], BF16)
    kallb = dscr("kallb", [16, 32, 240], BF16)
    gscr = dscr("gscr", [NSEQ, 2, D], F32)
    winTb = dscr("winTb", [128, 32 * 2 * 128], BF16)
    woutSb = dscr("woutSb", [128, 32 * 2 * 128], BF16)

    from contextlib import ExitStack

    with ExitStack() as es:
        def sb(name, shape, dt):
            return es.enter_context(nc.sbuf_tensor("sb_" + name, list(shape), dt))

        sems = [es.enter_context(nc.semaphore("e%d" % i)) for i in range(5)]
        dsems = [es.enter_context(nc.semaphore("d%d" % i)) for i in range(16)]
        B = Builder(nc, sems, dsems)
        PE, ACT, DVE, POOL, SP = B.pe, B.act, B.dve, B.pool, B.sp

        psum = [es.enter_context(nc.psum_tensor("ps%d" % i, [128, 512], F32)) for i in range(8)]
        ptok = [Tok() for _ in range(8)]
        pstate = {"i": 0, "n": 8}

        def next_bank():
            i = pstate["i"] % pstate["n"]
            pstate["i"] = (i + 1) % pstate["n"]
            return psum[i], ptok[i]

        def psbf(bank):
            return bank[:].bitcast(BF16)

        identf = sb("identf", [128, 128], F32)
        identb = sb("identb", [128, 128], BF16)
        ones1 = sb("ones1", [1, 8], F32)
        epst = sb("epst", [128, 1], F32)
        modfm = sb("modfm", [128, 48, 4], F32)
        sclmix = sb("sclmix", [128, 8, 4], F32)
        sclffn = sb("sclffn", [128, 8, 4], F32)
        nssm = sb("nssm", [128, 4], F32)
        natt = sb("natt", [128, 4], F32)
        A1 = sb("A1", [128, 64], F32)
        A2 = sb("A2", [128, 64], F32)
        T2 = sb("T2", [128, 8, 16 * 64], BF16)
        bglu_sb = sb("bglu_sb", [128, 512], F32)
        wglu = sb("wglu", [128, 4, 512], BF16)
        xstate = sb("xstate", [128, 64], F32)
        st1 = sb("st1", [128, 64], F32)
        st2 = sb("st2", [128, 64], F32)
        ssq = sb("ssq", [128, 8], F32)
        rstd = sb("rstd", [128, 8], F32)
        rec = sb("rec", [128, 8], F32)
        junk = sb("junk", [128, 1024], BF16)
        t_const = Tok()
        t_mod = Tok()
        t_ssmw = Tok()
        t_stat = Tok()
        t_junk = Tok()

        NA = 160 * 512
        arena = sb("arena", [128, NA], BF16)

        def av(off_kb, nbytes, dt=BF16):
            o = int(off_kb * 512)
            ap = arena[:, o:o + nbytes // 2]
            if dt is F32:
                ap = ap.bitcast(F32)
            return ap

        B.dma(SP, identf[:], identf_d[:, :], writes=[t_const])
        B.op(DVE, lambda e: e.tensor_copy(out=identb[:], in_=identf[:]), reads=[t_const], writes=[t_const])
        B.op(DVE, lambda e: e.memset(ones1[:], 1.0), writes=[t_const])
        B.op(DVE, lambda e: e.memset(epst[:], EPS), writes=[t_const])
        B.dma(SP, nssm[:], nssm_d[:, :], writes=[t_const])
        B.dma(SP, natt[:], natt_d[:, :], writes=[t_const])
        B.dma(SP, bglu_sb[:], bglu_d[:, :], writes=[t_const])
        winT = av(128, 16384).rearrange("p (g r c) -> p g r c", g=32, r=2)
        woutS = av(144, 16384).rearrange("p (g r c) -> p g r c", g=32, r=2)

        def f32tile(off_kb, shape):
            n = int(np.prod(shape))
            ap = av(off_kb, n * 4, F32)
            if len(shape) == 2:
                return ap.rearrange("p (a b) -> p a b", a=shape[0]) if False else ap
            return ap

        ts = Tok()
        off = [0.0]

        def alloc(ncols):
            o = off[0]
            off[0] += ncols * 4 / 1024.0
            return av(o, ncols * 4, F32)

        are = alloc(32); aim = alloc(32); ldt = alloc(32)
        lr = alloc(32); dtt = alloc(32); er = alloc(32); ph2 = alloc(64); k2 = alloc(64); sc2 = alloc(64)
        ar = alloc(32); ai = alloc(32); nr = alloc(32); den = alloc(32); zr = alloc(32); zi = alloc(32)
        tA = alloc(32); tB = alloc(32); a8r = alloc(32); a8i = alloc(32)
        B.dma(SP, are, are_d[:, :], writes=[ts])
        B.dma(SP, aim, aim_d[:, :], writes=[ts])
        B.dma(SP, ldt, ldt_d[:, :], writes=[ts])

        def V(fn):
            B.op(DVE, fn, reads=[ts], writes=[ts])

        def A(fn):
            B.op(ACT, fn, reads=[ts], writes=[ts])

        V(lambda e: e.tensor_scalar(out=lr, in0=are, scalar1=-1e-4, scalar2=None, op0=ALU.min))
        A(lambda e: e.activation(out=dtt, in_=ldt, func=AF.Exp))
        V(lambda e: e.tensor_tensor(out=er, in0=lr, in1=dtt, op=ALU.mult))
        A(lambda e: e.activation(out=er, in_=er, func=AF.Exp))
        V(lambda e: e.tensor_tensor(out=ph2[:, 0:32], in0=aim, in1=dtt, op=ALU.mult))
        V(lambda e: e.tensor_scalar(out=ph2[:, 32:64], in0=ph2[:, 0:32], scalar1=math.pi / 2, scalar2=None, op0=ALU.add))
        MAGIC = 12582912.0
        V(lambda e: e.tensor_scalar(out=k2, in0=ph2, scalar1=1.0 / (2 * math.pi), scalar2=MAGIC, op0=ALU.mult, op1=ALU.add))
        V(lambda e: e.tensor_scalar(out=k2, in0=k2, scalar1=-MAGIC, scalar2=None, op0=ALU.add))
        V(lambda e: e.scalar_tensor_tensor(out=ph2, in0=k2, scalar=-2 * math.pi, in1=ph2, op0=ALU.mult, op1=ALU.add))
        V(lambda e: e.tensor_scalar(out=ph2, in0=ph2, scalar1=3.14159, scalar2=-3.14159, op0=ALU.min, op1=ALU.max))
        A(lambda e: e.activation(out=sc2, in_=ph2, func=AF.Sin))
        V(lambda e: e.tensor_tensor(out=ai, in0=er, in1=sc2[:, 0:32], op=ALU.mult))
        V(lambda e: e.tensor_tensor(out=ar, in0=er, in1=sc2[:, 32:64], op=ALU.mult))
        V(lambda e: e.tensor_scalar(out=nr, in0=ar, scalar1=-1.0, scalar2=None, op0=ALU.add))
        V(lambda e: e.tensor_tensor(out=den, in0=lr, in1=lr, op=ALU.mult))
        V(lambda e: e.tensor_tensor(out=tA, in0=aim, in1=aim, op=ALU.mult))
        V(lambda e: e.tensor_tensor(out=den, in0=den, in1=tA, op=ALU.add))
        V(lambda e: e.reciprocal(out=den, in_=den))
        V(lambda e: e.tensor_tensor(out=tA, in0=nr, in1=lr, op=ALU.mult))
        V(lambda e: e.tensor_tensor(out=tB, in0=ai, in1=aim, op=ALU.mult))
        V(lambda e: e.tensor_tensor(out=tA, in0=tA, in1=tB, op=ALU.add))
        V(lambda e: e.tensor_tensor(out=zr, in0=tA, in1=den, op=ALU.mult))
        V(lambda e: e.tensor_tensor(out=tA, in0=ai, in1=lr, op=ALU.mult))
        V(lambda e: e.tensor_tensor(out=tB, in0=nr, in1=aim, op=ALU.mult))
        V(lambda e: e.tensor_tensor(out=tA, in0=tA, in1=tB, op=ALU.subtract))
        V(lambda e: e.tensor_tensor(out=zi, in0=tA, in1=den, op=ALU.mult))
        V(lambda e: e.tensor_copy(out=a8r, in_=ar))
        V(lambda e: e.tensor_copy(out=a8i, in_=ai))
        for _ in range(3):
            V(lambda e: e.tensor_tensor(out=tA, in0=a8r, in1=a8r, op=ALU.mult))
            V(lambda e: e.tensor_tensor(out=tB, in0=a8i, in1=a8i, op=ALU.mult))
            V(lambda e: e.tensor_tensor(out=a8i, in0=a8r, in1=a8i, op=ALU.mult))
            V(lambda e: e.tensor_tensor(out=a8r, in0=tA, in1=tB, op=ALU.subtract))
            V(lambda e: e.tensor_scalar(out=a8i, in0=a8i, scalar1=2.0, scalar2=None, op0=ALU.mult))
        B.op(DVE, lambda e: e.tensor_copy(out=A1[:, 0:32], in_=a8r), reads=[ts], writes=[t_ssmw])
        B.op(DVE, lambda e: e.tensor_copy(out=A1[:, 32:64], in_=a8r), reads=[ts], writes=[t_ssmw])
        B.op(DVE, lambda e: e.tensor_scalar(out=A2[:, 0:32], in0=a8i, scalar1=-1.0, scalar2=None, op0=ALU.mult), reads=[ts], writes=[t_ssmw])
        B.op(DVE, lambda e: e.tensor_copy(out=A2[:, 32:64], in_=a8i), reads=[ts], writes=[t_ssmw])

        def alloc3(nk):
            return alloc(nk * 512).rearrange("p (k g h) -> p k g h", k=nk, g=32)

        TR = alloc3(8); TI_off = off[0]; TI = alloc3(8); GR = alloc3(9); GI = alloc3(9)
        braw = alloc(512).rearrange("p (g h) -> p g h", g=32)
        biraw = alloc(512).rearrange("p (g h) -> p g h", g=32)
        tmp1 = alloc(512).rearrange("p (g h) -> p g h", g=32)
        tmp2 = alloc(512).rearrange("p (g h) -> p g h", g=32)
        B.dma(SP, braw, bre_d[:, :, :], writes=[ts])
        B.dma(SP, biraw, bim_d[:, :, :], writes=[ts])
        B.dma(SP, GR[:, 0], cre_d[:, :, :], writes=[ts])
        B.dma(SP, GI[:, 0], cim_d[:, :, :], writes=[ts])
        zrb = bcast_last(zr, 16); zib = bcast_last(zi, 16)
        arb = bcast_last(ar, 16); aib = bcast_last(ai, 16)

        def cmul(outR, outI, inR, inI, sR, sI):
            V(lambda e: e.tensor_tensor(out=tmp1, in0=inR, in1=sR, op=ALU.mult))
            V(lambda e: e.tensor_tensor(out=tmp2, in0=inI, in1=sI, op=ALU.mult))
            V(lambda e: e.tensor_tensor(out=outR, in0=tmp1, in1=tmp2, op=ALU.subtract))
            V(lambda e: e.tensor_tensor(out=tmp1, in0=inR, in1=sI, op=ALU.mult))
            V(lambda e: e.tensor_tensor(out=tmp2, in0=inI, in1=sR, op=ALU.mult))
            V(lambda e: e.tensor_tensor(out=outI, in0=tmp1, in1=tmp2, op=ALU.add))

        cmul(TR[:, 0], TI[:, 0], braw, biraw, zrb, zib)
        for k in range(1, 8):
            cmul(TR[:, k], TI[:, k], TR[:, k - 1], TI[:, k - 1], arb, aib)
        for k in range(1, 9):
            cmul(GR[:, k], GI[:, k], GR[:, k - 1], GI[:, k - 1], arb, aib)

        for t in range(8):
            for (lo, hi, k) in ((0, 64, t + 1), (64, 128, 8 - t)):
                B.op(DVE, lambda e, lo=lo, hi=hi, k=k, t=t: e.tensor_copy(
                    out=woutS[lo:hi, :, 0, t * 16:(t + 1) * 16], in_=GR[lo:hi, k]), reads=[ts], writes=[t_ssmw])
                B.op(DVE, lambda e, lo=lo, hi=hi, k=k, t=t: e.tensor_scalar(
                    out=woutS[lo:hi, :, 1, t * 16:(t + 1) * 16], in0=GI[lo:hi, k], scalar1=-1.0, scalar2=None,
                    op0=ALU.mult), reads=[ts], writes=[t_ssmw])
        ddt = alloc(512).rearrange("p (g h) -> p g h", g=32)
        winp_off = off[0]
        winp = alloc(32 * 2 * 128).rearrange("p (g r c) -> p g r c", g=32, r=2)
        for s in range(8):
            for (lo, hi, k) in ((0, 64, 7 - s), (64, 128, s)):
                V(lambda e, lo=lo, hi=hi, k=k, s=s: e.tensor_copy(out=winp[lo:hi, :, 0, s * 16:(s + 1) * 16], in_=TR[lo:hi, k]))
                V(lambda e, lo=lo, hi=hi, k=k, s=s: e.tensor_copy(out=winp[lo:hi, :, 1, s * 16:(s + 1) * 16], in_=TI[lo:hi, k]))
        for g in range(32):
            bank, bt = next_bank()
            for ri in range(2):
                B.op(PE, lambda e, g=g, ri=ri, bank=bank: e.transpose(
                    out=bank[:, ri * 128:(ri + 1) * 128], in_=winp[:, g, ri, :], identity=identf[:]),
                    reads=[ts, t_const], writes=[bt])
            B.op(ACT if g % 2 else DVE, (lambda e, g=g, bank=bank: e.tensor_copy(
                out=winT[:, g, :, :], in_=bank[:, 0:256].rearrange("p (r c) -> p r c", r=2))) if g % 2 == 0 else
                (lambda e, g=g, bank=bank: e.copy(
                    out=winT[:, g, :, :], in_=bank[:, 0:256].rearrange("p (r c) -> p r c", r=2))),
                reads=[bt], writes=[t_ssmw, bt])

        V(lambda e: e.tensor_scalar(out=tmp1, in0=TI[:, 0], scalar1=-1.0, scalar2=None, op0=ALU.mult))
        B.dma(SP, ddt[0:16], dd_d[:, :, :], writes=[ts])
        gbase = winp_off
        GallR = av(gbase, 16 * 240 * 4, F32).rearrange("p (g t h) -> p g t h", g=16, t=15)
        GallI = av(gbase + 15, 16 * 240 * 4, F32).rearrange("p (g t h) -> p g t h", g=16, t=15)
        kall = av(gbase + 30, 16 * 240 * 4, F32).rearrange("p (g c) -> p g c", g=16)
        kallbf = av(TI_off, 16 * 240 * 2).rearrange("p (g c) -> p g c", g=16)
        for gh in range(2):
            gs = slice(gh * 16, (gh + 1) * 16)
            V(lambda e: e.memset(GallR, 0.0))
            V(lambda e: e.memset(GallI, 0.0))
            for k in range(8):
                V(lambda e, k=k: e.tensor_copy(out=GallR[0:64, :, 7 + k, :], in_=GR[0:64, k, gs, :]))
                V(lambda e, k=k: e.tensor_copy(out=GallI[0:64, :, 7 + k, :], in_=GI[0:64, k, gs, :]))
                V(lambda e, k=k: e.tensor_copy(out=GallR[64:128, :, 7 - k, :], in_=GR[64:128, k, gs, :]))
                V(lambda e, k=k: e.tensor_copy(out=GallI[64:128, :, 7 - k, :], in_=GI[64:128, k, gs, :]))
            for g2 in range(8):
                bank, bt = next_bank()
                for gi in range(2):
                    gl = g2 * 2 + gi
                    g = gh * 16 + gl
                    B.op(PE, lambda e, g=g, gl=gl, gi=gi, bank=bank: e.matmul(
                        bank[0:16, gi * 240:(gi + 1) * 240], lhsT=TR[:, 0, g, :],
                        rhs=GallR[:, gl].rearrange("p t h -> p (t h)"), start=True, stop=False),
                        reads=[ts], writes=[bt])
                    B.op(PE, lambda e, g=g, gl=gl, gi=gi, bank=bank: e.matmul(
                        bank[0:16, gi * 240:(gi + 1) * 240], lhsT=tmp1[:, g, :],
                        rhs=GallI[:, gl].rearrange("p t h -> p (t h)"), start=False, stop=True),
                        reads=[ts], writes=[bt])
                B.op(DVE, lambda e, g2=g2, bank=bank: e.tensor_copy(
                    out=kall[0:16, g2 * 2:g2 * 2 + 2, :], in_=bank[0:16, 0:480].rearrange("p (g c) -> p g c", g=2)),
                    reads=[bt], writes=[ts, bt])
            V(lambda e: e.tensor_tensor(out=kall[0:16, :, 112:128], in0=kall[0:16, :, 112:128], in1=ddt[0:16, gs, :], op=ALU.add))
            V(lambda e: e.tensor_copy(out=kallbf[0:16], in_=kall[0:16]))
            B.dma(SP, kallb[:, gs, :], kallbf[0:16], reads=[ts], writes=[ts])
        for s in range(8):
            B.dma(SP, toepb[s * 16:(s + 1) * 16, :, :], kallb[:, :, (7 - s) * 16:(7 - s) * 16 + 128],
                  reads=[ts], writes=[ts])
        B.dma(SP, winTb[:, :], winT.rearrange("p g r c -> p (g r c)"), reads=[t_ssmw], writes=[ts])
        B.dma(SP, woutSb[:, :], woutS.rearrange("p g r c -> p (g r c)"), reads=[t_ssmw], writes=[ts])

        B.barrier()
        cT = av(0, 128, F32).rearrange("p (k n) -> p k n", k=8)
        sil = av(1, 128, F32).rearrange("p (k n) -> p k n", k=8)
        badar = av(2, 6 * D * 4, F32)
        wblk = [av(32 + 16 * i, 8 * 512 * 4, F32).rearrange("p (k c) -> p k c", k=8) for i in range(2)]
        wtok = [Tok(), Tok()]
        tm = Tok()
        B.dma(SP, cT, cT_d[:, :, :], writes=[tm])
        B.dma(SP, badar[0:1], bada_d[:, :], writes=[tm])
        B.op(ACT, lambda e: e.activation(out=sil, in_=cT, func=AF.Silu), reads=[tm], writes=[tm])
        wada_v = wada_d.rearrange("(k p) c -> p k c", p=128)
        mbank, mtok = next_bank()
        for blk in range(12):
            wb = wblk[blk % 2]
            B.dma(SP, wb, wada_v[:, :, blk * 512:(blk + 1) * 512], writes=[wtok[blk % 2]])
            for j in range(4):
                ct = blk * 4 + j
                o = (ct % 48) * 4
                B.op(PE, lambda e, ct=ct, o=o: e.matmul(
                    mbank[:, o:o + 4], lhsT=badar[0:1, ct * 128:(ct + 1) * 128], rhs=ones1[0:1, 0:4],
                    start=True, stop=False), reads=[tm, t_const], writes=[mtok])
                for kt in range(8):
                    B.op(PE, lambda e, wb=wb, j=j, kt=kt, o=o: e.matmul(
                        mbank[:, o:o + 4], lhsT=wb[:, kt, j * 128:(j + 1) * 128], rhs=sil[:, kt, :],
                        start=False, stop=(kt == 7)), reads=[tm, wtok[blk % 2]], writes=[mtok])
        B.op(DVE, lambda e: e.tensor_copy(out=modfm[:], in_=mbank[:, 0:192].rearrange("p (c n) -> p c n", n=4)),
             reads=[mtok], writes=[t_mod, mtok])
        nm = av(3, 32, F32); nf = av(3.5, 32, F32)
        B.dma(SP, nm, nmix_d[:, :], writes=[tm])
        B.dma(SP, nf, nffn_d[:, :], writes=[tm])
        for (dst, nrm, base) in ((sclmix, nm, 8), (sclffn, nf, 32)):
            B.op(DVE, lambda e, dst=dst, base=base: e.tensor_scalar(
                out=dst[:], in0=modfm[:, base:base + 8, :], scalar1=1.0, scalar2=None, op0=ALU.add),
                reads=[t_mod, tm], writes=[t_mod])
            B.op(DVE, lambda e, dst=dst, nrm=nrm: e.tensor_tensor(
                out=dst[:], in0=dst[:], in1=bcast_last(nrm, 4), op=ALU.mult), reads=[t_mod, tm], writes=[t_mod])
        for w, base in ((0, 16), (1, 40)):
            for n in range(NSEQ):
                B.dma(SP, gscr[n, w, :].rearrange("(k p) -> p k", p=128), modfm[:, base:base + 8, n],
                      reads=[t_mod], writes=[tm], allow_slow_non_contiguous=True)

        B.barrier()
        stg = [av(0 + 8 * i, 2048 * 4, F32) for i in range(3)]
        stgb = [av(32 + 4 * i, 2048 * 2) for i in range(3)]
        stok = [Tok() for _ in range(3)]
        sbtok = [Tok() for _ in range(3)]
        cnt = [0]

        def conv(src, dst, ncol):
            i = cnt[0] % 3
            cnt[0] += 1
            B.dma(SP, stg[i][:, 0:ncol], src, writes=[stok[i]])
            E = (ACT, DVE, POOL)[i]
            if E is ACT:
                B.op(E, lambda e: e.copy(out=stgb[i][:, 0:ncol], in_=stg[i][:, 0:ncol]), reads=[stok[i]], writes=[sbtok[i]])
            else:
                B.op(E, lambda e: e.tensor_copy(out=stgb[i][:, 0:ncol], in_=stg[i][:, 0:ncol]), reads=[stok[i]], writes=[sbtok[i]])
            return i

        for kt in range(8):
            i = conv(win_d[kt * 128:(kt + 1) * 128, :], None, 2048)
            B.dma(SP, winb[:, kt, :], stgb[i][:, 0:2048], reads=[sbtok[i]])
        for kt in range(4):
            i = conv(wglu_d[kt * 128:(kt + 1) * 128, :], None, 512)
            B.dma(SP, wglub[:, kt, :], stgb[i][:, 0:512], reads=[sbtok[i]])
        for kt in range(8):
            i = conv(wout_d[kt * 128:(kt + 1) * 128, :], None, 1024)
            B.dma(SP, woutb[:, kt, :], stgb[i][:, 0:1024], reads=[sbtok[i]])
        for (wsrc, wdst) in ((wg_d, wgb), (wu_d, wub)):
            for kt in range(8):
                for hf in range(2):
                    i = conv(wsrc[kt * 128:(kt + 1) * 128, hf * 1408:(hf + 1) * 1408], None, 1408)
                    B.dma(SP, wdst[hf * 11:(hf + 1) * 11, :, kt, :].rearrange("f p j -> p f j"),
                          stgb[i][:, 0:1408].rearrange("p (f j) -> p f j", j=128), reads=[sbtok[i]])
        for fc in range(NFC):
            i = conv(wd_d[fc * 128:(fc + 1) * 128, :], None, 1024)
            B.dma(SP, wdb[fc, :, :], stgb[i][:, 0:1024], reads=[sbtok[i]])

        B.barrier()
        m2t = av(64, 1024 * 4, F32)
        t2r = [av(72 + 4 * i, 1024 * 4, F32) for i in range(2)]
        t2tok = [Tok(), Tok()]
        tm2 = Tok()
        t_T2 = Tok()
        B.dma(SP, m2t, m2_d[:, :], writes=[tm2])
        for h in range(8):
            B.dma(SP, t2r[h % 2], t2raw_d[h, :, :], writes=[t2tok[h % 2]])
            B.op(DVE, lambda e, h=h: e.tensor_tensor(out=T2[:, h, :], in0=t2r[h % 2], in1=m2t, op=ALU.add),
                 reads=[t2tok[h % 2], tm2], writes=[t_T2])
        B.barrier()
        B.dma(SP, wglu[:], wglub[:, :, :], writes=[t_ssmw])

        hT = av(0, 32768).rearrange("p (k t) -> p k t", k=8)
        Xb = av(0, 32768).rearrange("p (c r g) -> p c r g", c=256, r=2)
        w_in = av(32, 32768).rearrange("p (k c) -> p k c", k=8)
        mixT = av(32, 32768).rearrange("p (k t) -> p k t", k=8)
        QT = av(64, 16384).rearrange("p (h t) -> p h t", h=4)
        KT = av(80, 16384).rearrange("p (h t) -> p h t", h=4)
        Vev = av(96, 16 * 8 * 65 * 2).rearrange("p (i h d) -> p i h d", i=16, h=8)
        Vod = av(112.5, 15 * 8 * 65 * 2).rearrange("p (i h d) -> p i h d", i=15, h=8)
        Utm = av(128, 16384).rearrange("p (a g s h) -> p a g s h", a=2, g=32, s=8)
        ytm = av(144, 16384).rearrange("p (i c) -> p i c", i=16)
        U8 = av(64, 16384).rearrange("p (g c) -> p g c", g=32)
        Sbufs = [av(80, 16384, F32).rearrange("p (c r g) -> p c r g", c=64, r=2),
                 av(96, 16384, F32).rearrange("p (c r g) -> p c r g", c=64, r=2)]
        Y8 = av(96, 16384).rearrange("p (g c) -> p g c", g=32)
        toep = av(112, 8192).rearrange("p (g c) -> p g c", g=32)
        w_out = av(0, 16384).rearrange("p (k c) -> p k c", k=8)
        h2T = av(16, 8192).rearrange("p (k t) -> p k t", k=8)
        hid = av(64, NFC * 512 * 2).rearrange("p (f t) -> p f t", f=NFC)
        x1 = av(86, 4 * 1024 * 4, F32).rearrange("p (t c) -> p t c", t=4)
        NWB = 4
        wgs = [av(102 + 4 * i, 2048).rearrange("p (k j) -> p k j", k=8) for i in range(NWB)]
        wus = [av(102 + 4 * i + 2, 2048).rearrange("p (k j) -> p k j", k=8) for i in range(NWB)]
        wds = [av(118 + 2 * i, 2048) for i in range(NWB)]
        gmb = av(24, 4096, F32)
        gfb = av(28, 4096, F32)
        nfb = av(156, 4096, F32)

        x_tiles = x_d

        t_statc = [Tok() for _ in range(8)]

        def rms_rstd(E, src_ap, ncol, col, reads, tokw):
            B.op(ACT, lambda e: e.activation(out=junk[:, 0:ncol], in_=src_ap, func=AF.Square,
                                             accum_out=ssq[:, col:col + 1]),
                 reads=list(reads) + [], writes=[t_statc[col], t_junk])
            B.op(ACT, lambda e: e.activation(out=rstd[:, col:col + 1], in_=ssq[:, col:col + 1], func=AF.Sqrt,
                                             bias=epst[:], scale=1.0 / ncol), reads=[t_statc[col], t_const], writes=[t_statc[col]])
            B.op(DVE, lambda e: e.reciprocal(out=rstd[:, col:col + 1], in_=rstd[:, col:col + 1]),
                 reads=[t_statc[col]], writes=[t_statc[col]])
            return t_statc[col]

        for n in range(NSEQ if 'MAIN' not in SKIP else 0):
            B.barrier()
            xtile = [av(144 + 4 * i, 4096, F32) for i in range(2)]
            xnb = [av(152 + 2 * i, 2048) for i in range(2)]
            xtok = [Tok(), Tok()]
            xntok = [Tok(), Tok()]
            t_hT = [Tok() for _ in range(16)]
            t_win = Tok()
            B.dma(SP, w_in, winb[:, :, :], writes=[t_win])
            for tt in range(16):
                i = tt % 2
                B.dma(SP, xtile[i], x_tiles[n, tt * 128:(tt + 1) * 128, :], writes=[xtok[i]])
                tsc = rms_rstd(ACT, xtile[i], 1024, i, [xtok[i]], None)
                B.op(DVE, lambda e, i=i: e.tensor_scalar(out=xnb[i], in0=xtile[i], scalar1=rstd[:, i:i + 1], scalar2=None,
                                                       op0=ALU.mult), reads=[xtok[i], tsc], writes=[xntok[i]])
                bank, bt = next_bank()
                pb = psbf(bank)
                for kt in range(8):
                    B.op(PE, lambda e, i=i, kt=kt, pb=pb: e.transpose(
                        out=pb[:, kt * 128:(kt + 1) * 128], in_=xnb[i][:, kt * 128:(kt + 1) * 128], identity=identb[:]),
                        reads=[xntok[i], t_const], writes=[bt])
                for kt in range(8):
                    if kt % 2 == 0:
                        B.op(DVE, lambda e, kt=kt, pb=pb, tt=tt: e.tensor_scalar(
                            out=hT[:, kt, tt * 128:(tt + 1) * 128], in0=pb[:, kt * 128:(kt + 1) * 128],
                            scalar1=sclmix[:, kt, n:n + 1], scalar2=modfm[:, kt, n:n + 1], op0=ALU.mult, op1=ALU.add),
                            reads=[bt, t_mod], writes=[t_hT[tt], bt])
                    else:
                        B.op(ACT, lambda e, kt=kt, pb=pb, tt=tt: e.activation(
                            out=hT[:, kt, tt * 128:(tt + 1) * 128], in_=pb[:, kt * 128:(kt + 1) * 128],
                            func=AF.Identity, scale=sclmix[:, kt, n:n + 1], bias=modfm[:, kt, n:n + 1]),
                            reads=[bt, t_mod], writes=[t_hT[tt], bt])
            t_Q = Tok(); t_K = Tok(); t_V = Tok(); t_U = Tok()
            flip = [0]

            def evac(out_ap, in_ap, reads, writes, scale=None):
                flip[0] ^= 1
                if flip[0]:
                    if scale is None:
                        B.op(DVE, lambda e: e.tensor_copy(out=out_ap, in_=in_ap), reads=reads, writes=writes)
                    else:
                        B.op(DVE, lambda e: e.tensor_scalar(out=out_ap, in0=in_ap, scalar1=scale, scalar2=None,
                                                          op0=ALU.mult), reads=reads, writes=writes)
                else:
                    if scale is None:
                        B.op(ACT, lambda e: e.copy(out=out_ap, in_=in_ap), reads=reads, writes=writes)
                    else:
                        B.op(ACT, lambda e: e.mul(out=out_ap, in_=in_ap, mul=scale), reads=reads, writes=writes)

            for (dst, cbase, tk, scl) in ((QT, 512, t_Q, 0.125), (KT, 1024, t_K, None)):
                for hp in range(4):
                    for tb in range(4):
                        bank, bt = next_bank()
                        for kt in range(8):
                            B.op(PE, lambda e, kt=kt, hp=hp, tb=tb, bank=bank, cbase=cbase: e.matmul(
                                bank[:, :], lhsT=w_in[:, kt, cbase + hp * 128:cbase + (hp + 1) * 128],
                                rhs=hT[:, kt, tb * 512:(tb + 1) * 512], start=(kt == 0), stop=(kt == 7)),
                                reads=[t_win] + t_hT[tb * 4:tb * 4 + 4], writes=[bt])
                        evac(dst[:, hp, tb * 512:(tb + 1) * 512], bank[:, :], [bt], [tk, bt], scale=scl)
            B.op(POOL, lambda e: e.memset(Vev[:, :, :, 64:65], 1.0), writes=[t_V])
            B.op(POOL, lambda e: e.memset(Vod[:, :, :, 64:65], 1.0), writes=[t_V])
            for (dst, ntile, tok0) in ((Vev, 16, 0), (Vod, 15, 64)):
                for i in range(ntile):
                    t0 = tok0 + i * 128
                    bank, bt = next_bank()
                    for kt in range(8):
                        B.op(PE, lambda e, kt=kt, t0=t0, bank=bank: e.matmul(
                            bank[:, :], lhsT=hT[:, kt, t0:t0 + 128], rhs=w_in[:, kt, 1536:2048],
                            start=(kt == 0), stop=(kt == 7)),
                            reads=[t_win] + t_hT[t0 // 128:(t0 + 127) // 128 + 1], writes=[bt])
                    evac(dst[:, i, :, 0:64], bank[:, :].rearrange("p (h d) -> p h d", h=8), [bt], [t_V, bt])
            for a in range(2):
                for s in range(8):
                    bank, bt = next_bank()
                    for kt in range(8):
                        hsl = hT[:, kt, a * 1024:(a + 1) * 1024].rearrange("p (c s) -> p c s", s=8)[:, :, s]
                        B.op(PE, lambda e, kt=kt, hsl=hsl, bank=bank: e.matmul(
                            bank[:, :], lhsT=hsl, rhs=w_in[:, kt, 0:512], start=(kt == 0), stop=(kt == 7)),
                            reads=[t_win] + t_hT[a * 8:a * 8 + 8], writes=[bt])
                    evac(Utm[:, a, :, s, :], bank[:, :].rearrange("p (g h) -> p g h", g=32), [bt], [t_U, bt])

            B.barrier()
            if debug and n == 0:
                B.dma(SP, dbg['d_hT'][:, :], av(0, 32768)); B.dma(SP, dbg['d_QT'][:, :], av(64, 16384)); B.dma(SP, dbg['d_KT'][:, :], av(80, 16384))
                B.dma(SP, dbg['d_Utm'][:, :], av(128, 16384)); B.dma(SP, dbg['d_Vev'][:, :], av(96, 16 * 8 * 65 * 2))
                B.dma(SP, dbg['d_mod'][:, :], modfm[:].rearrange("p c n -> p (c n)")); B.dma(SP, dbg['d_T2'][:, :], T2[:].rearrange("p h c -> p (h c)"))
                B.barrier()
            P3ON = 'P3' not in SKIP
            sct = [av(0 + 1 * i, 1024, F32).rearrange("p (t q) -> p t q", t=4) for i in range(4)]
            ptt = [av(4 + 0.5 * i, 512).rearrange("p (t q) -> p t q", t=4) for i in range(4)]
            sctok = [Tok() for _ in range(4)]
            pttok = [Tok() for _ in range(4)]
            ynb = av(8, 1024)
            t_yn = Tok()
            t_ytm = [Tok() for _ in range(16)]
            t_mix = [Tok() for _ in range(16)]
            pstate["n"] = 4
            pstate["i"] = 0
            NJ = 4
            LA = 2
            iters = []
            for i in range(16 if P3ON else 0):
                for hg in range(2):
                    for hl in range(4):
                        for a in range(2):
                            iters.append((i, hg, hl, a))
            sc_state = {}

            def stage_scores(idx):
                i, hg, hl, a = iters[idx]
                h = hg * 4 + hl
                hp, hb = h // 2, (h % 2) * 64
                r = 2 * i + a
                rs = min(max(r - 4, 0), 24)
                e0 = rs - r + 7
                j = idx % NJ
                sbank, sbt = next_bank()
                for t in range(4):
                    B.op(PE, lambda e, t=t: e.matmul(
                        sbank[:, t * 64:(t + 1) * 64],
                        lhsT=KT[hb:hb + 64, hp, (rs + 2 * t) * 64:(rs + 2 * t) * 64 + 128],
                        rhs=QT[hb:hb + 64, hp, r * 64:(r + 1) * 64], start=True, stop=True),
                        reads=[t_Q, t_K], writes=[sbt])
                t2v = T2[:, h, :].rearrange("p (e q) -> p e q", e=16)[:, e0:e0 + 7:2, :]
                B.op(DVE, lambda e: e.tensor_tensor(
                    out=sct[j], in0=sbank[:, 0:256].rearrange("p (t q) -> p t q", t=4), in1=t2v, op=ALU.add),
                    reads=[sbt, t_T2], writes=[sctok[j], sbt])
                B.op(ACT, lambda e: e.activation(out=ptt[j], in_=sct[j], func=AF.Exp),
                     reads=[sctok[j]], writes=[pttok[j]])

            def stage_pv(idx):
                i, hg, hl, a = iters[idx]
                h = hg * 4 + hl
                r = 2 * i + a
                rs = min(max(r - 4, 0), 24)
                Vt, vb = (Vev, rs // 2) if rs % 2 == 0 else (Vod, (rs - 1) // 2)
                j = idx % NJ
                bi = 4 + (i % 2) * 2 + hg
                pbank, pbt = psum[bi], ptok[bi]
                for t in range(4):
                    B.op(PE, lambda e, t=t: e.matmul(
                        pbank[a * 64:(a + 1) * 64, hl * 65:(hl + 1) * 65], lhsT=ptt[j][:, t, :],
                        rhs=Vt[:, vb + t, h, :], start=(t == 0), stop=(t == 3)),
                        reads=[pttok[j], t_V], writes=[pbt])
                if hg == 1 and hl == 3 and a == 1:
                    finish_rowpair(i)

            def finish_rowpair(i):
                for hg in range(2):
                    bi = 4 + (i % 2) * 2 + hg
                    pbank, pbt = psum[bi], ptok[bi]
                    pv = pbank[:, 0:260].rearrange("p (h d) -> p h d", h=4)
                    B.op(DVE, lambda e: e.reciprocal(out=rec[:, hg * 4:(hg + 1) * 4], in_=pv[:, :, 64]),
                         reads=[pbt], writes=[t_stat])
                    B.op(DVE, lambda e: e.tensor_tensor(
                        out=ytm[:, i, hg * 256:(hg + 1) * 256].rearrange("p (h d) -> p h d", h=4), in0=pv[:, :, 0:64],
                        in1=bcast_last(rec[:, hg * 4:(hg + 1) * 4], 64), op=ALU.mult),
                        reads=[pbt, t_stat], writes=[t_ytm[i], pbt])
                cc = 2 + (i % 2)
                tsc = rms_rstd(ACT, ytm[:, i, :], 512, cc, [t_ytm[i]], None)
                B.op(DVE, lambda e: e.tensor_scalar(out=ynb[:, 0:512], in0=ytm[:, i, :], scalar1=rstd[:, cc:cc + 1],
                                                  scalar2=None, op0=ALU.mult),
                     reads=[t_ytm[i], tsc], writes=[t_yn])
                bank, bt = next_bank()
                pb = psbf(bank)
                for jj in range(4):
                    B.op(PE, lambda e, jj=jj: e.transpose(out=pb[:, jj * 128:(jj + 1) * 128],
                                                          in_=ynb[:, jj * 128:(jj + 1) * 128], identity=identb[:]),
                         reads=[t_yn, t_const], writes=[bt])
                for jj in range(4):
                    B.op(ACT, lambda e, jj=jj: e.activation(
                        out=mixT[:, 4 + jj, i * 128:(i + 1) * 128], in_=pb[:, jj * 128:(jj + 1) * 128],
                        func=AF.Identity, scale=natt[:, jj:jj + 1]), reads=[bt, t_const], writes=[t_mix[i], bt])

            for idx in range(len(iters) + LA):
                if idx < len(iters):
                    stage_scores(idx)
                if idx >= LA and idx - LA < len(iters):
                    stage_pv(idx - LA)

            pstate["n"] = 8
            B.barrier()
            if debug and n == 0:
                B.dma(SP, dbg['d_yatt'][:, :], av(144, 16384))
                B.barrier()
            t_U8 = Tok(); t_X = Tok(); t_Y8 = Tok(); t_toep = Tok(); t_xs = [Tok(), Tok()]
            B.dma(SP, toep, toepb[:, :, :], writes=[t_toep])
            winT = av(128, 16384).rearrange("p (g r c) -> p g r c", g=32, r=2)
            woutS = av(80, 16384).rearrange("p (g r c) -> p g r c", g=32, r=2)
            t_wi = Tok(); t_wo2 = Tok()
            for g in range(32):
                bank, bt = next_bank()
                pb = psbf(bank)
                for a in range(2):
                    B.op(PE, lambda e, g=g, a=a, pb=pb: e.transpose(
                        out=pb[:, a * 128:(a + 1) * 128], in_=Utm[:, a, g, :, :].rearrange("p s h -> p (s h)"), identity=identb[:]),
                        reads=[t_U, t_const], writes=[bt])
                evac(U8[:, g, :], pb[:, 0:256], [bt], [t_U8, bt])
            B.dma(SP, winT.rearrange("p g r c -> p (g r c)"), winTb[:, :], writes=[t_wi, t_U])
            B.op(DVE, lambda e: e.memset(xstate[:], 0.0), writes=t_xs)
            B.op(DVE, lambda e: e.memset(Xb[0:64, 0], 0.0), writes=[t_X])
            B.op(POOL, lambda e: e.memset(Xb[64:128, 255], 0.0), writes=[t_X])
            halves = ((0, 128, DVE, 0),)

            def u8rev(g, cb):
                t = U8[:, g, cb * 64:(cb + 1) * 64]
                return AP(t.tensor, t.offset + 63, [list(t.ap[0]), [-1, 64]])
            t_Sb = [Tok(), Tok()]
            NOWAIT = 'SCANWAIT' not in SKIP
            for jb in range(4):
                Sblk = Sbufs[jb % 2]
                t_S = [t_Sb[jb % 2], t_Sb[jb % 2]]
                for gq in range(8):
                    bank, bt = next_bank()
                    for gl in range(4):
                        g = gq * 4 + gl
                        for ri in range(2):
                            col = (gl * 2 + ri) * 64
                            B.op(PE, lambda e, g=g, ri=ri, col=col, bank=bank: e.matmul(
                                bank[0:64, col:col + 64], lhsT=winT[:, g, ri, 0:64],
                                rhs=U8[:, g, jb * 64:(jb + 1) * 64], start=True, stop=True),
                                reads=[t_U8, t_wi], writes=[bt])
                            B.op(PE, lambda e, g=g, ri=ri, col=col, bank=bank: e.matmul(
                                bank[64:128, col:col + 64], lhsT=winT[:, g, ri, 64:128],
                                rhs=u8rev(g, 3 - jb), start=True, stop=True),
                                reads=[t_U8, t_wi], writes=[bt])
                    src = bank[:, :].rearrange("p (g r c) -> p g r c", g=4, r=2)
                    dst = Sblk[:, :, :, gq * 4:(gq + 1) * 4].rearrange("p c r g -> p g r c")
                    B.op(ACT, lambda e, src=src, dst=dst: e.copy(out=dst, in_=src), reads=[bt], writes=[t_S[0], t_S[1], bt])
                if NOWAIT:
                    B.op(DVE, lambda e: e.tensor_copy(out=st1[:, :], in_=st1[:, :]), reads=[t_S[0], t_xs[0], t_ssmw], writes=[t_xs[0]])
                    DVE.same_wait = False
                for stp in range(64 if 'SCAN' not in SKIP else 0):
                    for (lo, hi, E, d) in halves:
                        c = stp
                        if stp == 0:
                            prev = xstate[lo:hi, :]
                        else:
                            prev = Sblk[lo:hi, c - 1].rearrange("p r g -> p (r g)")
                        pv = prev
                        psw = AP(pv.tensor, pv.offset + 32, [list(pv.ap[0]), [-32, 2], [1, 32]])
                        cur = Sblk[lo:hi, c].rearrange("p r g -> p (r g)")
                        rd = [t_S[d], t_xs[d], t_ssmw]
                        B.op(E, lambda e, lo=lo, hi=hi, pv=pv: e.tensor_tensor(
                            out=st1[lo:hi, :], in0=A1[lo:hi, :], in1=pv, op=ALU.mult), reads=rd, writes=[t_xs[d]])
                        B.op(E, lambda e, lo=lo, hi=hi, psw=psw: e.tensor_tensor(
                            out=st2[lo:hi, :].rearrange("p (r g) -> p r g", r=2),
                            in0=A2[lo:hi, :].rearrange("p (r g) -> p r g", r=2), in1=psw, op=ALU.mult),
                            reads=rd, writes=[t_xs[d]])
                        B.op(E, lambda e, lo=lo, hi=hi: e.tensor_tensor(
                            out=st1[lo:hi, :], in0=st1[lo:hi, :], in1=st2[lo:hi, :], op=ALU.add),
                            reads=[t_xs[d]], writes=[t_xs[d]])
                        B.op(E, lambda e, lo=lo, hi=hi, cur=cur: e.tensor_tensor(
                            out=cur, in0=st1[lo:hi, :], in1=cur, op=ALU.add), reads=[t_xs[d]], writes=[t_S[d], t_xs[d]])
                DVE.same_wait = SAME_WAIT
                cb = jb
                nn = 64 if cb < 3 else 63
                B.op(ACT, lambda e: e.copy(out=Xb[0:64, cb * 64 + 1:cb * 64 + 1 + nn], in_=Sblk[0:64, 0:nn]),
                     reads=[t_S[0]], writes=[t_X])
                cbw = 3 - jb
                nb = 64 if cbw > 0 else 63
                xb2 = av(0, 32768)[64:128, :]
                xrev = AP(xb2.tensor, xb2.offset + (cbw * 64 + 62) * 64, [list(xb2.ap[0]), [-64, nb], [1, 64]])
                B.op(DVE, lambda e: e.tensor_copy(out=xrev, in_=Sblk[64:128, 0:nb].rearrange("p c r g -> p c (r g)")),
                     reads=[t_S[0]], writes=[t_X])
                B.op(DVE, lambda e: e.tensor_copy(out=xstate[:, :], in_=Sblk[:, 63].rearrange("p r g -> p (r g)")),
                     reads=[t_S[0]], writes=[t_xs[0]])
            B.dma(SP, woutS.rearrange("p g r c -> p (g r c)"), woutSb[:, :], writes=[t_wo2, t_Sb[0]])
            for g in range(32):
                bank, bt = next_bank()
                B.op(PE, lambda e, g=g, bank=bank: e.matmul(bank[:, 0:256], lhsT=toep[:, g, :], rhs=U8[:, g, :],
                                                            start=True, stop=False), reads=[t_toep, t_U8], writes=[bt])
                for ri in range(2):
                    B.op(PE, lambda e, g=g, ri=ri, bank=bank: e.matmul(
                        bank[:, 0:256], lhsT=woutS[:, g, ri, :], rhs=Xb[:, :, ri, g], start=False, stop=(ri == 1)),
                        reads=[t_wo2, t_X], writes=[bt])
                evac(Y8[:, g, :], bank[:, 0:256], [bt], [t_Y8, t_Sb[1], bt])
            ytv = ytm.rearrange("p (a s) c -> p a s c", a=2)
            t_ysm = Tok()
            for g in range(32):
                bank, bt = next_bank()
                pb = psbf(bank)
                for a in range(2):
                    B.op(PE, lambda e, g=g, a=a, pb=pb: e.transpose(
                        out=pb[:, a * 128:(a + 1) * 128], in_=Y8[:, g, a * 128:(a + 1) * 128], identity=identb[:]),
                        reads=[t_Y8, t_const], writes=[bt])
                evac(ytv[:, :, :, g * 16:(g + 1) * 16], pb[:, 0:256].rearrange("p (a s h) -> p a s h", a=2, s=8),
                     [bt], [t_ysm, bt])
            B.barrier()
            if debug and n == 0:
                B.dma(SP, dbg['d_yssm'][:, :], av(144, 16384)); B.dma(SP, dbg['d_U8'][:, :], av(64, 16384))
                B.dma(SP, dbg['d_Xb'][:, :], av(0, 32768)); B.dma(SP, dbg['d_toep'][:, :], av(112, 8192)); B.dma(SP, dbg['d_Y8'][:, :], av(96, 16384))
                B.barrier()
            gsets = []
            for q in range(2):
                o = 64 + 11 * q
                gsets.append((av(o, 2048, F32), av(o + 2, 2048, F32), av(o + 4, 1024),
                              av(o + 5, 1024).rearrange("p (j t) -> p j t", j=4), av(o + 6, 2048, F32),
                              av(o + 8, 2048, F32), av(o + 10, 1024), Tok()))
            C0 = 1.5957691216057308
            for ti in range(16):
                a, s = ti // 8, ti % 8
                uu, ww, ygb, ygT, zz, y2, ynb2, tg = gsets[ti % 2]
                yv = ytm[:, ti, :]
                B.op(ACT, lambda e, yv=yv: e.activation(out=uu, in_=yv, func=AF.Square), reads=[t_ysm, t_toep], writes=[tg])
                B.op(DVE, lambda e: e.tensor_scalar(out=ww, in0=uu, scalar1=0.044715, scalar2=1.0, op0=ALU.mult, op1=ALU.add),
                     reads=[tg], writes=[tg])
                B.op(DVE, lambda e, yv=yv: e.tensor_tensor(out=ww, in0=ww, in1=yv, op=ALU.mult), reads=[tg, t_ysm], writes=[tg])
                B.op(ACT, lambda e: e.activation(out=uu, in_=ww, func=AF.Sigmoid, scale=C0), reads=[tg], writes=[tg])
                B.op(DVE, lambda e, yv=yv: e.tensor_tensor(out=ygb, in0=uu, in1=yv, op=ALU.mult), reads=[tg, t_ysm], writes=[tg])
                bank, bt = next_bank()
                pb = psbf(bank)
                for jj in range(4):
                    B.op(PE, lambda e, jj=jj, pb=pb: e.transpose(out=pb[:, jj * 128:(jj + 1) * 128],
                                                                 in_=ygb[:, jj * 128:(jj + 1) * 128], identity=identb[:]),
                         reads=[tg, t_const], writes=[bt])
                B.op(ACT, lambda e, pb=pb: e.copy(out=ygT, in_=pb[:, 0:512].rearrange("p (j t) -> p j t", j=4)),
                     reads=[bt], writes=[tg, bt])
                zb, zt = next_bank()
                for jj in range(4):
                    B.op(PE, lambda e, jj=jj, zb=zb: e.matmul(zb[:, :], lhsT=ygT[:, jj, :], rhs=wglu[:, jj, :],
                                                              start=(jj == 0), stop=(jj == 3)), reads=[tg, t_ssmw], writes=[zt])
                B.op(DVE, lambda e, zb=zb: e.tensor_tensor(out=zz, in0=zb[:, :], in1=bglu_sb[:], op=ALU.add),
                     reads=[zt, t_const], writes=[tg, zt])
                B.op(ACT, lambda e: e.activation(out=zz, in_=zz, func=AF.Sigmoid), reads=[tg], writes=[tg])
                B.op(DVE, lambda e: e.tensor_tensor(out=y2, in0=zz, in1=ygb, op=ALU.mult), reads=[tg], writes=[tg])
                cc = 4 + (ti % 2)
                tsc = rms_rstd(ACT, y2, 512, cc, [tg], None)
                B.op(DVE, lambda e: e.tensor_scalar(out=ynb2[:, 0:512], in0=y2, scalar1=rstd[:, cc:cc + 1], scalar2=None,
                                                  op0=ALU.mult), reads=[tg, tsc], writes=[tg])
                bank, bt = next_bank()
                pb = psbf(bank)
                for jj in range(4):
                    B.op(PE, lambda e, jj=jj, pb=pb: e.transpose(out=pb[:, jj * 128:(jj + 1) * 128],
                                                                 in_=ynb2[:, jj * 128:(jj + 1) * 128], identity=identb[:]),
                         reads=[tg, t_const], writes=[bt])
                for jj in range(4):
                    dstv = mixT[:, jj, a * 1024:(a + 1) * 1024].rearrange("p (c s) -> p c s", s=8)[:, :, s]
                    B.op(ACT, lambda e, jj=jj, pb=pb, dstv=dstv: e.activation(
                        out=dstv, in_=pb[:, jj * 128:(jj + 1) * 128], func=AF.Identity, scale=nssm[:, jj:jj + 1]),
                        reads=[bt, t_const], writes=[t_mix[0], bt])

            B.barrier()
            if debug and n == 0:
                B.dma(SP, dbg['d_mixT'][:, :], av(32, 32768))
                B.barrier()
            t_wo = Tok(); t_g = Tok()
            B.dma(SP, w_out, woutb[:, :, :], writes=[t_wo])
            for (dst, w) in ((gmb, 0), (gfb, 1)):
                src = gscr[n, w:w + 1, :]
                srcb = AP(src.tensor, src.offset, [[0, 128], [1, D]])
                B.dma(SP, dst, srcb, writes=[t_g])
            B.dma(SP, nfb, nfin_d[:, :], writes=[t_g])
            xl = [av(128 + 4 * i, 4096, F32) for i in range(2)]
            xltok = [Tok(), Tok()]
            xn5 = [av(136 + 2 * i, 2048) for i in range(2)]
            xn5tok = [Tok(), Tok()]
            sgt = [av(140 + 2 * i, 2048, F32) for i in range(2)]
            sgtok = [Tok(), Tok()]
            ot = [av(144 + 4 * i, 4096, F32) for i in range(2)]
            ottok = [Tok(), Tok()]
            tmpv = av(152, 4096, F32)
            t_tmp = Tok()
            wgtok = [Tok() for _ in range(NWB)]; wdtok = [Tok() for _ in range(NWB)]
            for tb in range(4 if 'P5' not in SKIP else 0):
                t_x1 = [Tok() for _ in range(4)]
                t_h2 = [Tok() for _ in range(4)]
                t_hid = Tok()
                for tt in range(4):
                    T = tb * 4 + tt
                    i = T % 2
                    B.dma(SP, xl[i], x_tiles[n, T * 128:(T + 1) * 128, :], writes=[xltok[i]])
                    for ob in range(2):
                        bank, bt = next_bank()
                        for kt in range(8):
                            B.op(PE, lambda e, kt=kt, T=T, ob=ob, bank=bank: e.matmul(
                                bank[:, :], lhsT=mixT[:, kt, T * 128:(T + 1) * 128],
                                rhs=w_out[:, kt, ob * 512:(ob + 1) * 512], start=(kt == 0), stop=(kt == 7)),
                                reads=[t_wo, t_mix[0]] + t_mix, writes=[bt])
                        B.op(DVE, lambda e, ob=ob, bank=bank: e.tensor_tensor(
                            out=tmpv[:, 0:512], in0=bank[:, :], in1=gmb[:, ob * 512:(ob + 1) * 512], op=ALU.mult),
                            reads=[bt, t_g], writes=[t_tmp, bt])
                        B.op(DVE, lambda e, ob=ob, tt=tt, i=i: e.tensor_tensor(
                            out=x1[:, tt, ob * 512:(ob + 1) * 512], in0=tmpv[:, 0:512],
                            in1=xl[i][:, ob * 512:(ob + 1) * 512], op=ALU.add),
                            reads=[t_tmp, xltok[i]], writes=[t_x1[tt]])
                    tsc = rms_rstd(ACT, x1[:, tt, :], 1024, i, [t_x1[tt]], None)
                    B.op(DVE, lambda e, tt=tt, i=i: e.tensor_scalar(out=xn5[i], in0=x1[:, tt, :], scalar1=rstd[:, i:i + 1],
                                                                  scalar2=None, op0=ALU.mult),
                         reads=[t_x1[tt], tsc], writes=[xn5tok[i]])
                    bank, bt = next_bank()
                    pb = psbf(bank)
                    for kt in range(8):
                        B.op(PE, lambda e, i=i, kt=kt, pb=pb: e.transpose(
                            out=pb[:, kt * 128:(kt + 1) * 128], in_=xn5[i][:, kt * 128:(kt + 1) * 128], identity=identb[:]),
                            reads=[xn5tok[i], t_const], writes=[bt])
                    for kt in range(8):
                        if kt % 2 == 0:
                            B.op(DVE, lambda e, kt=kt, pb=pb, tt=tt: e.tensor_scalar(
                                out=h2T[:, kt, tt * 128:(tt + 1) * 128], in0=pb[:, kt * 128:(kt + 1) * 128],
                                scalar1=sclffn[:, kt, n:n + 1], scalar2=modfm[:, 24 + kt, n:n + 1],
                                op0=ALU.mult, op1=ALU.add), reads=[bt, t_mod], writes=[t_h2[tt], bt])
                        else:
                            B.op(ACT, lambda e, kt=kt, pb=pb, tt=tt: e.activation(
                                out=h2T[:, kt, tt * 128:(tt + 1) * 128], in_=pb[:, kt * 128:(kt + 1) * 128],
                                func=AF.Identity, scale=sclffn[:, kt, n:n + 1], bias=modfm[:, 24 + kt, n:n + 1]),
                                reads=[bt, t_mod], writes=[t_h2[tt], bt])
                for fc in range(NFC):
                    i = fc % NWB
                    B.dma(SP, wgs[i], wgb[fc, :, :, :], writes=[wgtok[i]])
                    B.dma(SP, wus[i], wub[fc, :, :, :], writes=[wgtok[i]])
                    gb, gt = next_bank()
                    for kt in range(8):
                        B.op(PE, lambda e, i=i, kt=kt, gb=gb: e.matmul(gb[:, :], lhsT=wgs[i][:, kt, :], rhs=h2T[:, kt, :],
                                                                        start=(kt == 0), stop=(kt == 7)),
                             reads=[wgtok[i]] + t_h2, writes=[gt])
                    ub, ut = next_bank()
                    for kt in range(8):
                        B.op(PE, lambda e, i=i, kt=kt, ub=ub: e.matmul(ub[:, :], lhsT=wus[i][:, kt, :], rhs=h2T[:, kt, :],
                                                                        start=(kt == 0), stop=(kt == 7)),
                             reads=[wgtok[i]] + t_h2, writes=[ut])
                    si = fc % 2
                    B.op(ACT, lambda e, si=si, gb=gb: e.activation(out=sgt[si], in_=gb[:, :], func=AF.Silu),
                         reads=[gt], writes=[sgtok[si], gt])
                    B.op(DVE, lambda e, i=i, ub=ub, fc=fc: e.tensor_tensor(out=hid[:, fc, :], in0=sgt[si], in1=ub[:, :],
                                                                         op=ALU.mult),
                         reads=[sgtok[si], ut], writes=[t_hid, ut])
                for fc in range(NFC):
                    i = fc % NWB
                    B.dma(SP, wds[i], wdb[fc, :, :], writes=[wdtok[i]])
                    for tt in range(4):
                        for ob in range(2):
                            bi = tt * 2 + ob
                            B.op(PE, lambda e, i=i, fc=fc, tt=tt, ob=ob, bi=bi: e.matmul(
                                psum[bi][:, :], lhsT=hid[:, fc, tt * 128:(tt + 1) * 128],
                                rhs=wds[i][:, ob * 512:(ob + 1) * 512], start=(fc == 0), stop=(fc == NFC - 1)),
                                reads=[wdtok[i], t_hid], writes=[ptok[bi]])
                for tt in range(4):
                    T = tb * 4 + tt
                    for ob in range(2):
                        bi = tt * 2 + ob
                        B.op(DVE, lambda e, ob=ob, bi=bi: e.tensor_tensor(
                            out=tmpv[:, 0:512], in0=psum[bi][:, :], in1=gfb[:, ob * 512:(ob + 1) * 512], op=ALU.mult),
                            reads=[ptok[bi], t_g], writes=[t_tmp, ptok[bi]])
                        B.op(DVE, lambda e, ob=ob, tt=tt: e.tensor_tensor(
                            out=x1[:, tt, ob * 512:(ob + 1) * 512], in0=tmpv[:, 0:512],
                            in1=x1[:, tt, ob * 512:(ob + 1) * 512], op=ALU.add),
                            reads=[t_tmp], writes=[t_x1[tt]])
                    i = T % 2
                    tsc = rms_rstd(ACT, x1[:, tt, :], 1024, 6 + i, [t_x1[tt]], None)
                    B.op(DVE, lambda e, tt=tt, i=i: e.scalar_tensor_tensor(
                        out=ot[i], in0=x1[:, tt, :], scalar=rstd[:, 6 + i:7 + i], in1=nfb, op0=ALU.mult, op1=ALU.mult),
                        reads=[t_x1[tt], tsc, t_g], writes=[ottok[i]])
                    B.dma(SP, y_d[n, T * 128:(T + 1) * 128, :], ot[i], reads=[ottok[i]])
        B.finish()
    return nc


_bglu_holder = {}


def _layout_inputs(inp):
    f = np.float32
    out = {}
    xs = np.concatenate([inp["x_prompt"], inp["x_sample"]], axis=0)
    cs = np.concatenate([inp["c_prompt"], inp["c_sample"]], axis=0)
    shared = {}
    shared["w_ada"] = np.ascontiguousarray(inp["w_ada"][0], dtype=f)
    shared["b_ada"] = np.ascontiguousarray(inp["b_ada"][0:1], dtype=f)
    shared["nmix"] = np.ascontiguousarray(inp["norm_mix"][0].reshape(8, 128).T, dtype=f)
    shared["nffn"] = np.ascontiguousarray(inp["norm_ffn"][0].reshape(8, 128).T, dtype=f)
    shared["nfin"] = np.ascontiguousarray(np.broadcast_to(inp["norm_final"][None, :], (128, D)), dtype=f)
    shared["nssm"] = np.ascontiguousarray(inp["norm_ssm_out"][0].reshape(4, 128).T, dtype=f)
    shared["natt"] = np.ascontiguousarray(inp["norm_attn_out"][0].reshape(4, 128).T, dtype=f)
    shared["bglu"] = np.ascontiguousarray(np.broadcast_to(inp["b_glu"][0][None, :], (128, 512)), dtype=f)
    shared["w_in"] = np.ascontiguousarray(inp["w_in"][0], dtype=f)
    shared["w_glu"] = np.ascontiguousarray(inp["w_glu"][0], dtype=f)
    shared["w_out"] = np.ascontiguousarray(inp["w_out"][0], dtype=f)
    shared["w_g"] = np.ascontiguousarray(inp["w_ffn_gate"][0], dtype=f)
    shared["w_u"] = np.ascontiguousarray(inp["w_ffn_up"][0], dtype=f)
    shared["w_d"] = np.ascontiguousarray(inp["w_ffn_down"][0], dtype=f)

    def dpg(a):
        return np.ascontiguousarray(a.transpose(0, 2, 1).reshape(128, 32), dtype=f)

    shared["are"] = dpg(inp["ssm_a_re"][0])
    shared["aim"] = dpg(inp["ssm_a_im"][0])
    shared["ldt"] = dpg(np.broadcast_to(inp["ssm_log_dt"][0][:, :, None], (2, 32, 64)))
    shared["bre"] = np.ascontiguousarray(inp["ssm_b_re"][0].transpose(0, 2, 1, 3).reshape(128, 32, 16), dtype=f)
    shared["bim"] = np.ascontiguousarray(inp["ssm_b_im"][0].transpose(0, 2, 1, 3).reshape(128, 32, 16), dtype=f)
    shared["cre"] = np.ascontiguousarray(inp["ssm_c_re"][0].transpose(0, 3, 1, 2).reshape(128, 32, 16), dtype=f)
    shared["cim"] = np.ascontiguousarray(inp["ssm_c_im"][0].transpose(0, 3, 1, 2).reshape(128, 32, 16), dtype=f)
    dvec = inp["ssm_d"][0].reshape(32, 16)
    dd = np.zeros((16, 32, 16), f)
    for h in range(16):
        dd[h, :, h] = dvec[:, h]
    shared["dd"] = dd
    rpb = inp["na_rpb"][0]
    b = np.arange(2)[:, None, None, None]
    kc = np.arange(64)[None, :, None, None]
    e = np.arange(16)[None, None, :, None]
    qc = np.arange(64)[None, None, None, :]
    ri = np.clip(e + b, 0, 14) + 0 * kc + 0 * qc
    ci = np.clip(kc - qc + 15, 0, 30) + 0 * e + 0 * b
    t2raw = rpb[:, ri, ci].reshape(8, 128, 16 * 64)
    shared["t2raw"] = np.ascontiguousarray(t2raw, dtype=f)
    cstart = np.clip(qc - 8, 0, 48)
    valid = (kc >= cstart) & (kc < cstart + 16) & ((e + b) <= 14)
    m2 = np.where(valid, 0.0, NEG).astype(f) + np.zeros((2, 64, 16, 64), f)
    shared["m2"] = np.ascontiguousarray(m2.reshape(128, 16 * 64))
    shared["identf"] = np.eye(128, dtype=f)
    in_maps = []
    for core in range(8):
        m = dict(shared)
        m["x"] = np.ascontiguousarray(xs[core * 3:(core + 1) * 3], dtype=f)
        cc = cs[core * 3:(core + 1) * 3]
        cT = np.zeros((128, 8, 4), f)
        cT[:, :, 0:3] = cc.reshape(3, 8, 128).transpose(2, 1, 0)
        m["cT"] = cT
        in_maps.append(m)
    return in_maps


def kernel(**inputs):
    inp = {k: np.asarray(v) for k, v in inputs.items()}
    in_maps = _layout_inputs(inp)
    nc = build_nc()
    res = run_bass_kernel_spmd(nc, in_maps, core_ids=list(range(8)))
    ys = np.concatenate([np.asarray(r["y"]).reshape(NSEQ, L, D) for r in res.results], axis=0).astype(np.float32)
    return ys[0:8], ys[8:24]
```

```python
import math
import numpy as np
import ml_dtypes
import concourse.bass as bass
import concourse.mybir as mybir
from concourse.bass_utils import run_bass_kernel_spmd
from concourse.ap import AP

F32 = mybir.dt.float32
BF16 = mybir.dt.bfloat16
ALU = mybir.AluOpType
AF = mybir.ActivationFunctionType

D = 1024
L = 2048
NSEQ = 3
DFF = 2816
NFC = 22
EPS = 1e-6
NEG = -30000.0


class Tok:
    __slots__ = ("w", "r")

    def __init__(self):
        self.w = None
        self.r = {}


class EngW:
    def __init__(self, eng, sem, same_wait):
        self.eng = eng
        self.sem = sem
        self.n = 0
        self.waited = {}
        self.same_wait = same_wait


SAME_WAIT = True


class Builder:
    def __init__(self, nc, sems, dma_sems):
        self.nc = nc
        self.pe = EngW(nc.tensor, sems[0], False)
        self.act = EngW(nc.scalar, sems[1], SAME_WAIT)
        self.dve = EngW(nc.vector, sems[2], SAME_WAIT)
        self.pool = EngW(nc.gpsimd, sems[3], True)
        self.sp = EngW(nc.sync, sems[4], False)
        self.engs = [self.pe, self.act, self.dve, self.pool, self.sp]
        self.dma_sems = dma_sems
        self.dma_n = [0] * len(dma_sems)
        self.dma_rr = 0

    def _wait(self, E, deps):
        for (sem, val) in deps:
            if sem is E.sem and not E.same_wait:
                continue
            k = id(sem)
            if E.waited.get(k, 0) < val:
                E.eng.wait_ge(sem, val)
                E.waited[k] = val

    @staticmethod
    def _deps(reads, writes):
        deps = []
        for t in reads:
            if t.w is not None:
                deps.append(t.w)
        for t in writes:
            if t.w is not None:
                deps.append(t.w)
            for k, v in t.r.items():
                deps.append(v)
        return deps

    @staticmethod
    def _upd(me, reads, writes):
        for t in reads:
            k = id(me[0])
            if k not in t.r or t.r[k][1] < me[1]:
                t.r[k] = me
        for t in writes:
            t.w = me
            t.r = {}

    def op(self, E, fn, reads=(), writes=()):
        self._wait(E, self._deps(reads, writes))
        inst = fn(E.eng)
        E.n += 1
        inst.then_inc(E.sem, 1)
        self._upd((E.sem, E.n), reads, writes)

    def dma(self, Q, out, in_, reads=(), writes=(), **kw):
        k = self.dma_rr
        self.dma_rr = (self.dma_rr + 1) % len(self.dma_sems)
        sem = self.dma_sems[k]
        deps = self._deps(reads, writes)
        if self.dma_n[k] > 0:
            deps.append((sem, 16 * self.dma_n[k]))
        self._wait(Q, deps)
        inst = Q.eng.dma_start(out=out, in_=in_, **kw)
        self.dma_n[k] += 1
        inst.then_inc(sem, 16)
        self._upd((sem, 16 * self.dma_n[k]), reads, writes)

    def barrier(self):
        for E in self.engs:
            deps = [(F.sem, F.n) for F in self.engs if F is not E and F.n > 0]
            deps += [(s, 16 * n) for s, n in zip(self.dma_sems, self.dma_n) if n > 0]
            self._wait(E, deps)

    def finish(self):
        deps = [(s, 16 * n) for s, n in zip(self.dma_sems, self.dma_n) if n > 0]
        deps += [(F.sem, F.n) for F in self.engs if F.n > 0 and F is not self.sp]
        self._wait(self.sp, deps)


def bcast_last(ap, n):
    return AP(ap.tensor, ap.offset, [list(d) for d in ap.ap] + [[0, n]])


SKIP = set()


def build_nc(debug=False):
    nc = bass.Bass("TRN2", target_bir_lowering=False)
    dbg = {}
    def dout(name, shape, dt):
        dbg[name] = nc.dram_tensor(name, list(shape), dt, kind="ExternalOutput").ap()
        return dbg[name]
    if debug:
        dout('d_hT', [128, 8 * 2048], BF16); dout('d_QT', [128, 4 * 2048], BF16); dout('d_KT', [128, 4 * 2048], BF16)
        dout('d_Utm', [128, 8192], BF16); dout('d_yatt', [128, 8192], BF16); dout('d_mixT', [128, 8 * 2048], BF16)
        dout('d_yssm', [128, 8192], BF16); dout('d_Vev', [128, 16 * 8 * 65], BF16); dout('d_mod', [128, 192], F32)
        dout('d_U8', [128, 8192], BF16); dout('d_Xb', [128, 16384], BF16); dout('d_toep', [128, 4096], BF16)
        dout('d_Y8', [128, 8192], BF16); dout('d_T2', [128, 8192], BF16)

    def din(name, shape, dt=F32):
        return nc.dram_tensor(name, list(shape), dt, kind="ExternalInput").ap()

    def dscr(name, shape, dt):
        return nc.dram_tensor(name, list(shape), dt, kind="Internal").ap()

    x_d = din("x", [NSEQ, L, D])
    y_d = nc.dram_tensor("y", [NSEQ, L, D], F32, kind="ExternalOutput").ap()
    cT_d = din("cT", [128, 8, 4])
    wada_d = din("w_ada", [D, 6 * D])
    bada_d = din("b_ada", [1, 6 * D])
    nmix_d = din("nmix", [128, 8])
    nffn_d = din("nffn", [128, 8])
    nfin_d = din("nfin", [128, D])
    nssm_d = din("nssm", [128, 4])
    natt_d = din("natt", [128, 4])
    bglu_d = din("bglu", [128, 512])
    win_d = din("w_in", [D, 2048])
    wglu_d = din("w_glu", [512, 512])
    wout_d = din("w_out", [D, D])
    wg_d = din("w_g", [D, DFF])
    wu_d = din("w_u", [D, DFF])
    wd_d = din("w_d", [DFF, D])
    are_d = din("are", [128, 32])
    aim_d = din("aim", [128, 32])
    ldt_d = din("ldt", [128, 32])
    bre_d = din("bre", [128, 32, 16])
    bim_d = din("bim", [128, 32, 16])
    cre_d = din("cre", [128, 32, 16])
    cim_d = din("cim", [128, 32, 16])
    dd_d = din("dd", [16, 32, 16])
    t2raw_d = din("t2raw", [8, 128, 16 * 64])
    m2_d = din("m2", [128, 16 * 64])
    identf_d = din("identf", [128, 128])

    winb = dscr("winb", [128, 8, 2048], BF16)
    wglub = dscr("wglub", [128, 4, 512], BF16)
    woutb = dscr("woutb", [128, 8, 1024], BF16)
    wgb = dscr("wgb", [NFC, 128, 8, 128], BF16)
    wub = dscr("wub", [NFC, 128, 8, 128], BF16)
    wdb = dscr("wdb", [NFC, 128, 1024], BF16)
    toepb = dscr("toepb", [128, 32, 128], BF16)
    kallb = dscr("kallb", [16, 32, 240], BF16)
    gscr = dscr("gscr", [NSEQ, 2, D], F32)
    winTb = dscr("winTb", [128, 32 * 2 * 128], BF16)
    woutSb = dscr("woutSb", [128, 32 * 2 * 128], BF16)

    from contextlib import ExitStack

    with ExitStack() as es:
        def sb(name, shape, dt):
            return es.enter_context(nc.sbuf_tensor("sb_" + name, list(shape), dt))

        sems = [es.enter_context(nc.semaphore("e%d" % i)) for i in range(5)]
        dsems = [es.enter_context(nc.semaphore("d%d" % i)) for i in range(16)]
        B = Builder(nc, sems, dsems)
        PE, ACT, DVE, POOL, SP = B.pe, B.act, B.dve, B.pool, B.sp

        psum = [es.enter_context(nc.psum_tensor("ps%d" % i, [128, 512], F32)) for i in range(8)]
        ptok = [Tok() for _ in range(8)]
        pstate = {"i": 0, "n": 8}

        def next_bank():
            i = pstate["i"] % pstate["n"]
            pstate["i"] = (i + 1) % pstate["n"]
            return psum[i], ptok[i]

        def psbf(bank):
            return bank[:].bitcast(BF16)

        identf = sb("identf", [128, 128], F32)
        identb = sb("identb", [128, 128], BF16)
        ones1 = sb("ones1", [1, 8], F32)
        epst = sb("epst", [128, 1], F32)
        modfm = sb("modfm", [128, 48, 4], F32)
        sclmix = sb("sclmix", [128, 8, 4], F32)
        sclffn = sb("sclffn", [128, 8, 4], F32)
        nssm = sb("nssm", [128, 4], F32)
        natt = sb("natt", [128, 4], F32)
        A1 = sb("A1", [128, 64], F32)
        A2 = sb("A2", [128, 64], F32)
        T2 = sb("T2", [128, 8, 16 * 64], BF16)
        bglu_sb = sb("bglu_sb", [128, 512], F32)
        wglu = sb("wglu", [128, 4, 512], BF16)
        xstate = sb("xstate", [128, 64], F32)
        st1 = sb("st1", [128, 64], F32)
        st2 = sb("st2", [128, 64], F32)
        ssq = sb("ssq", [128, 8], F32)
        rstd = sb("rstd", [128, 8], F32)
        rec = sb("rec", [128, 8], F32)
        junk = sb("junk", [128, 1024], BF16)
        t_const = Tok()
        t_mod = Tok()
        t_ssmw = Tok()
        t_stat = Tok()
        t_junk = Tok()

        NA = 160 * 512
        arena = sb("arena", [128, NA], BF16)

        def av(off_kb, nbytes, dt=BF16):
            o = int(off_kb * 512)
            ap = arena[:, o:o + nbytes // 2]
            if dt is F32:
                ap = ap.bitcast(F32)
            return ap

        B.dma(SP, identf[:], identf_d[:, :], writes=[t_const])
        B.op(DVE, lambda e: e.tensor_copy(out=identb[:], in_=identf[:]), reads=[t_const], writes=[t_const])
        B.op(DVE, lambda e: e.memset(ones1[:], 1.0), writes=[t_const])
        B.op(DVE, lambda e: e.memset(epst[:], EPS), writes=[t_const])
        B.dma(SP, nssm[:], nssm_d[:, :], writes=[t_const])
        B.dma(SP, natt[:], natt_d[:, :], writes=[t_const])
        B.dma(SP, bglu_sb[:], bglu_d[:, :], writes=[t_const])
        winT = av(128, 16384).rearrange("p (g r c) -> p g r c", g=32, r=2)
        woutS = av(144, 16384).rearrange("p (g r c) -> p g r c", g=32, r=2)

        def f32tile(off_kb, shape):
            n = int(np.prod(shape))
            ap = av(off_kb, n * 4, F32)
            if len(shape) == 2:
                return ap.rearrange("p (a b) -> p a b", a=shape[0]) if False else ap
            return ap

        ts = Tok()
        off = [0.0]

        def alloc(ncols):
            o = off[0]
            off[0] += ncols * 4 / 1024.0
            return av(o, ncols * 4, F32)

        are = alloc(32); aim = alloc(32); ldt = alloc(32)
        lr = alloc(32); dtt = alloc(32); er = alloc(32); ph2 = alloc(64); k2 = alloc(64); sc2 = alloc(64)
        ar = alloc(32); ai = alloc(32); nr = alloc(32); den = alloc(32); zr = alloc(32); zi = alloc(32)
        tA = alloc(32); tB = alloc(32); a8r = alloc(32); a8i = alloc(32)
        B.dma(SP, are, are_d[:, :], writes=[ts])
        B.dma(SP, aim, aim_d[:, :], writes=[ts])
        B.dma(SP, ldt, ldt_d[:, :], writes=[ts])

        def V(fn):
            B.op(DVE, fn, reads=[ts], writes=[ts])

        def A(fn):
            B.op(ACT, fn, reads=[ts], writes=[ts])

        V(lambda e: e.tensor_scalar(out=lr, in0=are, scalar1=-1e-4, scalar2=None, op0=ALU.min))
        A(lambda e: e.activation(out=dtt, in_=ldt, func=AF.Exp))
        V(lambda e: e.tensor_tensor(out=er, in0=lr, in1=dtt, op=ALU.mult))
        A(lambda e: e.activation(out=er, in_=er, func=AF.Exp))
        V(lambda e: e.tensor_tensor(out=ph2[:, 0:32], in0=aim, in1=dtt, op=ALU.mult))
        V(lambda e: e.tensor_scalar(out=ph2[:, 32:64], in0=ph2[:, 0:32], scalar1=math.pi / 2, scalar2=None, op0=ALU.add))
        MAGIC = 12582912.0
        V(lambda e: e.tensor_scalar(out=k2, in0=ph2, scalar1=1.0 / (2 * math.pi), scalar2=MAGIC, op0=ALU.mult, op1=ALU.add))
        V(lambda e: e.tensor_scalar(out=k2, in0=k2, scalar1=-MAGIC, scalar2=None, op0=ALU.add))
        V(lambda e: e.scalar_tensor_tensor(out=ph2, in0=k2, scalar=-2 * math.pi, in1=ph2, op0=ALU.mult, op1=ALU.add))
        V(lambda e: e.tensor_scalar(out=ph2, in0=ph2, scalar1=3.14159, scalar2=-3.14159, op0=ALU.min, op1=ALU.max))
        A(lambda e: e.activation(out=sc2, in_=ph2, func=AF.Sin))
        V(lambda e: e.tensor_tensor(out=ai, in0=er, in1=sc2[:, 0:32], op=ALU.mult))
        V(lambda e: e.tensor_tensor(out=ar, in0=er, in1=sc2[:, 32:64], op=ALU.mult))
        V(lambda e: e.tensor_scalar(out=nr, in0=ar, scalar1=-1.0, scalar2=None, op0=ALU.add))
        V(lambda e: e.tensor_tensor(out=den, in0=lr, in1=lr, op=ALU.mult))
        V(lambda e: e.tensor_tensor(out=tA, in0=aim, in1=aim, op=ALU.mult))
        V(lambda e: e.tensor_tensor(out=den, in0=den, in1=tA, op=ALU.add))
        V(lambda e: e.reciprocal(out=den, in_=den))
        V(lambda e: e.tensor_tensor(out=tA, in0=nr, in1=lr, op=ALU.mult))
        V(lambda e: e.tensor_tensor(out=tB, in0=ai, in1=aim, op=ALU.mult))
        V(lambda e: e.tensor_tensor(out=tA, in0=tA, in1=tB, op=ALU.add))
        V(lambda e: e.tensor_tensor(out=zr, in0=tA, in1=den, op=ALU.mult))
        V(lambda e: e.tensor_tensor(out=tA, in0=ai, in1=lr, op=ALU.mult))
        V(lambda e: e.tensor_tensor(out=tB, in0=nr, in1=aim, op=ALU.mult))
        V(lambda e: e.tensor_tensor(out=tA, in0=tA, in1=tB, op=ALU.subtract))
        V(lambda e: e.tensor_tensor(out=zi, in0=tA, in1=den, op=ALU.mult))
        V(lambda e: e.tensor_copy(out=a8r, in_=ar))
        V(lambda e: e.tensor_copy(out=a8i, in_=ai))
        for _ in range(3):
            V(lambda e: e.tensor_tensor(out=tA, in0=a8r, in1=a8r, op=ALU.mult))
            V(lambda e: e.tensor_tensor(out=tB, in0=a8i, in1=a8i, op=ALU.mult))
            V(lambda e: e.tensor_tensor(out=a8i, in0=a8r, in1=a8i, op=ALU.mult))
            V(lambda e: e.tensor_tensor(out=a8r, in0=tA, in1=tB, op=ALU.subtract))
            V(lambda e: e.tensor_scalar(out=a8i, in0=a8i, scalar1=2.0, scalar2=None, op0=ALU.mult))
        B.op(DVE, lambda e: e.tensor_copy(out=A1[:, 0:32], in_=a8r), reads=[ts], writes=[t_ssmw])
        B.op(DVE, lambda e: e.tensor_copy(out=A1[:, 32:64], in_=a8r), reads=[ts], writes=[t_ssmw])
        B.op(DVE, lambda e: e.tensor_scalar(out=A2[:, 0:32], in0=a8i, scalar1=-1.0, scalar2=None, op0=ALU.mult), reads=[ts], writes=[t_ssmw])
        B.op(DVE, lambda e: e.tensor_copy(out=A2[:, 32:64], in_=a8i), reads=[ts], writes=[t_ssmw])

        def alloc3(nk):
            return alloc(nk * 512).rearrange("p (k g h) -> p k g h", k=nk, g=32)

        TR = alloc3(8); TI_off = off[0]; TI = alloc3(8); GR = alloc3(9); GI = alloc3(9)
        braw = alloc(512).rearrange("p (g h) -> p g h", g=32)
        biraw = alloc(512).rearrange("p (g h) -> p g h", g=32)
        tmp1 = alloc(512).rearrange("p (g h) -> p g h", g=32)
        tmp2 = alloc(512).rearrange("p (g h) -> p g h", g=32)
        B.dma(SP, braw, bre_d[:, :, :], writes=[ts])
        B.dma(SP, biraw, bim_d[:, :, :], writes=[ts])
        B.dma(SP, GR[:, 0], cre_d[:, :, :], writes=[ts])
        B.dma(SP, GI[:, 0], cim_d[:, :, :], writes=[ts])
        zrb = bcast_last(zr, 16); zib = bcast_last(zi, 16)
        arb = bcast_last(ar, 16); aib = bcast_last(ai, 16)

        def cmul(outR, outI, inR, inI, sR, sI):
            V(lambda e: e.tensor_tensor(out=tmp1, in0=inR, in1=sR, op=ALU.mult))
            V(lambda e: e.tensor_tensor(out=tmp2, in0=inI, in1=sI, op=ALU.mult))
            V(lambda e: e.tensor_tensor(out=outR, in0=tmp1, in1=tmp2, op=ALU.subtract))
            V(lambda e: e.tensor_tensor(out=tmp1, in0=inR, in1=sI, op=ALU.mult))
            V(lambda e: e.tensor_tensor(out=tmp2, in0=inI, in1=sR, op=ALU.mult))
            V(lambda e: e.tensor_tensor(out=outI, in0=tmp1, in1=tmp2, op=ALU.add))

        cmul(TR[:, 0], TI[:, 0], braw, biraw, zrb, zib)
        for k in range(1, 8):
            cmul(TR[:, k], TI[:, k], TR[:, k - 1], TI[:, k - 1], arb, aib)
        for k in range(1, 9):
            cmul(GR[:, k], GI[:, k], GR[:, k - 1], GI[:, k - 1], arb, aib)

        for t in range(8):
            for (lo, hi, k) in ((0, 64, t + 1), (64, 128, 8 - t)):
                B.op(DVE, lambda e, lo=lo, hi=hi, k=k, t=t: e.tensor_copy(
                    out=woutS[lo:hi, :, 0, t * 16:(t + 1) * 16], in_=GR[lo:hi, k]), reads=[ts], writes=[t_ssmw])
                B.op(DVE, lambda e, lo=lo, hi=hi, k=k, t=t: e.tensor_scalar(
                    out=woutS[lo:hi, :, 1, t * 16:(t + 1) * 16], in0=GI[lo:hi, k], scalar1=-1.0, scalar2=None,
                    op0=ALU.mult), reads=[ts], writes=[t_ssmw])
        ddt = alloc(512).rearrange("p (g h) -> p g h", g=32)
        winp_off = off[0]
        winp = alloc(32 * 2 * 128).rearrange("p (g r c) -> p g r c", g=32, r=2)
        for s in range(8):
            for (lo, hi, k) in ((0, 64, 7 - s), (64, 128, s)):
                V(lambda e, lo=lo, hi=hi, k=k, s=s: e.tensor_copy(out=winp[lo:hi, :, 0, s * 16:(s + 1) * 16], in_=TR[lo:hi, k]))
                V(lambda e, lo=lo, hi=hi, k=k, s=s: e.tensor_copy(out=winp[lo:hi, :, 1, s * 16:(s + 1) * 16], in_=TI[lo:hi, k]))
        for g in range(32):
            bank, bt = next_bank()
            for ri in range(2):
                B.op(PE, lambda e, g=g, ri=ri, bank=bank: e.transpose(
                    out=bank[:, ri * 128:(ri + 1) * 128], in_=winp[:, g, ri, :], identity=identf[:]),
                    reads=[ts, t_const], writes=[bt])
            B.op(ACT if g % 2 else DVE, (lambda e, g=g, bank=bank: e.tensor_copy(
                out=winT[:, g, :, :], in_=bank[:, 0:256].rearrange("p (r c) -> p r c", r=2))) if g % 2 == 0 else
                (lambda e, g=g, bank=bank: e.copy(
                    out=winT[:, g, :, :], in_=bank[:, 0:256].rearrange("p (r c) -> p r c", r=2))),
                reads=[bt], writes=[t_ssmw, bt])

        V(lambda e: e.tensor_scalar(out=tmp1, in0=TI[:, 0], scalar1=-1.0, scalar2=None, op0=ALU.mult))
        B.dma(SP, ddt[0:16], dd_d[:, :, :], writes=[ts])
        gbase = winp_off
        GallR = av(gbase, 16 * 240 * 4, F32).rearrange("p (g t h) -> p g t h", g=16, t=15)
        GallI = av(gbase + 15, 16 * 240 * 4, F32).rearrange("p (g t h) -> p g t h", g=16, t=15)
        kall = av(gbase + 30, 16 * 240 * 4, F32).rearrange("p (g c) -> p g c", g=16)
        kallbf = av(TI_off, 16 * 240 * 2).rearrange("p (g c) -> p g c", g=16)
        for gh in range(2):
            gs = slice(gh * 16, (gh + 1) * 16)
            V(lambda e: e.memset(GallR, 0.0))
            V(lambda e: e.memset(GallI, 0.0))
            for k in range(8):
                V(lambda e, k=k: e.tensor_copy(out=GallR[0:64, :, 7 + k, :], in_=GR[0:64, k, gs, :]))
                V(lambda e, k=k: e.tensor_copy(out=GallI[0:64, :, 7 + k, :], in_=GI[0:64, k, gs, :]))
                V(lambda e, k=k: e.tensor_copy(out=GallR[64:128, :, 7 - k, :], in_=GR[64:128, k, gs, :]))
                V(lambda e, k=k: e.tensor_copy(out=GallI[64:128, :, 7 - k, :], in_=GI[64:128, k, gs, :]))
            for g2 in range(8):
                bank, bt = next_bank()
                for gi in range(2):
                    gl = g2 * 2 + gi
                    g = gh * 16 + gl
                    B.op(PE, lambda e, g=g, gl=gl, gi=gi, bank=bank: e.matmul(
                        bank[0:16, gi * 240:(gi + 1) * 240], lhsT=TR[:, 0, g, :],
                        rhs=GallR[:, gl].rearrange("p t h -> p (t h)"), start=True, stop=False),
                        reads=[ts], writes=[bt])
                    B.op(PE, lambda e, g=g, gl=gl, gi=gi, bank=bank: e.matmul(
                        bank[0:16, gi * 240:(gi + 1) * 240], lhsT=tmp1[:, g, :],
                        rhs=GallI[:, gl].rearrange("p t h -> p (t h)"), start=False, stop=True),
                        reads=[ts], writes=[bt])
                B.op(DVE, lambda e, g2=g2, bank=bank: e.tensor_copy(
                    out=kall[0:16, g2 * 2:g2 * 2 + 2, :], in_=bank[0:16, 0:480].rearrange("p (g c) -> p g c", g=2)),
                    reads=[bt], writes=[ts, bt])
            V(lambda e: e.tensor_tensor(out=kall[0:16, :, 112:128], in0=kall[0:16, :, 112:128], in1=ddt[0:16, gs, :], op=ALU.add))
            V(lambda e: e.tensor_copy(out=kallbf[0:16], in_=kall[0:16]))
            B.dma(SP, kallb[:, gs, :], kallbf[0:16], reads=[ts], writes=[ts])
        for s in range(8):
            B.dma(SP, toepb[s * 16:(s + 1) * 16, :, :], kallb[:, :, (7 - s) * 16:(7 - s) * 16 + 128],
                  reads=[ts], writes=[ts])
        B.dma(SP, winTb[:, :], winT.rearrange("p g r c -> p (g r c)"), reads=[t_ssmw], writes=[ts])
        B.dma(SP, woutSb[:, :], woutS.rearrange("p g r c -> p (g r c)"), reads=[t_ssmw], writes=[ts])

        B.barrier()
        cT = av(0, 128, F32).rearrange("p (k n) -> p k n", k=8)
        sil = av(1, 128, F32).rearrange("p (k n) -> p k n", k=8)
        badar = av(2, 6 * D * 4, F32)
        wblk = [av(32 + 16 * i, 8 * 512 * 4, F32).rearrange("p (k c) -> p k c", k=8) for i in range(2)]
        wtok = [Tok(), Tok()]
        tm = Tok()
        B.dma(SP, cT, cT_d[:, :, :], writes=[tm])
        B.dma(SP, badar[0:1], bada_d[:, :], writes=[tm])
        B.op(ACT, lambda e: e.activation(out=sil, in_=cT, func=AF.Silu), reads=[tm], writes=[tm])
        wada_v = wada_d.rearrange("(k p) c -> p k c", p=128)
        mbank, mtok = next_bank()
        for blk in range(12):
            wb = wblk[blk % 2]
            B.dma(SP, wb, wada_v[:, :, blk * 512:(blk + 1) * 512], writes=[wtok[blk % 2]])
            for j in range(4):
                ct = blk * 4 + j
                o = (ct % 48) * 4
                B.op(PE, lambda e, ct=ct, o=o: e.matmul(
                    mbank[:, o:o + 4], lhsT=badar[0:1, ct * 128:(ct + 1) * 128], rhs=ones1[0:1, 0:4],
                    start=True, stop=False), reads=[tm, t_const], writes=[mtok])
                for kt in range(8):
                    B.op(PE, lambda e, wb=wb, j=j, kt=kt, o=o: e.matmul(
                        mbank[:, o:o + 4], lhsT=wb[:, kt, j * 128:(j + 1) * 128], rhs=sil[:, kt, :],
                        start=False, stop=(kt == 7)), reads=[tm, wtok[blk % 2]], writes=[mtok])
        B.op(DVE, lambda e: e.tensor_copy(out=modfm[:], in_=mbank[:, 0:192].rearrange("p (c n) -> p c n", n=4)),
             reads=[mtok], writes=[t_mod, mtok])
        nm = av(3, 32, F32); nf = av(3.5, 32, F32)
        B.dma(SP, nm, nmix_d[:, :], writes=[tm])
        B.dma(SP, nf, nffn_d[:, :], writes=[tm])
        for (dst, nrm, base) in ((sclmix, nm, 8), (sclffn, nf, 32)):
            B.op(DVE, lambda e, dst=dst, base=base: e.tensor_scalar(
                out=dst[:], in0=modfm[:, base:base + 8, :], scalar1=1.0, scalar2=None, op0=ALU.add),
                reads=[t_mod, tm], writes=[t_mod])
            B.op(DVE, lambda e, dst=dst, nrm=nrm: e.tensor_tensor(
                out=dst[:], in0=dst[:], in1=bcast_last(nrm, 4), op=ALU.mult), reads=[t_mod, tm], writes=[t_mod])
        for w, base in ((0, 16), (1, 40)):
            for n in range(NSEQ):
                B.dma(SP, gscr[n, w, :].rearrange("(k p) -> p k", p=128), modfm[:, base:base + 8, n],
                      reads=[t_mod], writes=[tm], allow_slow_non_contiguous=True)

        B.barrier()
        def conv_gen(jobs, bufs, D=2):
            nb = len(bufs)
            for k in range(len(jobs) + D):
                if k < len(jobs):
                    src, ncol, dst, vw = jobs[k]
                    st, stb, tk, tkb, E = bufs[k % nb]
                    B.dma(SP, st[:, 0:ncol], src, writes=[tk])
                    if E is ACT:
                        B.op(E, lambda e: e.copy(out=stb[:, 0:ncol], in_=st[:, 0:ncol]), reads=[tk], writes=[tkb])
                    else:
                        B.op(E, lambda e: e.tensor_copy(out=stb[:, 0:ncol], in_=st[:, 0:ncol]), reads=[tk], writes=[tkb])
                if k - D >= 0:
                    src, ncol, dst, vw = jobs[k - D]
                    st, stb, tk, tkb, E = bufs[(k - D) % nb]
                    B.dma(SP, dst, vw(stb[:, 0:ncol]), reads=[tkb])
                yield

        ident_v = lambda x: x
        fj = lambda x: x.rearrange("p (f j) -> p f j", j=128)
        jobs_a = []
        for kt in range(8):
            jobs_a.append((win_d[kt * 128:(kt + 1) * 128, :], 2048, winb[:, kt, :], ident_v))
        for kt in range(4):
            jobs_a.append((wglu_d[kt * 128:(kt + 1) * 128, :], 512, wglub[:, kt, :], ident_v))
        for kt in range(8):
            jobs_a.append((wout_d[kt * 128:(kt + 1) * 128, :], 1024, woutb[:, kt, :], ident_v))
        jobs_f = []
        for (wsrc, wdst) in ((wg_d, wgb), (wu_d, wub)):
            for kt in range(8):
                for (f0, f1) in ((0, 8), (8, 16), (16, 22)):
                    jobs_f.append((wsrc[kt * 128:(kt + 1) * 128, f0 * 128:f1 * 128], (f1 - f0) * 128,
                                   wdst[f0:f1, :, kt, :].rearrange("f p j -> p f j"), fj))
        for fc in range(NFC):
            jobs_f.append((wd_d[fc * 128:(fc + 1) * 128, :], 1024, wdb[fc, :, :], ident_v))
        bufs_a = [(av(0 + 8 * i, 2048 * 4, F32), av(32 + 4 * i, 2048 * 2), Tok(), Tok(), (ACT, DVE, POOL)[i]) for i in range(3)]
        for _ in conv_gen(jobs_a, bufs_a):
            pass
        bufs_f = [(av(10 + 6 * i, 1024 * 4, F32), av(14 + 6 * i, 1024 * 2), Tok(), Tok(), POOL) for i in range(3)]
        ffn_conv = conv_gen(jobs_f, bufs_f)

        B.barrier()
        m2t = av(64, 1024 * 4, F32)
        t2r = [av(72 + 4 * i, 1024 * 4, F32) for i in range(2)]
        t2tok = [Tok(), Tok()]
        tm2 = Tok()
        t_T2 = Tok()
        B.dma(SP, m2t, m2_d[:, :], writes=[tm2])
        for h in range(8):
            B.dma(SP, t2r[h % 2], t2raw_d[h, :, :], writes=[t2tok[h % 2]])
            B.op(DVE, lambda e, h=h: e.tensor_tensor(out=T2[:, h, :], in0=t2r[h % 2], in1=m2t, op=ALU.add),
                 reads=[t2tok[h % 2], tm2], writes=[t_T2])
        B.barrier()
        B.dma(SP, wglu[:], wglub[:, :, :], writes=[t_ssmw])

        hT = av(0, 32768).rearrange("p (k t) -> p k t", k=8)
        Xb = av(0, 32768).rearrange("p (c r g) -> p c r g", c=256, r=2)
        w_in = av(32, 32768).rearrange("p (k c) -> p k c", k=8)
        mixT = av(32, 32768).rearrange("p (k t) -> p k t", k=8)
        QT = av(64, 16384).rearrange("p (h t) -> p h t", h=4)
        KT = av(80, 16384).rearrange("p (h t) -> p h t", h=4)
        Vev = av(96, 16 * 8 * 65 * 2).rearrange("p (i h d) -> p i h d", i=16, h=8)
        Vod = av(112.5, 15 * 8 * 65 * 2).rearrange("p (i h d) -> p i h d", i=15, h=8)
        Utm = av(128, 16384).rearrange("p (a g s h) -> p a g s h", a=2, g=32, s=8)
        ytm = av(144, 16384).rearrange("p (i c) -> p i c", i=16)
        U8 = av(64, 16384).rearrange("p (g c) -> p g c", g=32)
        Sbufs = [av(80, 16384, F32).rearrange("p (c r g) -> p c r g", c=64, r=2),
                 av(96, 16384, F32).rearrange("p (c r g) -> p c r g", c=64, r=2)]
        Y8 = av(96, 16384).rearrange("p (g c) -> p g c", g=32)
        toep = av(112, 8192).rearrange("p (g c) -> p g c", g=32)
        w_out = av(0, 16384).rearrange("p (k c) -> p k c", k=8)
        h2T = av(16, 8192).rearrange("p (k t) -> p k t", k=8)
        hid = av(64, NFC * 512 * 2).rearrange("p (f t) -> p f t", f=NFC)
        x1 = av(86, 4 * 1024 * 4, F32).rearrange("p (t c) -> p t c", t=4)
        NWB = 4
        wgs = [av(102 + 4 * i, 2048).rearrange("p (k j) -> p k j", k=8) for i in range(NWB)]
        wus = [av(102 + 4 * i + 2, 2048).rearrange("p (k j) -> p k j", k=8) for i in range(NWB)]
        wds = [av(118 + 2 * i, 2048) for i in range(NWB)]
        gmb = av(24, 4096, F32)
        gfb = av(28, 4096, F32)
        nfb = av(156, 4096, F32)

        x_tiles = x_d

        t_statc = [Tok() for _ in range(8)]

        def rms_rstd(E, src_ap, ncol, col, reads, tokw):
            B.op(ACT, lambda e: e.activation(out=junk[:, 0:ncol], in_=src_ap, func=AF.Square,
                                             accum_out=ssq[:, col:col + 1]),
                 reads=list(reads) + [], writes=[t_statc[col], t_junk])
            B.op(ACT, lambda e: e.activation(out=rstd[:, col:col + 1], in_=ssq[:, col:col + 1], func=AF.Sqrt,
                                             bias=epst[:], scale=1.0 / ncol), reads=[t_statc[col], t_const], writes=[t_statc[col]])
            B.op(DVE, lambda e: e.reciprocal(out=rstd[:, col:col + 1], in_=rstd[:, col:col + 1]),
                 reads=[t_statc[col]], writes=[t_statc[col]])
            return t_statc[col]

        for n in range(NSEQ if 'MAIN' not in SKIP else 0):
            B.barrier()
            xtile = [av(144 + 4 * i, 4096, F32) for i in range(2)]
            xnb = [av(152 + 2 * i, 2048) for i in range(2)]
            xtok = [Tok(), Tok()]
            xntok = [Tok(), Tok()]
            t_hT = [Tok() for _ in range(16)]
            t_win = Tok()
            B.dma(SP, w_in, winb[:, :, :], writes=[t_win])
            for tt in range(16):
                i = tt % 2
                B.dma(SP, xtile[i], x_tiles[n, tt * 128:(tt + 1) * 128, :], writes=[xtok[i]])
                tsc = rms_rstd(ACT, xtile[i], 1024, i, [xtok[i]], None)
                B.op(DVE, lambda e, i=i: e.tensor_scalar(out=xnb[i], in0=xtile[i], scalar1=rstd[:, i:i + 1], scalar2=None,
                                                       op0=ALU.mult), reads=[xtok[i], tsc], writes=[xntok[i]])
                bank, bt = next_bank()
                pb = psbf(bank)
                for kt in range(8):
                    B.op(PE, lambda e, i=i, kt=kt, pb=pb: e.transpose(
                        out=pb[:, kt * 128:(kt + 1) * 128], in_=xnb[i][:, kt * 128:(kt + 1) * 128], identity=identb[:]),
                        reads=[xntok[i], t_const], writes=[bt])
                for kt in range(8):
                    if kt % 2 == 0:
                        B.op(DVE, lambda e, kt=kt, pb=pb, tt=tt: e.tensor_scalar(
                            out=hT[:, kt, tt * 128:(tt + 1) * 128], in0=pb[:, kt * 128:(kt + 1) * 128],
                            scalar1=sclmix[:, kt, n:n + 1], scalar2=modfm[:, kt, n:n + 1], op0=ALU.mult, op1=ALU.add),
                            reads=[bt, t_mod], writes=[t_hT[tt], bt])
                    else:
                        B.op(ACT, lambda e, kt=kt, pb=pb, tt=tt: e.activation(
                            out=hT[:, kt, tt * 128:(tt + 1) * 128], in_=pb[:, kt * 128:(kt + 1) * 128],
                            func=AF.Identity, scale=sclmix[:, kt, n:n + 1], bias=modfm[:, kt, n:n + 1]),
                            reads=[bt, t_mod], writes=[t_hT[tt], bt])
            t_Q = Tok(); t_K = Tok(); t_V = Tok(); t_U = Tok()
            flip = [0]

            def evac(out_ap, in_ap, reads, writes, scale=None):
                flip[0] ^= 1
                if flip[0]:
                    if scale is None:
                        B.op(DVE, lambda e: e.tensor_copy(out=out_ap, in_=in_ap), reads=reads, writes=writes)
                    else:
                        B.op(DVE, lambda e: e.tensor_scalar(out=out_ap, in0=in_ap, scalar1=scale, scalar2=None,
                                                          op0=ALU.mult), reads=reads, writes=writes)
                else:
                    if scale is None:
                        B.op(ACT, lambda e: e.copy(out=out_ap, in_=in_ap), reads=reads, writes=writes)
                    else:
                        B.op(ACT, lambda e: e.mul(out=out_ap, in_=in_ap, mul=scale), reads=reads, writes=writes)

            for (dst, cbase, tk, scl) in ((QT, 512, t_Q, 0.125), (KT, 1024, t_K, None)):
                for hp in range(4):
                    for tb in range(4):
                        bank, bt = next_bank()
                        for kt in range(8):
                            B.op(PE, lambda e, kt=kt, hp=hp, tb=tb, bank=bank, cbase=cbase: e.matmul(
                                bank[:, :], lhsT=w_in[:, kt, cbase + hp * 128:cbase + (hp + 1) * 128],
                                rhs=hT[:, kt, tb * 512:(tb + 1) * 512], start=(kt == 0), stop=(kt == 7)),
                                reads=[t_win] + t_hT[tb * 4:tb * 4 + 4], writes=[bt])
                        evac(dst[:, hp, tb * 512:(tb + 1) * 512], bank[:, :], [bt], [tk, bt], scale=scl)
            B.op(POOL, lambda e: e.memset(Vev[:, :, :, 64:65], 1.0), writes=[t_V])
            B.op(POOL, lambda e: e.memset(Vod[:, :, :, 64:65], 1.0), writes=[t_V])
            for (dst, ntile, tok0) in ((Vev, 16, 0), (Vod, 15, 64)):
                for i in range(ntile):
                    t0 = tok0 + i * 128
                    bank, bt = next_bank()
                    for kt in range(8):
                        B.op(PE, lambda e, kt=kt, t0=t0, bank=bank: e.matmul(
                            bank[:, :], lhsT=hT[:, kt, t0:t0 + 128], rhs=w_in[:, kt, 1536:2048],
                            start=(kt == 0), stop=(kt == 7)),
                            reads=[t_win] + t_hT[t0 // 128:(t0 + 127) // 128 + 1], writes=[bt])
                    evac(dst[:, i, :, 0:64], bank[:, :].rearrange("p (h d) -> p h d", h=8), [bt], [t_V, bt])
            for a in range(2):
                for s in range(8):
                    bank, bt = next_bank()
                    for kt in range(8):
                        hsl = hT[:, kt, a * 1024:(a + 1) * 1024].rearrange("p (c s) -> p c s", s=8)[:, :, s]
                        B.op(PE, lambda e, kt=kt, hsl=hsl, bank=bank: e.matmul(
                            bank[:, :], lhsT=hsl, rhs=w_in[:, kt, 0:512], start=(kt == 0), stop=(kt == 7)),
                            reads=[t_win] + t_hT[a * 8:a * 8 + 8], writes=[bt])
                    evac(Utm[:, a, :, s, :], bank[:, :].rearrange("p (g h) -> p g h", g=32), [bt], [t_U, bt])

            B.barrier()
            if debug and n == 0:
                B.dma(SP, dbg['d_hT'][:, :], av(0, 32768)); B.dma(SP, dbg['d_QT'][:, :], av(64, 16384)); B.dma(SP, dbg['d_KT'][:, :], av(80, 16384))
                B.dma(SP, dbg['d_Utm'][:, :], av(128, 16384)); B.dma(SP, dbg['d_Vev'][:, :], av(96, 16 * 8 * 65 * 2))
                B.dma(SP, dbg['d_mod'][:, :], modfm[:].rearrange("p c n -> p (c n)")); B.dma(SP, dbg['d_T2'][:, :], T2[:].rearrange("p h c -> p (h c)"))
                B.barrier()
            P3ON = 'P3' not in SKIP
            sct = [av(0 + 1 * i, 1024, F32).rearrange("p (t q) -> p t q", t=4) for i in range(4)]
            ptt = [av(4 + 0.5 * i, 512).rearrange("p (t q) -> p t q", t=4) for i in range(4)]
            sctok = [Tok() for _ in range(4)]
            pttok = [Tok() for _ in range(4)]
            ynb = av(8, 1024)
            t_yn = Tok()
            t_ytm = [Tok() for _ in range(16)]
            t_mix = [Tok() for _ in range(16)]
            pstate["n"] = 4
            pstate["i"] = 0
            NJ = 4
            LA = 2
            iters = []
            for i in range(16 if P3ON else 0):
                for hg in range(2):
                    for hl in range(4):
                        for a in range(2):
                            iters.append((i, hg, hl, a))
            sc_state = {}

            def stage_scores(idx):
                i, hg, hl, a = iters[idx]
                h = hg * 4 + hl
                hp, hb = h // 2, (h % 2) * 64
                r = 2 * i + a
                rs = min(max(r - 4, 0), 24)
                e0 = rs - r + 7
                j = idx % NJ
                sbank, sbt = next_bank()
                for t in range(4):
                    B.op(PE, lambda e, t=t: e.matmul(
                        sbank[:, t * 64:(t + 1) * 64],
                        lhsT=KT[hb:hb + 64, hp, (rs + 2 * t) * 64:(rs + 2 * t) * 64 + 128],
                        rhs=QT[hb:hb + 64, hp, r * 64:(r + 1) * 64], start=True, stop=True),
                        reads=[t_Q, t_K], writes=[sbt])
                t2v = T2[:, h, :].rearrange("p (e q) -> p e q", e=16)[:, e0:e0 + 7:2, :]
                B.op(DVE, lambda e: e.tensor_tensor(
                    out=sct[j], in0=sbank[:, 0:256].rearrange("p (t q) -> p t q", t=4), in1=t2v, op=ALU.add),
                    reads=[sbt, t_T2], writes=[sctok[j], sbt])
                B.op(ACT, lambda e: e.activation(out=ptt[j], in_=sct[j], func=AF.Exp),
                     reads=[sctok[j]], writes=[pttok[j]])

            def stage_pv(idx):
                i, hg, hl, a = iters[idx]
                h = hg * 4 + hl
                r = 2 * i + a
                rs = min(max(r - 4, 0), 24)
                Vt, vb = (Vev, rs // 2) if rs % 2 == 0 else (Vod, (rs - 1) // 2)
                j = idx % NJ
                bi = 4 + (i % 2) * 2 + hg
                pbank, pbt = psum[bi], ptok[bi]
                for t in range(4):
                    B.op(PE, lambda e, t=t: e.matmul(
                        pbank[a * 64:(a + 1) * 64, hl * 65:(hl + 1) * 65], lhsT=ptt[j][:, t, :],
                        rhs=Vt[:, vb + t, h, :], start=(t == 0), stop=(t == 3)),
                        reads=[pttok[j], t_V], writes=[pbt])
                if hg == 1 and hl == 3 and a == 1:
                    finish_rowpair(i)

            def finish_rowpair(i):
                for hg in range(2):
                    bi = 4 + (i % 2) * 2 + hg
                    pbank, pbt = psum[bi], ptok[bi]
                    pv = pbank[:, 0:260].rearrange("p (h d) -> p h d", h=4)
                    B.op(DVE, lambda e: e.reciprocal(out=rec[:, hg * 4:(hg + 1) * 4], in_=pv[:, :, 64]),
                         reads=[pbt], writes=[t_stat])
                    B.op(DVE, lambda e: e.tensor_tensor(
                        out=ytm[:, i, hg * 256:(hg + 1) * 256].rearrange("p (h d) -> p h d", h=4), in0=pv[:, :, 0:64],
                        in1=bcast_last(rec[:, hg * 4:(hg + 1) * 4], 64), op=ALU.mult),
                        reads=[pbt, t_stat], writes=[t_ytm[i], pbt])
                cc = 2 + (i % 2)
                tsc = rms_rstd(ACT, ytm[:, i, :], 512, cc, [t_ytm[i]], None)
                B.op(DVE, lambda e: e.tensor_scalar(out=ynb[:, 0:512], in0=ytm[:, i, :], scalar1=rstd[:, cc:cc + 1],
                                                  scalar2=None, op0=ALU.mult),
                     reads=[t_ytm[i], tsc], writes=[t_yn])
                bank, bt = next_bank()
                pb = psbf(bank)
                for jj in range(4):
                    B.op(PE, lambda e, jj=jj: e.transpose(out=pb[:, jj * 128:(jj + 1) * 128],
                                                          in_=ynb[:, jj * 128:(jj + 1) * 128], identity=identb[:]),
                         reads=[t_yn, t_const], writes=[bt])
                for jj in range(4):
                    B.op(ACT, lambda e, jj=jj: e.activation(
                        out=mixT[:, 4 + jj, i * 128:(i + 1) * 128], in_=pb[:, jj * 128:(jj + 1) * 128],
                        func=AF.Identity, scale=natt[:, jj:jj + 1]), reads=[bt, t_const], writes=[t_mix[i], bt])

            for idx in range(len(iters) + LA):
                if idx < len(iters):
                    stage_scores(idx)
                if idx >= LA and idx - LA < len(iters):
                    stage_pv(idx - LA)
                if n == 0 and idx % 3 == 0:
                    next(ffn_conv, None)
            if n == 0:
                for _ in ffn_conv:
                    pass

            pstate["n"] = 8
            B.barrier()
            if debug and n == 0:
                B.dma(SP, dbg['d_yatt'][:, :], av(144, 16384))
                B.barrier()
            t_U8 = Tok(); t_X = Tok(); t_Y8 = Tok(); t_toep = Tok(); t_xs = [Tok(), Tok()]
            B.dma(SP, toep, toepb[:, :, :], writes=[t_toep])
            winT = av(128, 16384).rearrange("p (g r c) -> p g r c", g=32, r=2)
            woutS = av(80, 16384).rearrange("p (g r c) -> p g r c", g=32, r=2)
            t_wi = Tok(); t_wo2 = Tok()
            for g in range(32):
                bank, bt = next_bank()
                pb = psbf(bank)
                for a in range(2):
                    B.op(PE, lambda e, g=g, a=a, pb=pb: e.transpose(
                        out=pb[:, a * 128:(a + 1) * 128], in_=Utm[:, a, g, :, :].rearrange("p s h -> p (s h)"), identity=identb[:]),
                        reads=[t_U, t_const], writes=[bt])
                evac(U8[:, g, :], pb[:, 0:256], [bt], [t_U8, bt])
            B.dma(SP, winT.rearrange("p g r c -> p (g r c)"), winTb[:, :], writes=[t_wi, t_U])
            B.op(DVE, lambda e: e.memset(xstate[:], 0.0), writes=t_xs)
            B.op(DVE, lambda e: e.memset(Xb[0:64, 0], 0.0), writes=[t_X])
            B.op(POOL, lambda e: e.memset(Xb[64:128, 255], 0.0), writes=[t_X])
            halves = ((0, 128, DVE, 0),)

            def u8rev(g, cb):
                t = U8[:, g, cb * 64:(cb + 1) * 64]
                return AP(t.tensor, t.offset + 63, [list(t.ap[0]), [-1, 64]])
            t_Sb = [Tok(), Tok()]
            NOWAIT = 'SCANWAIT' not in SKIP
            for jb in range(4):
                Sblk = Sbufs[jb % 2]
                t_S = [t_Sb[jb % 2], t_Sb[jb % 2]]
                for gq in range(8):
                    bank, bt = next_bank()
                    for gl in range(4):
                        g = gq * 4 + gl
                        for ri in range(2):
                            col = (gl * 2 + ri) * 64
                            B.op(PE, lambda e, g=g, ri=ri, col=col, bank=bank: e.matmul(
                                bank[0:64, col:col + 64], lhsT=winT[:, g, ri, 0:64],
                                rhs=U8[:, g, jb * 64:(jb + 1) * 64], start=True, stop=True),
                                reads=[t_U8, t_wi], writes=[bt])
                            B.op(PE, lambda e, g=g, ri=ri, col=col, bank=bank: e.matmul(
                                bank[64:128, col:col + 64], lhsT=winT[:, g, ri, 64:128],
                                rhs=u8rev(g, 3 - jb), start=True, stop=True),
                                reads=[t_U8, t_wi], writes=[bt])
                    src = bank[:, :].rearrange("p (g r c) -> p g r c", g=4, r=2)
                    dst = Sblk[:, :, :, gq * 4:(gq + 1) * 4].rearrange("p c r g -> p g r c")
                    B.op(ACT, lambda e, src=src, dst=dst: e.copy(out=dst, in_=src), reads=[bt], writes=[t_S[0], t_S[1], bt])
                if NOWAIT:
                    B.op(DVE, lambda e: e.tensor_copy(out=st1[:, :], in_=st1[:, :]), reads=[t_S[0], t_xs[0], t_ssmw], writes=[t_xs[0]])
                    DVE.same_wait = False
                for stp in range(64 if 'SCAN' not in SKIP else 0):
                    for (lo, hi, E, d) in halves:
                        c = stp
                        if stp == 0:
                            prev = xstate[lo:hi, :]
                        else:
                            prev = Sblk[lo:hi, c - 1].rearrange("p r g -> p (r g)")
                        pv = prev
                        psw = AP(pv.tensor, pv.offset + 32, [list(pv.ap[0]), [-32, 2], [1, 32]])
                        cur = Sblk[lo:hi, c].rearrange("p r g -> p (r g)")
                        rd = [t_S[d], t_xs[d], t_ssmw]
                        B.op(E, lambda e, lo=lo, hi=hi, pv=pv: e.tensor_tensor(
                            out=st1[lo:hi, :], in0=A1[lo:hi, :], in1=pv, op=ALU.mult), reads=rd, writes=[t_xs[d]])
                        B.op(E, lambda e, lo=lo, hi=hi, psw=psw: e.tensor_tensor(
                            out=st2[lo:hi, :].rearrange("p (r g) -> p r g", r=2),
                            in0=A2[lo:hi, :].rearrange("p (r g) -> p r g", r=2), in1=psw, op=ALU.mult),
                            reads=rd, writes=[t_xs[d]])
                        B.op(E, lambda e, lo=lo, hi=hi: e.tensor_tensor(
                            out=st1[lo:hi, :], in0=st1[lo:hi, :], in1=st2[lo:hi, :], op=ALU.add),
                            reads=[t_xs[d]], writes=[t_xs[d]])
                        B.op(E, lambda e, lo=lo, hi=hi, cur=cur: e.tensor_tensor(
                            out=cur, in0=st1[lo:hi, :], in1=cur, op=ALU.add), reads=[t_xs[d]], writes=[t_S[d], t_xs[d]])
                DVE.same_wait = SAME_WAIT
                cb = jb
                nn = 64 if cb < 3 else 63
                B.op(ACT, lambda e: e.copy(out=Xb[0:64, cb * 64 + 1:cb * 64 + 1 + nn], in_=Sblk[0:64, 0:nn]),
                     reads=[t_S[0]], writes=[t_X])
                cbw = 3 - jb
                nb = 64 if cbw > 0 else 63
                xb2 = av(0, 32768)[64:128, :]
                xrev = AP(xb2.tensor, xb2.offset + (cbw * 64 + 62) * 64, [list(xb2.ap[0]), [-64, nb], [1, 64]])
                B.op(DVE, lambda e: e.tensor_copy(out=xrev, in_=Sblk[64:128, 0:nb].rearrange("p c r g -> p c (r g)")),
                     reads=[t_S[0]], writes=[t_X])
                B.op(DVE, lambda e: e.tensor_copy(out=xstate[:, :], in_=Sblk[:, 63].rearrange("p r g -> p (r g)")),
                     reads=[t_S[0]], writes=[t_xs[0]])
            B.dma(SP, woutS.rearrange("p g r c -> p (g r c)"), woutSb[:, :], writes=[t_wo2, t_Sb[0]])
            for g in range(32):
                bank, bt = next_bank()
                B.op(PE, lambda e, g=g, bank=bank: e.matmul(bank[:, 0:256], lhsT=toep[:, g, :], rhs=U8[:, g, :],
                                                            start=True, stop=False), reads=[t_toep, t_U8], writes=[bt])
                for ri in range(2):
                    B.op(PE, lambda e, g=g, ri=ri, bank=bank: e.matmul(
                        bank[:, 0:256], lhsT=woutS[:, g, ri, :], rhs=Xb[:, :, ri, g], start=False, stop=(ri == 1)),
                        reads=[t_wo2, t_X], writes=[bt])
                evac(Y8[:, g, :], bank[:, 0:256], [bt], [t_Y8, t_Sb[1], bt])
            ytv = ytm.rearrange("p (a s) c -> p a s c", a=2)
            t_ysm = Tok()
            for g in range(32):
                bank, bt = next_bank()
                pb = psbf(bank)
                for a in range(2):
                    B.op(PE, lambda e, g=g, a=a, pb=pb: e.transpose(
                        out=pb[:, a * 128:(a + 1) * 128], in_=Y8[:, g, a * 128:(a + 1) * 128], identity=identb[:]),
                        reads=[t_Y8, t_const], writes=[bt])
                evac(ytv[:, :, :, g * 16:(g + 1) * 16], pb[:, 0:256].rearrange("p (a s h) -> p a s h", a=2, s=8),
                     [bt], [t_ysm, bt])
            B.barrier()
            if debug and n == 0:
                B.dma(SP, dbg['d_yssm'][:, :], av(144, 16384)); B.dma(SP, dbg['d_U8'][:, :], av(64, 16384))
                B.dma(SP, dbg['d_Xb'][:, :], av(0, 32768)); B.dma(SP, dbg['d_toep'][:, :], av(112, 8192)); B.dma(SP, dbg['d_Y8'][:, :], av(96, 16384))
                B.barrier()
            gsets = []
            for q in range(2):
                o = 64 + 11 * q
                gsets.append((av(o, 2048, F32), av(o + 2, 2048, F32), av(o + 4, 1024),
                              av(o + 5, 1024).rearrange("p (j t) -> p j t", j=4), av(o + 6, 2048, F32),
                              av(o + 8, 2048, F32), av(o + 10, 1024), Tok()))
            C0 = 1.5957691216057308
            for ti in range(16):
                a, s = ti // 8, ti % 8
                uu, ww, ygb, ygT, zz, y2, ynb2, tg = gsets[ti % 2]
                yv = ytm[:, ti, :]
                B.op(ACT, lambda e, yv=yv: e.activation(out=uu, in_=yv, func=AF.Square), reads=[t_ysm, t_toep], writes=[tg])
                B.op(DVE, lambda e: e.tensor_scalar(out=ww, in0=uu, scalar1=0.044715, scalar2=1.0, op0=ALU.mult, op1=ALU.add),
                     reads=[tg], writes=[tg])
                B.op(DVE, lambda e, yv=yv: e.tensor_tensor(out=ww, in0=ww, in1=yv, op=ALU.mult), reads=[tg, t_ysm], writes=[tg])
                B.op(ACT, lambda e: e.activation(out=uu, in_=ww, func=AF.Sigmoid, scale=C0), reads=[tg], writes=[tg])
                B.op(DVE, lambda e, yv=yv: e.tensor_tensor(out=ygb, in0=uu, in1=yv, op=ALU.mult), reads=[tg, t_ysm], writes=[tg])
                bank, bt = next_bank()
                pb = psbf(bank)
                for jj in range(4):
                    B.op(PE, lambda e, jj=jj, pb=pb: e.transpose(out=pb[:, jj * 128:(jj + 1) * 128],
                                                                 in_=ygb[:, jj * 128:(jj + 1) * 128], identity=identb[:]),
                         reads=[tg, t_const], writes=[bt])
                B.op(ACT, lambda e, pb=pb: e.copy(out=ygT, in_=pb[:, 0:512].rearrange("p (j t) -> p j t", j=4)),
                     reads=[bt], writes=[tg, bt])
                zb, zt = next_bank()
                for jj in range(4):
                    B.op(PE, lambda e, jj=jj, zb=zb: e.matmul(zb[:, :], lhsT=ygT[:, jj, :], rhs=wglu[:, jj, :],
                                                              start=(jj == 0), stop=(jj == 3)), reads=[tg, t_ssmw], writes=[zt])
                B.op(DVE, lambda e, zb=zb: e.tensor_tensor(out=zz, in0=zb[:, :], in1=bglu_sb[:], op=ALU.add),
                     reads=[zt, t_const], writes=[tg, zt])
                B.op(ACT, lambda e: e.activation(out=zz, in_=zz, func=AF.Sigmoid), reads=[tg], writes=[tg])
                B.op(DVE, lambda e: e.tensor_tensor(out=y2, in0=zz, in1=ygb, op=ALU.mult), reads=[tg], writes=[tg])
                cc = 4 + (ti % 2)
                tsc = rms_rstd(ACT, y2, 512, cc, [tg], None)
                B.op(DVE, lambda e: e.tensor_scalar(out=ynb2[:, 0:512], in0=y2, scalar1=rstd[:, cc:cc + 1], scalar2=None,
                                                  op0=ALU.mult), reads=[tg, tsc], writes=[tg])
                bank, bt = next_bank()
                pb = psbf(bank)
                for jj in range(4):
                    B.op(PE, lambda e, jj=jj, pb=pb: e.transpose(out=pb[:, jj * 128:(jj + 1) * 128],
                                                                 in_=ynb2[:, jj * 128:(jj + 1) * 128], identity=identb[:]),
                         reads=[tg, t_const], writes=[bt])
                for jj in range(4):
                    dstv = mixT[:, jj, a * 1024:(a + 1) * 1024].rearrange("p (c s) -> p c s", s=8)[:, :, s]
                    B.op(ACT, lambda e, jj=jj, pb=pb, dstv=dstv: e.activation(
                        out=dstv, in_=pb[:, jj * 128:(jj + 1) * 128], func=AF.Identity, scale=nssm[:, jj:jj + 1]),
                        reads=[bt, t_const], writes=[t_mix[0], bt])

            B.barrier()
            if debug and n == 0:
                B.dma(SP, dbg['d_mixT'][:, :], av(32, 32768))
                B.barrier()
            t_wo = Tok(); t_g = Tok()
            B.dma(SP, w_out, woutb[:, :, :], writes=[t_wo])
            for (dst, w) in ((gmb, 0), (gfb, 1)):
                src = gscr[n, w:w + 1, :]
                srcb = AP(src.tensor, src.offset, [[0, 128], [1, D]])
                B.dma(SP, dst, srcb, writes=[t_g])
            B.dma(SP, nfb, nfin_d[:, :], writes=[t_g])
            xl = [av(128 + 4 * i, 4096, F32) for i in range(2)]
            xltok = [Tok(), Tok()]
            xn5 = [av(136 + 2 * i, 2048) for i in range(2)]
            xn5tok = [Tok(), Tok()]
            sgt = [av(140 + 2 * i, 2048, F32) for i in range(2)]
            sgtok = [Tok(), Tok()]
            ot = [av(144 + 4 * i, 4096, F32) for i in range(2)]
            ottok = [Tok(), Tok()]
            tmpv = av(152, 4096, F32)
            t_tmp = Tok()
            wgtok = [Tok() for _ in range(NWB)]; wdtok = [Tok() for _ in range(NWB)]
            for tb in range(4 if 'P5' not in SKIP else 0):
                t_x1 = [Tok() for _ in range(4)]
                t_h2 = [Tok() for _ in range(4)]
                t_hid = Tok()
                for tt in range(4):
                    T = tb * 4 + tt
                    i = T % 2
                    B.dma(SP, xl[i], x_tiles[n, T * 128:(T + 1) * 128, :], writes=[xltok[i]])
                    for ob in range(2):
                        bank, bt = next_bank()
                        for kt in range(8):
                            B.op(PE, lambda e, kt=kt, T=T, ob=ob, bank=bank: e.matmul(
                                bank[:, :], lhsT=mixT[:, kt, T * 128:(T + 1) * 128],
                                rhs=w_out[:, kt, ob * 512:(ob + 1) * 512], start=(kt == 0), stop=(kt == 7)),
                                reads=[t_wo, t_mix[0]] + t_mix, writes=[bt])
                        B.op(DVE, lambda e, ob=ob, bank=bank: e.tensor_tensor(
                            out=tmpv[:, 0:512], in0=bank[:, :], in1=gmb[:, ob * 512:(ob + 1) * 512], op=ALU.mult),
                            reads=[bt, t_g], writes=[t_tmp, bt])
                        B.op(DVE, lambda e, ob=ob, tt=tt, i=i: e.tensor_tensor(
                            out=x1[:, tt, ob * 512:(ob + 1) * 512], in0=tmpv[:, 0:512],
                            in1=xl[i][:, ob * 512:(ob + 1) * 512], op=ALU.add),
                            reads=[t_tmp, xltok[i]], writes=[t_x1[tt]])
                    tsc = rms_rstd(ACT, x1[:, tt, :], 1024, i, [t_x1[tt]], None)
                    B.op(DVE, lambda e, tt=tt, i=i: e.tensor_scalar(out=xn5[i], in0=x1[:, tt, :], scalar1=rstd[:, i:i + 1],
                                                                  scalar2=None, op0=ALU.mult),
                         reads=[t_x1[tt], tsc], writes=[xn5tok[i]])
                    bank, bt = next_bank()
                    pb = psbf(bank)
                    for kt in range(8):
                        B.op(PE, lambda e, i=i, kt=kt, pb=pb: e.transpose(
                            out=pb[:, kt * 128:(kt + 1) * 128], in_=xn5[i][:, kt * 128:(kt + 1) * 128], identity=identb[:]),
                            reads=[xn5tok[i], t_const], writes=[bt])
                    for kt in range(8):
                        if kt % 2 == 0:
                            B.op(DVE, lambda e, kt=kt, pb=pb, tt=tt: e.tensor_scalar(
                                out=h2T[:, kt, tt * 128:(tt + 1) * 128], in0=pb[:, kt * 128:(kt + 1) * 128],
                                scalar1=sclffn[:, kt, n:n + 1], scalar2=modfm[:, 24 + kt, n:n + 1],
                                op0=ALU.mult, op1=ALU.add), reads=[bt, t_mod], writes=[t_h2[tt], bt])
                        else:
                            B.op(ACT, lambda e, kt=kt, pb=pb, tt=tt: e.activation(
                                out=h2T[:, kt, tt * 128:(tt + 1) * 128], in_=pb[:, kt * 128:(kt + 1) * 128],
                                func=AF.Identity, scale=sclffn[:, kt, n:n + 1], bias=modfm[:, 24 + kt, n:n + 1]),
                                reads=[bt, t_mod], writes=[t_h2[tt], bt])
                for fc in range(NFC):
                    i = fc % NWB
                    B.dma(SP, wgs[i], wgb[fc, :, :, :], writes=[wgtok[i]])
                    B.dma(SP, wus[i], wub[fc, :, :, :], writes=[wgtok[i]])
                    gb, gt = next_bank()
                    for kt in range(8):
                        B.op(PE, lambda e, i=i, kt=kt, gb=gb: e.matmul(gb[:, :], lhsT=wgs[i][:, kt, :], rhs=h2T[:, kt, :],
                                                                        start=(kt == 0), stop=(kt == 7)),
                             reads=[wgtok[i]] + t_h2, writes=[gt])
                    ub, ut = next_bank()
                    for kt in range(8):
                        B.op(PE, lambda e, i=i, kt=kt, ub=ub: e.matmul(ub[:, :], lhsT=wus[i][:, kt, :], rhs=h2T[:, kt, :],
                                                                        start=(kt == 0), stop=(kt == 7)),
                             reads=[wgtok[i]] + t_h2, writes=[ut])
                    si = fc % 2
                    B.op(ACT, lambda e, si=si, gb=gb: e.activation(out=sgt[si], in_=gb[:, :], func=AF.Silu),
                         reads=[gt], writes=[sgtok[si], gt])
                    B.op(DVE, lambda e, i=i, ub=ub, fc=fc: e.tensor_tensor(out=hid[:, fc, :], in0=sgt[si], in1=ub[:, :],
                                                                         op=ALU.mult),
                         reads=[sgtok[si], ut], writes=[t_hid, ut])
                for fc in range(NFC):
                    i = fc % NWB
                    B.dma(SP, wds[i], wdb[fc, :, :], writes=[wdtok[i]])
                    for tt in range(4):
                        for ob in range(2):
                            bi = tt * 2 + ob
                            B.op(PE, lambda e, i=i, fc=fc, tt=tt, ob=ob, bi=bi: e.matmul(
                                psum[bi][:, :], lhsT=hid[:, fc, tt * 128:(tt + 1) * 128],
                                rhs=wds[i][:, ob * 512:(ob + 1) * 512], start=(fc == 0), stop=(fc == NFC - 1)),
                                reads=[wdtok[i], t_hid], writes=[ptok[bi]])
                for tt in range(4):
                    T = tb * 4 + tt
                    for ob in range(2):
                        bi = tt * 2 + ob
                        B.op(DVE, lambda e, ob=ob, bi=bi: e.tensor_tensor(
                            out=tmpv[:, 0:512], in0=psum[bi][:, :], in1=gfb[:, ob * 512:(ob + 1) * 512], op=ALU.mult),
                            reads=[ptok[bi], t_g], writes=[t_tmp, ptok[bi]])
                        B.op(DVE, lambda e, ob=ob, tt=tt: e.tensor_tensor(
                            out=x1[:, tt, ob * 512:(ob + 1) * 512], in0=tmpv[:, 0:512],
                            in1=x1[:, tt, ob * 512:(ob + 1) * 512], op=ALU.add),
                            reads=[t_tmp], writes=[t_x1[tt]])
                    i = T % 2
                    tsc = rms_rstd(ACT, x1[:, tt, :], 1024, 6 + i, [t_x1[tt]], None)
                    B.op(DVE, lambda e, tt=tt, i=i: e.scalar_tensor_tensor(
                        out=ot[i], in0=x1[:, tt, :], scalar=rstd[:, 6 + i:7 + i], in1=nfb, op0=ALU.mult, op1=ALU.mult),
                        reads=[t_x1[tt], tsc, t_g], writes=[ottok[i]])
                    B.dma(SP, y_d[n, T * 128:(T + 1) * 128, :], ot[i], reads=[ottok[i]])
        B.finish()
    return nc


_bglu_holder = {}


def _layout_inputs(inp):
    f = np.float32
    out = {}
    xs = np.concatenate([inp["x_prompt"], inp["x_sample"]], axis=0)
    cs = np.concatenate([inp["c_prompt"], inp["c_sample"]], axis=0)
    shared = {}
    shared["w_ada"] = np.ascontiguousarray(inp["w_ada"][0], dtype=f)
    shared["b_ada"] = np.ascontiguousarray(inp["b_ada"][0:1], dtype=f)
    shared["nmix"] = np.ascontiguousarray(inp["norm_mix"][0].reshape(8, 128).T, dtype=f)
    shared["nffn"] = np.ascontiguousarray(inp["norm_ffn"][0].reshape(8, 128).T, dtype=f)
    shared["nfin"] = np.ascontiguousarray(np.broadcast_to(inp["norm_final"][None, :], (128, D)), dtype=f)
    shared["nssm"] = np.ascontiguousarray(inp["norm_ssm_out"][0].reshape(4, 128).T, dtype=f)
    shared["natt"] = np.ascontiguousarray(inp["norm_attn_out"][0].reshape(4, 128).T, dtype=f)
    shared["bglu"] = np.ascontiguousarray(np.broadcast_to(inp["b_glu"][0][None, :], (128, 512)), dtype=f)
    shared["w_in"] = np.ascontiguousarray(inp["w_in"][0], dtype=f)
    shared["w_glu"] = np.ascontiguousarray(inp["w_glu"][0], dtype=f)
    shared["w_out"] = np.ascontiguousarray(inp["w_out"][0], dtype=f)
    shared["w_g"] = np.ascontiguousarray(inp["w_ffn_gate"][0], dtype=f)
    shared["w_u"] = np.ascontiguousarray(inp["w_ffn_up"][0], dtype=f)
    shared["w_d"] = np.ascontiguousarray(inp["w_ffn_down"][0], dtype=f)

    def dpg(a):
        return np.ascontiguousarray(a.transpose(0, 2, 1).reshape(128, 32), dtype=f)

    shared["are"] = dpg(inp["ssm_a_re"][0])
    shared["aim"] = dpg(inp["ssm_a_im"][0])
    shared["ldt"] = dpg(np.broadcast_to(inp["ssm_log_dt"][0][:, :, None], (2, 32, 64)))
    shared["bre"] = np.ascontiguousarray(inp["ssm_b_re"][0].transpose(0, 2, 1, 3).reshape(128, 32, 16), dtype=f)
    shared["bim"] = np.ascontiguousarray(inp["ssm_b_im"][0].transpose(0, 2, 1, 3).reshape(128, 32, 16), dtype=f)
    shared["cre"] = np.ascontiguousarray(inp["ssm_c_re"][0].transpose(0, 3, 1, 2).reshape(128, 32, 16), dtype=f)
    shared["cim"] = np.ascontiguousarray(inp["ssm_c_im"][0].transpose(0, 3, 1, 2).reshape(128, 32, 16), dtype=f)
    dvec = inp["ssm_d"][0].reshape(32, 16)
    dd = np.zeros((16, 32, 16), f)
    for h in range(16):
        dd[h, :, h] = dvec[:, h]
    shared["dd"] = dd
    rpb = inp["na_rpb"][0]
    b = np.arange(2)[:, None, None, None]
    kc = np.arange(64)[None, :, None, None]
    e = np.arange(16)[None, None, :, None]
    qc = np.arange(64)[None, None, None, :]
    ri = np.clip(e + b, 0, 14) + 0 * kc + 0 * qc
    ci = np.clip(kc - qc + 15, 0, 30) + 0 * e + 0 * b
    t2raw = rpb[:, ri, ci].reshape(8, 128, 16 * 64)
    shared["t2raw"] = np.ascontiguousarray(t2raw, dtype=f)
    cstart = np.clip(qc - 8, 0, 48)
    valid = (kc >= cstart) & (kc < cstart + 16) & ((e + b) <= 14)
    m2 = np.where(valid, 0.0, NEG).astype(f) + np.zeros((2, 64, 16, 64), f)
    shared["m2"] = np.ascontiguousarray(m2.reshape(128, 16 * 64))
    shared["identf"] = np.eye(128, dtype=f)
    in_maps = []
    for core in range(8):
        m = dict(shared)
        m["x"] = np.ascontiguousarray(xs[core * 3:(core + 1) * 3], dtype=f)
        cc = cs[core * 3:(core + 1) * 3]
        cT = np.zeros((128, 8, 4), f)
        cT[:, :, 0:3] = cc.reshape(3, 8, 128).transpose(2, 1, 0)
        m["cT"] = cT
        in_maps.append(m)
    return in_maps


def kernel(**inputs):
    inp = {k: np.asarray(v) for k, v in inputs.items()}
    in_maps = _layout_inputs(inp)
    nc = build_nc()
    res = run_bass_kernel_spmd(nc, in_maps, core_ids=list(range(8)))
    ys = np.concatenate([np.asarray(r["y"]).reshape(NSEQ, L, D) for r in res.results], axis=0).astype(np.float32)
    return ys[0:8], ys[8:24]
```

```python
import math
import numpy as np
import ml_dtypes
import concourse.bass as bass
import concourse.mybir as mybir
from concourse.bass_utils import run_bass_kernel_spmd
from concourse.ap import AP

F32 = mybir.dt.float32
BF16 = mybir.dt.bfloat16
ALU = mybir.AluOpType
AF = mybir.ActivationFunctionType

D = 1024
L = 2048
NSEQ = 3
DFF = 2816
NFC = 22
EPS = 1e-6
NEG = -30000.0


class Tok:
    __slots__ = ("w", "r")

    def __init__(self):
        self.w = None
        self.r = {}


class EngW:
    def __init__(self, eng, sem, same_wait):
        self.eng = eng
        self.sem = sem
        self.n = 0
        self.waited = {}
        self.same_wait = same_wait


SAME_WAIT = True


class Builder:
    def __init__(self, nc, sems, dma_sems):
        self.nc = nc
        self.pe = EngW(nc.tensor, sems[0], False)
        self.act = EngW(nc.scalar, sems[1], SAME_WAIT)
        self.dve = EngW(nc.vector, sems[2], SAME_WAIT)
        self.pool = EngW(nc.gpsimd, sems[3], True)
        self.sp = EngW(nc.sync, sems[4], False)
        self.engs = [self.pe, self.act, self.dve, self.pool, self.sp]
        self.dma_sems = dma_sems
        self.dma_n = [0] * len(dma_sems)
        self.dma_rr = 0

    def _wait(self, E, deps):
        for (sem, val) in deps:
            if sem is E.sem and not E.same_wait:
                continue
            k = id(sem)
            if E.waited.get(k, 0) < val:
                E.eng.wait_ge(sem, val)
                E.waited[k] = val

    @staticmethod
    def _deps(reads, writes):
        deps = []
        for t in reads:
            if t.w is not None:
                deps.append(t.w)
        for t in writes:
            if t.w is not None:
                deps.append(t.w)
            for k, v in t.r.items():
                deps.append(v)
        return deps

    @staticmethod
    def _upd(me, reads, writes):
        for t in reads:
            k = id(me[0])
            if k not in t.r or t.r[k][1] < me[1]:
                t.r[k] = me
        for t in writes:
            t.w = me
            t.r = {}

    def op(self, E, fn, reads=(), writes=()):
        self._wait(E, self._deps(reads, writes))
        inst = fn(E.eng)
        E.n += 1
        inst.then_inc(E.sem, 1)
        self._upd((E.sem, E.n), reads, writes)

    def dma(self, Q, out, in_, reads=(), writes=(), **kw):
        k = self.dma_rr
        self.dma_rr = (self.dma_rr + 1) % len(self.dma_sems)
        sem = self.dma_sems[k]
        deps = self._deps(reads, writes)
        if self.dma_n[k] > 0:
            deps.append((sem, 16 * self.dma_n[k]))
        self._wait(Q, deps)
        inst = Q.eng.dma_start(out=out, in_=in_, **kw)
        self.dma_n[k] += 1
        inst.then_inc(sem, 16)
        self._upd((sem, 16 * self.dma_n[k]), reads, writes)

    def barrier(self):
        for E in self.engs:
            deps = [(F.sem, F.n) for F in self.engs if F is not E and F.n > 0]
            deps += [(s, 16 * n) for s, n in zip(self.dma_sems, self.dma_n) if n > 0]
            self._wait(E, deps)

    def finish(self):
        deps = [(s, 16 * n) for s, n in zip(self.dma_sems, self.dma_n) if n > 0]
        deps += [(F.sem, F.n) for F in self.engs if F.n > 0 and F is not self.sp]
        self._wait(self.sp, deps)


def bcast_last(ap, n):
    return AP(ap.tensor, ap.offset, [list(d) for d in ap.ap] + [[0, n]])


SKIP = set()


def build_nc(debug=False):
    nc = bass.Bass("TRN2", target_bir_lowering=False)
    dbg = {}
    def dout(name, shape, dt):
        dbg[name] = nc.dram_tensor(name, list(shape), dt, kind="ExternalOutput").ap()
        return dbg[name]
    if debug:
        dout('d_hT', [128, 8 * 2048], BF16); dout('d_QT', [128, 4 * 2048], BF16); dout('d_KT', [128, 4 * 2048], BF16)
        dout('d_Utm', [128, 8192], BF16); dout('d_yatt', [128, 8192], BF16); dout('d_mixT', [128, 8 * 2048], BF16)
        dout('d_yssm', [128, 8192], BF16); dout('d_Vev', [128, 16 * 8 * 65], BF16); dout('d_mod', [128, 192], F32)
        dout('d_U8', [128, 8192], BF16); dout('d_Xb', [128, 16384], BF16); dout('d_toep', [128, 4096], BF16)
        dout('d_Y8', [128, 8192], BF16); dout('d_T2', [128, 8192], BF16)

    def din(name, shape, dt=F32):
        return nc.dram_tensor(name, list(shape), dt, kind="ExternalInput").ap()

    def dscr(name, shape, dt):
        return nc.dram_tensor(name, list(shape), dt, kind="Internal").ap()

    x_d = din("x", [NSEQ, L, D])
    y_d = nc.dram_tensor("y", [NSEQ, L, D], F32, kind="ExternalOutput").ap()
    cT_d = din("cT", [128, 8, 4])
    wada_d = din("w_ada", [D, 6 * D])
    bada_d = din("b_ada", [1, 6 * D])
    nmix_d = din("nmix", [128, 8])
    nffn_d = din("nffn", [128, 8])
    nfin_d = din("nfin", [128, D])
    nssm_d = din("nssm", [128, 4])
    natt_d = din("natt", [128, 4])
    bglu_d = din("bglu", [128, 512])
    win_d = din("w_in", [D, 2048])
    wglu_d = din("w_glu", [512, 512])
    wout_d = din("w_out", [D, D])
    wg_d = din("w_g", [D, DFF])
    wu_d = din("w_u", [D, DFF])
    wd_d = din("w_d", [DFF, D])
    are_d = din("are", [128, 32])
    aim_d = din("aim", [128, 32])
    ldt_d = din("ldt", [128, 32])
    bre_d = din("bre", [128, 32, 16])
    bim_d = din("bim", [128, 32, 16])
    cre_d = din("cre", [128, 32, 16])
    cim_d = din("cim", [128, 32, 16])
    dd_d = din("dd", [16, 32, 16])
    t2raw_d = din("t2raw", [8, 128, 16 * 64])
    m2_d = din("m2", [128, 16 * 64])
    identf_d = din("identf", [128, 128])

    winb = dscr("winb", [128, 8, 2048], BF16)
    wglub = dscr("wglub", [128, 4, 512], BF16)
    woutb = dscr("woutb", [128, 8, 1024], BF16)
    wgb = dscr("wgb", [NFC, 128, 8, 128], BF16)
    wub = dscr("wub", [NFC, 128, 8, 128], BF16)
    wdb = dscr("wdb", [NFC, 128, 1024], BF16)
    toepb = dscr("toepb", [128, 32, 128], BF16)
    kallb = dscr("kallb", [16, 32, 240], BF16)
    gscr = dscr("gscr", [NSEQ, 2, D], F32)
    winTb = dscr("winTb", [128, 32 * 2 * 128], BF16)
    woutSb = dscr("woutSb", [128, 32 * 2 * 128], BF16)

    from contextlib import ExitStack

    with ExitStack() as es:
        def sb(name, shape, dt):
            return es.enter_context(nc.sbuf_tensor("sb_" + name, list(shape), dt))

        sems = [es.enter_context(nc.semaphore("e%d" % i)) for i in range(5)]
        dsems = [es.enter_context(nc.semaphore("d%d" % i)) for i in range(16)]
        B = Builder(nc, sems, dsems)
        PE, ACT, DVE, POOL, SP = B.pe, B.act, B.dve, B.pool, B.sp

        psum = [es.enter_context(nc.psum_tensor("ps%d" % i, [128, 512], F32)) for i in range(8)]
        ptok = [Tok() for _ in range(8)]
        pstate = {"i": 0, "n": 8}

        def next_bank():
            i = pstate["i"] % pstate["n"]
            pstate["i"] = (i + 1) % pstate["n"]
            return psum[i], ptok[i]

        def psbf(bank):
            return bank[:].bitcast(BF16)

        identf = sb("identf", [128, 128], F32)
        identb = sb("identb", [128, 128], BF16)
        ones1 = sb("ones1", [1, 8], F32)
        epst = sb("epst", [128, 1], F32)
        modfm = sb("modfm", [128, 48, 4], F32)
        sclmix = sb("sclmix", [128, 8, 4], F32)
        sclffn = sb("sclffn", [128, 8, 4], F32)
        nssm = sb("nssm", [128, 4], F32)
        natt = sb("natt", [128, 4], F32)
        A1 = sb("A1", [128, 64], F32)
        A2 = sb("A2", [128, 64], F32)
        T2 = sb("T2", [128, 8, 16 * 64], BF16)
        bglu_sb = sb("bglu_sb", [128, 512], F32)
        wglu = sb("wglu", [128, 4, 512], BF16)
        xstate = sb("xstate", [128, 64], F32)
        st1 = sb("st1", [128, 64], F32)
        st2 = sb("st2", [128, 64], F32)
        ssq = sb("ssq", [128, 8], F32)
        rstd = sb("rstd", [128, 8], F32)
        rec = sb("rec", [128, 8], F32)
        junk = sb("junk", [128, 1024], BF16)
        t_const = Tok()
        t_mod = Tok()
        t_ssmw = Tok()
        t_stat = Tok()
        t_junk = Tok()

        NA = 160 * 512
        arena = sb("arena", [128, NA], BF16)

        def av(off_kb, nbytes, dt=BF16):
            o = int(off_kb * 512)
            ap = arena[:, o:o + nbytes // 2]
            if dt is F32:
                ap = ap.bitcast(F32)
            return ap

        B.dma(SP, identf[:], identf_d[:, :], writes=[t_const])
        B.op(DVE, lambda e: e.tensor_copy(out=identb[:], in_=identf[:]), reads=[t_const], writes=[t_const])
        B.op(DVE, lambda e: e.memset(ones1[:], 1.0), writes=[t_const])
        B.op(DVE, lambda e: e.memset(epst[:], EPS), writes=[t_const])
        B.dma(SP, nssm[:], nssm_d[:, :], writes=[t_const])
        B.dma(SP, natt[:], natt_d[:, :], writes=[t_const])
        B.dma(SP, bglu_sb[:], bglu_d[:, :], writes=[t_const])
        winT = av(128, 16384).rearrange("p (g r c) -> p g r c", g=32, r=2)
        woutS = av(144, 16384).rearrange("p (g r c) -> p g r c", g=32, r=2)

        def f32tile(off_kb, shape):
            n = int(np.prod(shape))
            ap = av(off_kb, n * 4, F32)
            if len(shape) == 2:
                return ap.rearrange("p (a b) -> p a b", a=shape[0]) if False else ap
            return ap

        ts = Tok()
        off = [0.0]

        def alloc(ncols):
            o = off[0]
            off[0] += ncols * 4 / 1024.0
            return av(o, ncols * 4, F32)

        are = alloc(32); aim = alloc(32); ldt = alloc(32)
        lr = alloc(32); dtt = alloc(32); er = alloc(32); ph2 = alloc(64); k2 = alloc(64); sc2 = alloc(64)
        ar = alloc(32); ai = alloc(32); nr = alloc(32); den = alloc(32); zr = alloc(32); zi = alloc(32)
        tA = alloc(32); tB = alloc(32); a8r = alloc(32); a8i = alloc(32)
        B.dma(SP, are, are_d[:, :], writes=[ts])
        B.dma(SP, aim, aim_d[:, :], writes=[ts])
        B.dma(SP, ldt, ldt_d[:, :], writes=[ts])

        def V(fn):
            B.op(DVE, fn, reads=[ts], writes=[ts])

        def A(fn):
            B.op(ACT, fn, reads=[ts], writes=[ts])

        V(lambda e: e.tensor_scalar(out=lr, in0=are, scalar1=-1e-4, scalar2=None, op0=ALU.min))
        A(lambda e: e.activation(out=dtt, in_=ldt, func=AF.Exp))
        V(lambda e: e.tensor_tensor(out=er, in0=lr, in1=dtt, op=ALU.mult))
        A(lambda e: e.activation(out=er, in_=er, func=AF.Exp))
        V(lambda e: e.tensor_tensor(out=ph2[:, 0:32], in0=aim, in1=dtt, op=ALU.mult))
        V(lambda e: e.tensor_scalar(out=ph2[:, 32:64], in0=ph2[:, 0:32], scalar1=math.pi / 2, scalar2=None, op0=ALU.add))
        MAGIC = 12582912.0
        V(lambda e: e.tensor_scalar(out=k2, in0=ph2, scalar1=1.0 / (2 * math.pi), scalar2=MAGIC, op0=ALU.mult, op1=ALU.add))
        V(lambda e: e.tensor_scalar(out=k2, in0=k2, scalar1=-MAGIC, scalar2=None, op0=ALU.add))
        V(lambda e: e.scalar_tensor_tensor(out=ph2, in0=k2, scalar=-2 * math.pi, in1=ph2, op0=ALU.mult, op1=ALU.add))
        V(lambda e: e.tensor_scalar(out=ph2, in0=ph2, scalar1=3.14159, scalar2=-3.14159, op0=ALU.min, op1=ALU.max))
        A(lambda e: e.activation(out=sc2, in_=ph2, func=AF.Sin))
        V(lambda e: e.tensor_tensor(out=ai, in0=er, in1=sc2[:, 0:32], op=ALU.mult))
        V(lambda e: e.tensor_tensor(out=ar, in0=er, in1=sc2[:, 32:64], op=ALU.mult))
        V(lambda e: e.tensor_scalar(out=nr, in0=ar, scalar1=-1.0, scalar2=None, op0=ALU.add))
        V(lambda e: e.tensor_tensor(out=den, in0=lr, in1=lr, op=ALU.mult))
        V(lambda e: e.tensor_tensor(out=tA, in0=aim, in1=aim, op=ALU.mult))
        V(lambda e: e.tensor_tensor(out=den, in0=den, in1=tA, op=ALU.add))
        V(lambda e: e.reciprocal(out=den, in_=den))
        V(lambda e: e.tensor_tensor(out=tA, in0=nr, in1=lr, op=ALU.mult))
        V(lambda e: e.tensor_tensor(out=tB, in0=ai, in1=aim, op=ALU.mult))
        V(lambda e: e.tensor_tensor(out=tA, in0=tA, in1=tB, op=ALU.add))
        V(lambda e: e.tensor_tensor(out=zr, in0=tA, in1=den, op=ALU.mult))
        V(lambda e: e.tensor_tensor(out=tA, in0=ai, in1=lr, op=ALU.mult))
        V(lambda e: e.tensor_tensor(out=tB, in0=nr, in1=aim, op=ALU.mult))
        V(lambda e: e.tensor_tensor(out=tA, in0=tA, in1=tB, op=ALU.subtract))
        V(lambda e: e.tensor_tensor(out=zi, in0=tA, in1=den, op=ALU.mult))
        V(lambda e: e.tensor_copy(out=a8r, in_=ar))
        V(lambda e: e.tensor_copy(out=a8i, in_=ai))
        for _ in range(3):
            V(lambda e: e.tensor_tensor(out=tA, in0=a8r, in1=a8r, op=ALU.mult))
            V(lambda e: e.tensor_tensor(out=tB, in0=a8i, in1=a8i, op=ALU.mult))
            V(lambda e: e.tensor_tensor(out=a8i, in0=a8r, in1=a8i, op=ALU.mult))
            V(lambda e: e.tensor_tensor(out=a8r, in0=tA, in1=tB, op=ALU.subtract))
            V(lambda e: e.tensor_scalar(out=a8i, in0=a8i, scalar1=2.0, scalar2=None, op0=ALU.mult))
        B.op(DVE, lambda e: e.tensor_copy(out=A1[:, 0:32], in_=a8r), reads=[ts], writes=[t_ssmw])
        B.op(DVE, lambda e: e.tensor_copy(out=A1[:, 32:64], in_=a8r), reads=[ts], writes=[t_ssmw])
        B.op(DVE, lambda e: e.tensor_scalar(out=A2[:, 0:32], in0=a8i, scalar1=-1.0, scalar2=None, op0=ALU.mult), reads=[ts], writes=[t_ssmw])
        B.op(DVE, lambda e: e.tensor_copy(out=A2[:, 32:64], in_=a8i), reads=[ts], writes=[t_ssmw])

        def alloc3(nk):
            return alloc(nk * 512).rearrange("p (k g h) -> p k g h", k=nk, g=32)

        TR = alloc3(8); TI_off = off[0]; TI = alloc3(8); GR = alloc3(9); GI = alloc3(9)
        braw = alloc(512).rearrange("p (g h) -> p g h", g=32)
        biraw = alloc(512).rearrange("p (g h) -> p g h", g=32)
        tmp1 = alloc(512).rearrange("p (g h) -> p g h", g=32)
        tmp2 = alloc(512).rearrange("p (g h) -> p g h", g=32)
        B.dma(SP, braw, bre_d[:, :, :], writes=[ts])
        B.dma(SP, biraw, bim_d[:, :, :], writes=[ts])
        B.dma(SP, GR[:, 0], cre_d[:, :, :], writes=[ts])
        B.dma(SP, GI[:, 0], cim_d[:, :, :], writes=[ts])
        zrb = bcast_last(zr, 16); zib = bcast_last(zi, 16)
        arb = bcast_last(ar, 16); aib = bcast_last(ai, 16)

        def cmul(outR, outI, inR, inI, sR, sI):
            V(lambda e: e.tensor_tensor(out=tmp1, in0=inR, in1=sR, op=ALU.mult))
            V(lambda e: e.tensor_tensor(out=tmp2, in0=inI, in1=sI, op=ALU.mult))
            V(lambda e: e.tensor_tensor(out=outR, in0=tmp1, in1=tmp2, op=ALU.subtract))
            V(lambda e: e.tensor_tensor(out=tmp1, in0=inR, in1=sI, op=ALU.mult))
            V(lambda e: e.tensor_tensor(out=tmp2, in0=inI, in1=sR, op=ALU.mult))
            V(lambda e: e.tensor_tensor(out=outI, in0=tmp1, in1=tmp2, op=ALU.add))

        cmul(TR[:, 0], TI[:, 0], braw, biraw, zrb, zib)
        for k in range(1, 8):
            cmul(TR[:, k], TI[:, k], TR[:, k - 1], TI[:, k - 1], arb, aib)
        for k in range(1, 9):
            cmul(GR[:, k], GI[:, k], GR[:, k - 1], GI[:, k - 1], arb, aib)

        for t in range(8):
            for (lo, hi, k) in ((0, 64, t + 1), (64, 128, 8 - t)):
                B.op(DVE, lambda e, lo=lo, hi=hi, k=k, t=t: e.tensor_copy(
                    out=woutS[lo:hi, :, 0, t * 16:(t + 1) * 16], in_=GR[lo:hi, k]), reads=[ts], writes=[t_ssmw])
                B.op(DVE, lambda e, lo=lo, hi=hi, k=k, t=t: e.tensor_scalar(
                    out=woutS[lo:hi, :, 1, t * 16:(t + 1) * 16], in0=GI[lo:hi, k], scalar1=-1.0, scalar2=None,
                    op0=ALU.mult), reads=[ts], writes=[t_ssmw])
        ddt = alloc(512).rearrange("p (g h) -> p g h", g=32)
        winp_off = off[0]
        winp = alloc(32 * 2 * 128).rearrange("p (g r c) -> p g r c", g=32, r=2)
        for s in range(8):
            for (lo, hi, k) in ((0, 64, 7 - s), (64, 128, s)):
                V(lambda e, lo=lo, hi=hi, k=k, s=s: e.tensor_copy(out=winp[lo:hi, :, 0, s * 16:(s + 1) * 16], in_=TR[lo:hi, k]))
                V(lambda e, lo=lo, hi=hi, k=k, s=s: e.tensor_copy(out=winp[lo:hi, :, 1, s * 16:(s + 1) * 16], in_=TI[lo:hi, k]))
        for g in range(32):
            bank, bt = next_bank()
            for ri in range(2):
                B.op(PE, lambda e, g=g, ri=ri, bank=bank: e.transpose(
                    out=bank[:, ri * 128:(ri + 1) * 128], in_=winp[:, g, ri, :], identity=identf[:]),
                    reads=[ts, t_const], writes=[bt])
            B.op(ACT if g % 2 else DVE, (lambda e, g=g, bank=bank: e.tensor_copy(
                out=winT[:, g, :, :], in_=bank[:, 0:256].rearrange("p (r c) -> p r c", r=2))) if g % 2 == 0 else
                (lambda e, g=g, bank=bank: e.copy(
                    out=winT[:, g, :, :], in_=bank[:, 0:256].rearrange("p (r c) -> p r c", r=2))),
                reads=[bt], writes=[t_ssmw, bt])

        V(lambda e: e.tensor_scalar(out=tmp1, in0=TI[:, 0], scalar1=-1.0, scalar2=None, op0=ALU.mult))
        B.dma(SP, ddt[0:16], dd_d[:, :, :], writes=[ts])
        gbase = winp_off
        GallR = av(gbase, 16 * 240 * 4, F32).rearrange("p (g t h) -> p g t h", g=16, t=15)
        GallI = av(gbase + 15, 16 * 240 * 4, F32).rearrange("p (g t h) -> p g t h", g=16, t=15)
        kall = av(gbase + 30, 16 * 240 * 4, F32).rearrange("p (g c) -> p g c", g=16)
        kallbf = av(TI_off, 16 * 240 * 2).rearrange("p (g c) -> p g c", g=16)
        for gh in range(2):
            gs = slice(gh * 16, (gh + 1) * 16)
            V(lambda e: e.memset(GallR, 0.0))
            V(lambda e: e.memset(GallI, 0.0))
            for k in range(8):
                V(lambda e, k=k: e.tensor_copy(out=GallR[0:64, :, 7 + k, :], in_=GR[0:64, k, gs, :]))
                V(lambda e, k=k: e.tensor_copy(out=GallI[0:64, :, 7 + k, :], in_=GI[0:64, k, gs, :]))
                V(lambda e, k=k: e.tensor_copy(out=GallR[64:128, :, 7 - k, :], in_=GR[64:128, k, gs, :]))
                V(lambda e, k=k: e.tensor_copy(out=GallI[64:128, :, 7 - k, :], in_=GI[64:128, k, gs, :]))
            for g2 in range(8):
                bank, bt = next_bank()
                for gi in range(2):
                    gl = g2 * 2 + gi
                    g = gh * 16 + gl
                    B.op(PE, lambda e, g=g, gl=gl, gi=gi, bank=bank: e.matmul(
                        bank[0:16, gi * 240:(gi + 1) * 240], lhsT=TR[:, 0, g, :],
                        rhs=GallR[:, gl].rearrange("p t h -> p (t h)"), start=True, stop=False),
                        reads=[ts], writes=[bt])
                    B.op(PE, lambda e, g=g, gl=gl, gi=gi, bank=bank: e.matmul(
                        bank[0:16, gi * 240:(gi + 1) * 240], lhsT=tmp1[:, g, :],
                        rhs=GallI[:, gl].rearrange("p t h -> p (t h)"), start=False, stop=True),
                        reads=[ts], writes=[bt])
                B.op(DVE, lambda e, g2=g2, bank=bank: e.tensor_copy(
                    out=kall[0:16, g2 * 2:g2 * 2 + 2, :], in_=bank[0:16, 0:480].rearrange("p (g c) -> p g c", g=2)),
                    reads=[bt], writes=[ts, bt])
            V(lambda e: e.tensor_tensor(out=kall[0:16, :, 112:128], in0=kall[0:16, :, 112:128], in1=ddt[0:16, gs, :], op=ALU.add))
            V(lambda e: e.tensor_copy(out=kallbf[0:16], in_=kall[0:16]))
            B.dma(SP, kallb[:, gs, :], kallbf[0:16], reads=[ts], writes=[ts])
        for s in range(8):
            B.dma(SP, toepb[s * 16:(s + 1) * 16, :, :], kallb[:, :, (7 - s) * 16:(7 - s) * 16 + 128],
                  reads=[ts], writes=[ts])
        B.dma(SP, winTb[:, :], winT.rearrange("p g r c -> p (g r c)"), reads=[t_ssmw], writes=[ts])
        B.dma(SP, woutSb[:, :], woutS.rearrange("p g r c -> p (g r c)"), reads=[t_ssmw], writes=[ts])

        B.barrier()
        cT = av(0, 128, F32).rearrange("p (k n) -> p k n", k=8)
        sil = av(1, 128, F32).rearrange("p (k n) -> p k n", k=8)
        badar = av(2, 6 * D * 4, F32)
        wblk = [av(32 + 16 * i, 8 * 512 * 4, F32).rearrange("p (k c) -> p k c", k=8) for i in range(2)]
        wtok = [Tok(), Tok()]
        tm = Tok()
        B.dma(SP, cT, cT_d[:, :, :], writes=[tm])
        B.dma(SP, badar[0:1], bada_d[:, :], writes=[tm])
        B.op(ACT, lambda e: e.activation(out=sil, in_=cT, func=AF.Silu), reads=[tm], writes=[tm])
        wada_v = wada_d.rearrange("(k p) c -> p k c", p=128)
        mbank, mtok = next_bank()
        for blk in range(12):
            wb = wblk[blk % 2]
            B.dma(SP, wb, wada_v[:, :, blk * 512:(blk + 1) * 512], writes=[wtok[blk % 2]])
            for j in range(4):
                ct = blk * 4 + j
                o = (ct % 48) * 4
                B.op(PE, lambda e, ct=ct, o=o: e.matmul(
                    mbank[:, o:o + 4], lhsT=badar[0:1, ct * 128:(ct + 1) * 128], rhs=ones1[0:1, 0:4],
                    start=True, stop=False), reads=[tm, t_const], writes=[mtok])
                for kt in range(8):
                    B.op(PE, lambda e, wb=wb, j=j, kt=kt, o=o: e.matmul(
                        mbank[:, o:o + 4], lhsT=wb[:, kt, j * 128:(j + 1) * 128], rhs=sil[:, kt, :],
                        start=False, stop=(kt == 7)), reads=[tm, wtok[blk % 2]], writes=[mtok])
        B.op(DVE, lambda e: e.tensor_copy(out=modfm[:], in_=mbank[:, 0:192].rearrange("p (c n) -> p c n", n=4)),
             reads=[mtok], writes=[t_mod, mtok])
        nm = av(3, 32, F32); nf = av(3.5, 32, F32)
        B.dma(SP, nm, nmix_d[:, :], writes=[tm])
        B.dma(SP, nf, nffn_d[:, :], writes=[tm])
        for (dst, nrm, base) in ((sclmix, nm, 8), (sclffn, nf, 32)):
            B.op(DVE, lambda e, dst=dst, base=base: e.tensor_scalar(
                out=dst[:], in0=modfm[:, base:base + 8, :], scalar1=1.0, scalar2=None, op0=ALU.add),
                reads=[t_mod, tm], writes=[t_mod])
            B.op(DVE, lambda e, dst=dst, nrm=nrm: e.tensor_tensor(
                out=dst[:], in0=dst[:], in1=bcast_last(nrm, 4), op=ALU.mult), reads=[t_mod, tm], writes=[t_mod])
        for w, base in ((0, 16), (1, 40)):
            for n in range(NSEQ):
                B.dma(SP, gscr[n, w, :].rearrange("(k p) -> p k", p=128), modfm[:, base:base + 8, n],
                      reads=[t_mod], writes=[tm], allow_slow_non_contiguous=True)

        B.barrier()
        def conv_gen(jobs, bufs, D=2):
            nb = len(bufs)
            for k in range(len(jobs) + D):
                if k < len(jobs):
                    src, ncol, dst, vw = jobs[k]
                    st, stb, tk, tkb, E = bufs[k % nb]
                    B.dma(SP, st[:, 0:ncol], src, writes=[tk])
                    if E is ACT:
                        B.op(E, lambda e: e.copy(out=stb[:, 0:ncol], in_=st[:, 0:ncol]), reads=[tk], writes=[tkb])
                    else:
                        B.op(E, lambda e: e.tensor_copy(out=stb[:, 0:ncol], in_=st[:, 0:ncol]), reads=[tk], writes=[tkb])
                if k - D >= 0:
                    src, ncol, dst, vw = jobs[k - D]
                    st, stb, tk, tkb, E = bufs[(k - D) % nb]
                    B.dma(SP, dst, vw(stb[:, 0:ncol]), reads=[tkb])
                yield

        ident_v = lambda x: x
        fj = lambda x: x.rearrange("p (f j) -> p f j", j=128)
        jobs_a = []
        for kt in range(8):
            jobs_a.append((win_d[kt * 128:(kt + 1) * 128, :], 2048, winb[:, kt, :], ident_v))
        for kt in range(4):
            jobs_a.append((wglu_d[kt * 128:(kt + 1) * 128, :], 512, wglub[:, kt, :], ident_v))
        for kt in range(8):
            jobs_a.append((wout_d[kt * 128:(kt + 1) * 128, :], 1024, woutb[:, kt, :], ident_v))
        jobs_f = []
        for (wsrc, wdst) in ((wg_d, wgb), (wu_d, wub)):
            for kt in range(8):
                for (f0, f1) in ((0, 8), (8, 16), (16, 22)):
                    jobs_f.append((wsrc[kt * 128:(kt + 1) * 128, f0 * 128:f1 * 128], (f1 - f0) * 128,
                                   wdst[f0:f1, :, kt, :].rearrange("f p j -> p f j"), fj))
        for fc in range(NFC):
            jobs_f.append((wd_d[fc * 128:(fc + 1) * 128, :], 1024, wdb[fc, :, :], ident_v))
        bufs_a = [(av(0 + 8 * i, 2048 * 4, F32), av(32 + 4 * i, 2048 * 2), Tok(), Tok(), (ACT, DVE, POOL)[i]) for i in range(3)]
        for _ in conv_gen(jobs_a, bufs_a):
            pass
        bufs_f = [(av(10 + 6 * i, 1024 * 4, F32), av(14 + 6 * i, 1024 * 2), Tok(), Tok(), POOL) for i in range(3)]
        ffn_conv = conv_gen(jobs_f, bufs_f)

        B.barrier()
        m2t = av(64, 1024 * 4, F32)
        t2r = [av(72 + 4 * i, 1024 * 4, F32) for i in range(2)]
        t2tok = [Tok(), Tok()]
        tm2 = Tok()
        t_T2 = Tok()
        B.dma(SP, m2t, m2_d[:, :], writes=[tm2])
        for h in range(8):
            B.dma(SP, t2r[h % 2], t2raw_d[h, :, :], writes=[t2tok[h % 2]])
            B.op(DVE, lambda e, h=h: e.tensor_tensor(out=T2[:, h, :], in0=t2r[h % 2], in1=m2t, op=ALU.add),
                 reads=[t2tok[h % 2], tm2], writes=[t_T2])
        B.barrier()
        B.dma(SP, wglu[:], wglub[:, :, :], writes=[t_ssmw])

        hT = av(0, 32768).rearrange("p (k t) -> p k t", k=8)
        Xb = av(0, 32768).rearrange("p (c r g) -> p c r g", c=256, r=2)
        w_in = av(32, 32768).rearrange("p (k c) -> p k c", k=8)
        mixT = av(32, 32768).rearrange("p (k t) -> p k t", k=8)
        QT = av(64, 16384).rearrange("p (h t) -> p h t", h=4)
        KT = av(80, 16384).rearrange("p (h t) -> p h t", h=4)
        Vev = av(96, 16 * 8 * 65 * 2).rearrange("p (i h d) -> p i h d", i=16, h=8)
        Vod = av(112.5, 15 * 8 * 65 * 2).rearrange("p (i h d) -> p i h d", i=15, h=8)
        Utm = av(128, 16384).rearrange("p (a g s h) -> p a g s h", a=2, g=32, s=8)
        ytm = av(144, 16384).rearrange("p (i c) -> p i c", i=16)
        U8 = av(64, 16384).rearrange("p (g c) -> p g c", g=32)
        Sbufs = [av(80, 16384, F32).rearrange("p (c r g) -> p c r g", c=64, r=2),
                 av(96, 16384, F32).rearrange("p (c r g) -> p c r g", c=64, r=2)]
        Y8 = av(96, 16384).rearrange("p (g c) -> p g c", g=32)
        toep = av(112, 8192).rearrange("p (g c) -> p g c", g=32)
        w_out = av(0, 16384).rearrange("p (k c) -> p k c", k=8)
        h2T = av(16, 8192).rearrange("p (k t) -> p k t", k=8)
        hid = av(64, NFC * 512 * 2).rearrange("p (f t) -> p f t", f=NFC)
        x1 = av(86, 4 * 1024 * 4, F32).rearrange("p (t c) -> p t c", t=4)
        NWB = 3
        NWD = 2
        wgs = [av(102 + 4 * i, 2048).rearrange("p (k j) -> p k j", k=8) for i in range(NWB)]
        wus = [av(102 + 4 * i + 2, 2048).rearrange("p (k j) -> p k j", k=8) for i in range(NWB)]
        wds = [av(122 + 2 * i, 2048) for i in range(NWD)]
        gmb = av(24, 4096, F32)
        gfb = av(28, 4096, F32)
        nfb = av(118, 4096, F32)
        ftmp = av(126, 2048, F32)
        x1s = [x1, av(144, 4 * 1024 * 4, F32).rearrange("p (t c) -> p t c", t=4)]

        x_tiles = x_d

        t_statc = [Tok() for _ in range(8)]

        def rms_rstd(E, src_ap, ncol, col, reads, tokw):
            B.op(ACT, lambda e: e.activation(out=junk[:, 0:ncol], in_=src_ap, func=AF.Square,
                                             accum_out=ssq[:, col:col + 1]),
                 reads=list(reads) + [], writes=[t_statc[col], t_junk])
            B.op(ACT, lambda e: e.activation(out=rstd[:, col:col + 1], in_=ssq[:, col:col + 1], func=AF.Sqrt,
                                             bias=epst[:], scale=1.0 / ncol), reads=[t_statc[col], t_const], writes=[t_statc[col]])
            B.op(DVE, lambda e: e.reciprocal(out=rstd[:, col:col + 1], in_=rstd[:, col:col + 1]),
                 reads=[t_statc[col]], writes=[t_statc[col]])
            return t_statc[col]

        for n in range(NSEQ if 'MAIN' not in SKIP else 0):
            B.barrier()
            xtile = [av(144 + 4 * i, 4096, F32) for i in range(2)]
            xnb = [av(152 + 2 * i, 2048) for i in range(2)]
            xtok = [Tok(), Tok()]
            xntok = [Tok(), Tok()]
            t_hT = [Tok() for _ in range(16)]
            t_win = Tok()
            B.dma(SP, w_in, winb[:, :, :], writes=[t_win])
            for tt in range(16):
                i = tt % 2
                B.dma(SP, xtile[i], x_tiles[n, tt * 128:(tt + 1) * 128, :], writes=[xtok[i]])
                tsc = rms_rstd(ACT, xtile[i], 1024, i, [xtok[i]], None)
                B.op(DVE, lambda e, i=i: e.tensor_scalar(out=xnb[i], in0=xtile[i], scalar1=rstd[:, i:i + 1], scalar2=None,
                                                       op0=ALU.mult), reads=[xtok[i], tsc], writes=[xntok[i]])
                bank, bt = next_bank()
                pb = psbf(bank)
                for kt in range(8):
                    B.op(PE, lambda e, i=i, kt=kt, pb=pb: e.transpose(
                        out=pb[:, kt * 128:(kt + 1) * 128], in_=xnb[i][:, kt * 128:(kt + 1) * 128], identity=identb[:]),
                        reads=[xntok[i], t_const], writes=[bt])
                for kt in range(8):
                    if kt % 2 == 0:
                        B.op(DVE, lambda e, kt=kt, pb=pb, tt=tt: e.tensor_scalar(
                            out=hT[:, kt, tt * 128:(tt + 1) * 128], in0=pb[:, kt * 128:(kt + 1) * 128],
                            scalar1=sclmix[:, kt, n:n + 1], scalar2=modfm[:, kt, n:n + 1], op0=ALU.mult, op1=ALU.add),
                            reads=[bt, t_mod], writes=[t_hT[tt], bt])
                    else:
                        B.op(ACT, lambda e, kt=kt, pb=pb, tt=tt: e.activation(
                            out=hT[:, kt, tt * 128:(tt + 1) * 128], in_=pb[:, kt * 128:(kt + 1) * 128],
                            func=AF.Identity, scale=sclmix[:, kt, n:n + 1], bias=modfm[:, kt, n:n + 1]),
                            reads=[bt, t_mod], writes=[t_hT[tt], bt])
            t_Q = Tok(); t_K = Tok(); t_V = Tok(); t_U = Tok()
            flip = [0]

            def evac(out_ap, in_ap, reads, writes, scale=None):
                flip[0] ^= 1
                if flip[0]:
                    if scale is None:
                        B.op(DVE, lambda e: e.tensor_copy(out=out_ap, in_=in_ap), reads=reads, writes=writes)
                    else:
                        B.op(DVE, lambda e: e.tensor_scalar(out=out_ap, in0=in_ap, scalar1=scale, scalar2=None,
                                                          op0=ALU.mult), reads=reads, writes=writes)
                else:
                    if scale is None:
                        B.op(ACT, lambda e: e.copy(out=out_ap, in_=in_ap), reads=reads, writes=writes)
                    else:
                        B.op(ACT, lambda e: e.mul(out=out_ap, in_=in_ap, mul=scale), reads=reads, writes=writes)

            for (dst, cbase, tk, scl) in ((QT, 512, t_Q, 0.125), (KT, 1024, t_K, None)):
                for hp in range(4):
                    for tb in range(4):
                        bank, bt = next_bank()
                        for kt in range(8):
                            B.op(PE, lambda e, kt=kt, hp=hp, tb=tb, bank=bank, cbase=cbase: e.matmul(
                                bank[:, :], lhsT=w_in[:, kt, cbase + hp * 128:cbase + (hp + 1) * 128],
                                rhs=hT[:, kt, tb * 512:(tb + 1) * 512], start=(kt == 0), stop=(kt == 7)),
                                reads=[t_win] + t_hT[tb * 4:tb * 4 + 4], writes=[bt])
                        evac(dst[:, hp, tb * 512:(tb + 1) * 512], bank[:, :], [bt], [tk, bt], scale=scl)
            B.op(POOL, lambda e: e.memset(Vev[:, :, :, 64:65], 1.0), writes=[t_V])
            B.op(POOL, lambda e: e.memset(Vod[:, :, :, 64:65], 1.0), writes=[t_V])
            for (dst, ntile, tok0) in ((Vev, 16, 0), (Vod, 15, 64)):
                for i in range(ntile):
                    t0 = tok0 + i * 128
                    bank, bt = next_bank()
                    for kt in range(8):
                        B.op(PE, lambda e, kt=kt, t0=t0, bank=bank: e.matmul(
                            bank[:, :], lhsT=hT[:, kt, t0:t0 + 128], rhs=w_in[:, kt, 1536:2048],
                            start=(kt == 0), stop=(kt == 7)),
                            reads=[t_win] + t_hT[t0 // 128:(t0 + 127) // 128 + 1], writes=[bt])
                    evac(dst[:, i, :, 0:64], bank[:, :].rearrange("p (h d) -> p h d", h=8), [bt], [t_V, bt])
            for a in range(2):
                for s in range(8):
                    bank, bt = next_bank()
                    for kt in range(8):
                        hsl = hT[:, kt, a * 1024:(a + 1) * 1024].rearrange("p (c s) -> p c s", s=8)[:, :, s]
                        B.op(PE, lambda e, kt=kt, hsl=hsl, bank=bank: e.matmul(
                            bank[:, :], lhsT=hsl, rhs=w_in[:, kt, 0:512], start=(kt == 0), stop=(kt == 7)),
                            reads=[t_win] + t_hT[a * 8:a * 8 + 8], writes=[bt])
                    evac(Utm[:, a, :, s, :], bank[:, :].rearrange("p (g h) -> p g h", g=32), [bt], [t_U, bt])

            B.barrier()
            if debug and n == 0:
                B.dma(SP, dbg['d_hT'][:, :], av(0, 32768)); B.dma(SP, dbg['d_QT'][:, :], av(64, 16384)); B.dma(SP, dbg['d_KT'][:, :], av(80, 16384))
                B.dma(SP, dbg['d_Utm'][:, :], av(128, 16384)); B.dma(SP, dbg['d_Vev'][:, :], av(96, 16 * 8 * 65 * 2))
                B.dma(SP, dbg['d_mod'][:, :], modfm[:].rearrange("p c n -> p (c n)")); B.dma(SP, dbg['d_T2'][:, :], T2[:].rearrange("p h c -> p (h c)"))
                B.barrier()
            P3ON = 'P3' not in SKIP
            sct = [av(0 + 1 * i, 1024, F32).rearrange("p (t q) -> p t q", t=4) for i in range(4)]
            ptt = [av(4 + 0.5 * i, 512).rearrange("p (t q) -> p t q", t=4) for i in range(4)]
            sctok = [Tok() for _ in range(4)]
            pttok = [Tok() for _ in range(4)]
            ynb = av(8, 1024)
            t_yn = Tok()
            t_ytm = [Tok() for _ in range(16)]
            t_mix = [Tok() for _ in range(16)]
            pstate["n"] = 4
            pstate["i"] = 0
            NJ = 4
            LA = 2
            iters = []
            for i in range(16 if P3ON else 0):
                for hg in range(2):
                    for hl in range(4):
                        for a in range(2):
                            iters.append((i, hg, hl, a))
            sc_state = {}

            def stage_scores(idx):
                i, hg, hl, a = iters[idx]
                h = hg * 4 + hl
                hp, hb = h // 2, (h % 2) * 64
                r = 2 * i + a
                rs = min(max(r - 4, 0), 24)
                e0 = rs - r + 7
                j = idx % NJ
                sbank, sbt = next_bank()
                for t in range(4):
                    B.op(PE, lambda e, t=t: e.matmul(
                        sbank[:, t * 64:(t + 1) * 64],
                        lhsT=KT[hb:hb + 64, hp, (rs + 2 * t) * 64:(rs + 2 * t) * 64 + 128],
                        rhs=QT[hb:hb + 64, hp, r * 64:(r + 1) * 64], start=True, stop=True),
                        reads=[t_Q, t_K], writes=[sbt])
                t2v = T2[:, h, :].rearrange("p (e q) -> p e q", e=16)[:, e0:e0 + 7:2, :]
                B.op(DVE, lambda e: e.tensor_tensor(
                    out=sct[j], in0=sbank[:, 0:256].rearrange("p (t q) -> p t q", t=4), in1=t2v, op=ALU.add),
                    reads=[sbt, t_T2], writes=[sctok[j], sbt])
                B.op(ACT, lambda e: e.activation(out=ptt[j], in_=sct[j], func=AF.Exp),
                     reads=[sctok[j]], writes=[pttok[j]])

            def stage_pv(idx):
                i, hg, hl, a = iters[idx]
                h = hg * 4 + hl
                r = 2 * i + a
                rs = min(max(r - 4, 0), 24)
                Vt, vb = (Vev, rs // 2) if rs % 2 == 0 else (Vod, (rs - 1) // 2)
                j = idx % NJ
                bi = 4 + (i % 2) * 2 + hg
                pbank, pbt = psum[bi], ptok[bi]
                for t in range(4):
                    B.op(PE, lambda e, t=t: e.matmul(
                        pbank[a * 64:(a + 1) * 64, hl * 65:(hl + 1) * 65], lhsT=ptt[j][:, t, :],
                        rhs=Vt[:, vb + t, h, :], start=(t == 0), stop=(t == 3)),
                        reads=[pttok[j], t_V], writes=[pbt])
                if hg == 1 and hl == 3 and a == 1:
                    finish_rowpair(i)

            def finish_rowpair(i):
                for hg in range(2):
                    bi = 4 + (i % 2) * 2 + hg
                    pbank, pbt = psum[bi], ptok[bi]
                    pv = pbank[:, 0:260].rearrange("p (h d) -> p h d", h=4)
                    B.op(DVE, lambda e: e.reciprocal(out=rec[:, hg * 4:(hg + 1) * 4], in_=pv[:, :, 64]),
                         reads=[pbt], writes=[t_stat])
                    B.op(DVE, lambda e: e.tensor_tensor(
                        out=ytm[:, i, hg * 256:(hg + 1) * 256].rearrange("p (h d) -> p h d", h=4), in0=pv[:, :, 0:64],
                        in1=bcast_last(rec[:, hg * 4:(hg + 1) * 4], 64), op=ALU.mult),
                        reads=[pbt, t_stat], writes=[t_ytm[i], pbt])
                cc = 2 + (i % 2)
                tsc = rms_rstd(ACT, ytm[:, i, :], 512, cc, [t_ytm[i]], None)
                B.op(DVE, lambda e: e.tensor_scalar(out=ynb[:, 0:512], in0=ytm[:, i, :], scalar1=rstd[:, cc:cc + 1],
                                                  scalar2=None, op0=ALU.mult),
                     reads=[t_ytm[i], tsc], writes=[t_yn])
                bank, bt = next_bank()
                pb = psbf(bank)
                for jj in range(4):
                    B.op(PE, lambda e, jj=jj: e.transpose(out=pb[:, jj * 128:(jj + 1) * 128],
                                                          in_=ynb[:, jj * 128:(jj + 1) * 128], identity=identb[:]),
                         reads=[t_yn, t_const], writes=[bt])
                for jj in range(4):
                    B.op(ACT, lambda e, jj=jj: e.activation(
                        out=mixT[:, 4 + jj, i * 128:(i + 1) * 128], in_=pb[:, jj * 128:(jj + 1) * 128],
                        func=AF.Identity, scale=natt[:, jj:jj + 1]), reads=[bt, t_const], writes=[t_mix[i], bt])

            for idx in range(len(iters) + LA):
                if idx < len(iters):
                    stage_scores(idx)
                if idx >= LA and idx - LA < len(iters):
                    stage_pv(idx - LA)
                if n == 0 and idx % 3 == 0:
                    next(ffn_conv, None)
            if n == 0:
                for _ in ffn_conv:
                    pass

            pstate["n"] = 8
            B.barrier()
            if debug and n == 0:
                B.dma(SP, dbg['d_yatt'][:, :], av(144, 16384))
                B.barrier()
            t_U8 = Tok(); t_X = Tok(); t_Y8 = Tok(); t_toep = Tok(); t_xs = [Tok(), Tok()]
            B.dma(SP, toep, toepb[:, :, :], writes=[t_toep])
            winT = av(128, 16384).rearrange("p (g r c) -> p g r c", g=32, r=2)
            woutS = av(80, 16384).rearrange("p (g r c) -> p g r c", g=32, r=2)
            t_wi = Tok(); t_wo2 = Tok()
            for g in range(32):
                bank, bt = next_bank()
                pb = psbf(bank)
                for a in range(2):
                    B.op(PE, lambda e, g=g, a=a, pb=pb: e.transpose(
                        out=pb[:, a * 128:(a + 1) * 128], in_=Utm[:, a, g, :, :].rearrange("p s h -> p (s h)"), identity=identb[:]),
                        reads=[t_U, t_const], writes=[bt])
                evac(U8[:, g, :], pb[:, 0:256], [bt], [t_U8, bt])
            B.dma(SP, winT.rearrange("p g r c -> p (g r c)"), winTb[:, :], writes=[t_wi, t_U])
            B.op(DVE, lambda e: e.memset(xstate[:], 0.0), writes=t_xs)
            B.op(DVE, lambda e: e.memset(Xb[0:64, 0], 0.0), writes=[t_X])
            B.op(POOL, lambda e: e.memset(Xb[64:128, 255], 0.0), writes=[t_X])
            halves = ((0, 128, DVE, 0),)

            def u8rev(g, cb):
                t = U8[:, g, cb * 64:(cb + 1) * 64]
                return AP(t.tensor, t.offset + 63, [list(t.ap[0]), [-1, 64]])
            t_Sb = [Tok(), Tok()]
            NOWAIT = 'SCANWAIT' not in SKIP
            for jb in range(4):
                Sblk = Sbufs[jb % 2]
                t_S = [t_Sb[jb % 2], t_Sb[jb % 2]]
                for gq in range(8):
                    bank, bt = next_bank()
                    for gl in range(4):
                        g = gq * 4 + gl
                        for ri in range(2):
                            col = (gl * 2 + ri) * 64
                            B.op(PE, lambda e, g=g, ri=ri, col=col, bank=bank: e.matmul(
                                bank[0:64, col:col + 64], lhsT=winT[:, g, ri, 0:64],
                                rhs=U8[:, g, jb * 64:(jb + 1) * 64], start=True, stop=True),
                                reads=[t_U8, t_wi], writes=[bt])
                            B.op(PE, lambda e, g=g, ri=ri, col=col, bank=bank: e.matmul(
                                bank[64:128, col:col + 64], lhsT=winT[:, g, ri, 64:128],
                                rhs=u8rev(g, 3 - jb), start=True, stop=True),
                                reads=[t_U8, t_wi], writes=[bt])
                    src = bank[:, :].rearrange("p (g r c) -> p g r c", g=4, r=2)
                    dst = Sblk[:, :, :, gq * 4:(gq + 1) * 4].rearrange("p c r g -> p g r c")
                    B.op(ACT, lambda e, src=src, dst=dst: e.copy(out=dst, in_=src), reads=[bt], writes=[t_S[0], t_S[1], bt])
                if NOWAIT:
                    B.op(DVE, lambda e: e.tensor_copy(out=st1[:, :], in_=st1[:, :]), reads=[t_S[0], t_xs[0], t_ssmw], writes=[t_xs[0]])
                    DVE.same_wait = False
                for stp in range(64 if 'SCAN' not in SKIP else 0):
                    for (lo, hi, E, d) in halves:
                        c = stp
                        if stp == 0:
                            prev = xstate[lo:hi, :]
                        else:
                            prev = Sblk[lo:hi, c - 1].rearrange("p r g -> p (r g)")
                        pv = prev
                        psw = AP(pv.tensor, pv.offset + 32, [list(pv.ap[0]), [-32, 2], [1, 32]])
                        cur = Sblk[lo:hi, c].rearrange("p r g -> p (r g)")
                        rd = [t_S[d], t_xs[d], t_ssmw]
                        B.op(E, lambda e, lo=lo, hi=hi, pv=pv: e.tensor_tensor(
                            out=st1[lo:hi, :], in0=A1[lo:hi, :], in1=pv, op=ALU.mult), reads=rd, writes=[t_xs[d]])
                        B.op(E, lambda e, lo=lo, hi=hi, psw=psw: e.tensor_tensor(
                            out=st2[lo:hi, :].rearrange("p (r g) -> p r g", r=2),
                            in0=A2[lo:hi, :].rearrange("p (r g) -> p r g", r=2), in1=psw, op=ALU.mult),
                            reads=rd, writes=[t_xs[d]])
                        B.op(E, lambda e, lo=lo, hi=hi: e.tensor_tensor(
                            out=st1[lo:hi, :], in0=st1[lo:hi, :], in1=st2[lo:hi, :], op=ALU.add),
                            reads=[t_xs[d]], writes=[t_xs[d]])
                        B.op(E, lambda e, lo=lo, hi=hi, cur=cur: e.tensor_tensor(
                            out=cur, in0=st1[lo:hi, :], in1=cur, op=ALU.add), reads=[t_xs[d]], writes=[t_S[d], t_xs[d]])
                DVE.same_wait = SAME_WAIT
                cb = jb
                nn = 64 if cb < 3 else 63
                B.op(ACT, lambda e: e.copy(out=Xb[0:64, cb * 64 + 1:cb * 64 + 1 + nn], in_=Sblk[0:64, 0:nn]),
                     reads=[t_S[0]], writes=[t_X])
                cbw = 3 - jb
                nb = 64 if cbw > 0 else 63
                xb2 = av(0, 32768)[64:128, :]
                xrev = AP(xb2.tensor, xb2.offset + (cbw * 64 + 62) * 64, [list(xb2.ap[0]), [-64, nb], [1, 64]])
                B.op(DVE, lambda e: e.tensor_copy(out=xrev, in_=Sblk[64:128, 0:nb].rearrange("p c r g -> p c (r g)")),
                     reads=[t_S[0]], writes=[t_X])
                B.op(DVE, lambda e: e.tensor_copy(out=xstate[:, :], in_=Sblk[:, 63].rearrange("p r g -> p (r g)")),
                     reads=[t_S[0]], writes=[t_xs[0]])
            B.dma(SP, woutS.rearrange("p g r c -> p (g r c)"), woutSb[:, :], writes=[t_wo2, t_Sb[0]])
            for g in range(32):
                bank, bt = next_bank()
                B.op(PE, lambda e, g=g, bank=bank: e.matmul(bank[:, 0:256], lhsT=toep[:, g, :], rhs=U8[:, g, :],
                                                            start=True, stop=False), reads=[t_toep, t_U8], writes=[bt])
                for ri in range(2):
                    B.op(PE, lambda e, g=g, ri=ri, bank=bank: e.matmul(
                        bank[:, 0:256], lhsT=woutS[:, g, ri, :], rhs=Xb[:, :, ri, g], start=False, stop=(ri == 1)),
                        reads=[t_wo2, t_X], writes=[bt])
                evac(Y8[:, g, :], bank[:, 0:256], [bt], [t_Y8, t_Sb[1], bt])
            ytv = ytm.rearrange("p (a s) c -> p a s c", a=2)
            t_ysm = Tok()
            for g in range(32):
                bank, bt = next_bank()
                pb = psbf(bank)
                for a in range(2):
                    B.op(PE, lambda e, g=g, a=a, pb=pb: e.transpose(
                        out=pb[:, a * 128:(a + 1) * 128], in_=Y8[:, g, a * 128:(a + 1) * 128], identity=identb[:]),
                        reads=[t_Y8, t_const], writes=[bt])
                evac(ytv[:, :, :, g * 16:(g + 1) * 16], pb[:, 0:256].rearrange("p (a s h) -> p a s h", a=2, s=8),
                     [bt], [t_ysm, bt])
            B.barrier()
            if debug and n == 0:
                B.dma(SP, dbg['d_yssm'][:, :], av(144, 16384)); B.dma(SP, dbg['d_U8'][:, :], av(64, 16384))
                B.dma(SP, dbg['d_Xb'][:, :], av(0, 32768)); B.dma(SP, dbg['d_toep'][:, :], av(112, 8192)); B.dma(SP, dbg['d_Y8'][:, :], av(96, 16384))
                B.barrier()
            gsets = []
            for q in range(2):
                o = 64 + 11 * q
                gsets.append((av(o, 2048, F32), av(o + 2, 2048, F32), av(o + 4, 1024),
                              av(o + 5, 1024).rearrange("p (j t) -> p j t", j=4), av(o + 6, 2048, F32),
                              av(o + 8, 2048, F32), av(o + 10, 1024), Tok()))
            C0 = 1.5957691216057308
            for ti in range(16):
                a, s = ti // 8, ti % 8
                uu, ww, ygb, ygT, zz, y2, ynb2, tg = gsets[ti % 2]
                yv = ytm[:, ti, :]
                B.op(ACT, lambda e, yv=yv: e.activation(out=uu, in_=yv, func=AF.Square), reads=[t_ysm, t_toep], writes=[tg])
                B.op(DVE, lambda e: e.tensor_scalar(out=ww, in0=uu, scalar1=0.044715, scalar2=1.0, op0=ALU.mult, op1=ALU.add),
                     reads=[tg], writes=[tg])
                B.op(DVE, lambda e, yv=yv: e.tensor_tensor(out=ww, in0=ww, in1=yv, op=ALU.mult), reads=[tg, t_ysm], writes=[tg])
                B.op(ACT, lambda e: e.activation(out=uu, in_=ww, func=AF.Sigmoid, scale=C0), reads=[tg], writes=[tg])
                B.op(DVE, lambda e, yv=yv: e.tensor_tensor(out=ygb, in0=uu, in1=yv, op=ALU.mult), reads=[tg, t_ysm], writes=[tg])
                bank, bt = next_bank()
                pb = psbf(bank)
                for jj in range(4):
                    B.op(PE, lambda e, jj=jj, pb=pb: e.transpose(out=pb[:, jj * 128:(jj + 1) * 128],
                                                                 in_=ygb[:, jj * 128:(jj + 1) * 128], identity=identb[:]),
                         reads=[tg, t_const], writes=[bt])
                B.op(ACT, lambda e, pb=pb: e.copy(out=ygT, in_=pb[:, 0:512].rearrange("p (j t) -> p j t", j=4)),
                     reads=[bt], writes=[tg, bt])
                zb, zt = next_bank()
                for jj in range(4):
                    B.op(PE, lambda e, jj=jj, zb=zb: e.matmul(zb[:, :], lhsT=ygT[:, jj, :], rhs=wglu[:, jj, :],
                                                              start=(jj == 0), stop=(jj == 3)), reads=[tg, t_ssmw], writes=[zt])
                B.op(DVE, lambda e, zb=zb: e.tensor_tensor(out=zz, in0=zb[:, :], in1=bglu_sb[:], op=ALU.add),
                     reads=[zt, t_const], writes=[tg, zt])
                B.op(ACT, lambda e: e.activation(out=zz, in_=zz, func=AF.Sigmoid), reads=[tg], writes=[tg])
                B.op(DVE, lambda e: e.tensor_tensor(out=y2, in0=zz, in1=ygb, op=ALU.mult), reads=[tg], writes=[tg])
                cc = 4 + (ti % 2)
                tsc = rms_rstd(ACT, y2, 512, cc, [tg], None)
                B.op(DVE, lambda e: e.tensor_scalar(out=ynb2[:, 0:512], in0=y2, scalar1=rstd[:, cc:cc + 1], scalar2=None,
                                                  op0=ALU.mult), reads=[tg, tsc], writes=[tg])
                bank, bt = next_bank()
                pb = psbf(bank)
                for jj in range(4):
                    B.op(PE, lambda e, jj=jj, pb=pb: e.transpose(out=pb[:, jj * 128:(jj + 1) * 128],
                                                                 in_=ynb2[:, jj * 128:(jj + 1) * 128], identity=identb[:]),
                         reads=[tg, t_const], writes=[bt])
                for jj in range(4):
                    dstv = mixT[:, jj, a * 1024:(a + 1) * 1024].rearrange("p (c s) -> p c s", s=8)[:, :, s]
                    B.op(ACT, lambda e, jj=jj, pb=pb, dstv=dstv: e.activation(
                        out=dstv, in_=pb[:, jj * 128:(jj + 1) * 128], func=AF.Identity, scale=nssm[:, jj:jj + 1]),
                        reads=[bt, t_const], writes=[t_mix[0], bt])

            B.barrier()
            if debug and n == 0:
                B.dma(SP, dbg['d_mixT'][:, :], av(32, 32768))
                B.barrier()
            t_wo = Tok(); t_g = Tok()
            B.dma(SP, w_out, woutb[:, :, :], writes=[t_wo])
            for (dst, w) in ((gmb, 0), (gfb, 1)):
                src = gscr[n, w:w + 1, :]
                srcb = AP(src.tensor, src.offset, [[0, 128], [1, D]])
                B.dma(SP, dst, srcb, writes=[t_g])
            B.dma(SP, nfb, nfin_d[:, :], writes=[t_g])
            xl = [av(128 + 4 * i, 4096, F32) for i in range(2)]
            xltok = [Tok(), Tok()]
            xn5 = [av(136 + 2 * i, 2048) for i in range(4)]
            xn5tok = [Tok() for _ in range(4)]
            sgt = [av(114 + 2 * i, 2048, F32) for i in range(2)]
            sgtok = [Tok(), Tok()]
            t_ft = Tok()
            wgtok = [Tok() for _ in range(NWB)]; wdtok = [Tok() for _ in range(NWD)]
            t_x1 = [[Tok() for _ in range(4)] for _ in range(2)]
            t_h2 = [Tok() for _ in range(4)]
            t_hid = Tok()
            NB5 = 4 if 'P5' not in SKIP else 0

            def OPA(tb):
                xb = x1s[tb % 2]
                for tt in range(4):
                    T = tb * 4 + tt
                    i = T % 2
                    tk = t_x1[tb % 2][tt]
                    B.dma(POOL, xl[i], x_tiles[n, T * 128:(T + 1) * 128, :], writes=[xltok[i]])
                    for ob in range(2):
                        bank, bt = next_bank()
                        for kt in range(8):
                            B.op(PE, lambda e, kt=kt: e.matmul(
                                bank[:, :], lhsT=mixT[:, kt, T * 128:(T + 1) * 128],
                                rhs=w_out[:, kt, ob * 512:(ob + 1) * 512], start=(kt == 0), stop=(kt == 7)),
                                reads=[t_wo] + t_mix, writes=[bt])
                        B.op(DVE, lambda e: e.tensor_tensor(
                            out=xb[:, tt, ob * 512:(ob + 1) * 512], in0=bank[:, :], in1=gmb[:, ob * 512:(ob + 1) * 512],
                            op=ALU.mult), reads=[bt, t_g], writes=[tk, bt])
                        B.op(DVE, lambda e: e.tensor_tensor(
                            out=xb[:, tt, ob * 512:(ob + 1) * 512], in0=xb[:, tt, ob * 512:(ob + 1) * 512],
                            in1=xl[i][:, ob * 512:(ob + 1) * 512], op=ALU.add),
                            reads=[xltok[i]], writes=[tk])
                    tsc = rms_rstd(ACT, xb[:, tt, :], 1024, i, [tk], None)
                    B.op(DVE, lambda e: e.tensor_scalar(out=xn5[tt], in0=xb[:, tt, :], scalar1=rstd[:, i:i + 1],
                                                      scalar2=None, op0=ALU.mult),
                         reads=[tk, tsc], writes=[xn5tok[tt]])

            def OPB(tb):
                for tt in range(4):
                    bank, bt = next_bank()
                    pb = psbf(bank)
                    for kt in range(8):
                        B.op(PE, lambda e, kt=kt: e.transpose(
                            out=pb[:, kt * 128:(kt + 1) * 128], in_=xn5[tt][:, kt * 128:(kt + 1) * 128], identity=identb[:]),
                            reads=[xn5tok[tt], t_const], writes=[bt])
                    for kt in range(8):
                        if kt % 2 == 0:
                            B.op(DVE, lambda e, kt=kt: e.tensor_scalar(
                                out=h2T[:, kt, tt * 128:(tt + 1) * 128], in0=pb[:, kt * 128:(kt + 1) * 128],
                                scalar1=sclffn[:, kt, n:n + 1], scalar2=modfm[:, 24 + kt, n:n + 1],
                                op0=ALU.mult, op1=ALU.add), reads=[bt, t_mod], writes=[t_h2[tt], bt])
                        else:
                            B.op(ACT, lambda e, kt=kt: e.activation(
                                out=h2T[:, kt, tt * 128:(tt + 1) * 128], in_=pb[:, kt * 128:(kt + 1) * 128],
                                func=AF.Identity, scale=sclffn[:, kt, n:n + 1], bias=modfm[:, 24 + kt, n:n + 1]),
                                reads=[bt, t_mod], writes=[t_h2[tt], bt])

            def GU(tb, hook=None):
                for fc in range(NFC):
                    i = fc % NWB
                    B.dma(SP, wgs[i], wgb[fc, :, :, :], writes=[wgtok[i]])
                    B.dma(SP, wus[i], wub[fc, :, :, :], writes=[wgtok[i]])
                    gb, gt = next_bank()
                    for kt in range(8):
                        B.op(PE, lambda e, kt=kt: e.matmul(gb[:, :], lhsT=wgs[i][:, kt, :], rhs=h2T[:, kt, :],
                                                           start=(kt == 0), stop=(kt == 7)),
                             reads=[wgtok[i]] + t_h2, writes=[gt])
                    ub, ut = next_bank()
                    for kt in range(8):
                        B.op(PE, lambda e, kt=kt: e.matmul(ub[:, :], lhsT=wus[i][:, kt, :], rhs=h2T[:, kt, :],
                                                           start=(kt == 0), stop=(kt == 7)),
                             reads=[wgtok[i]] + t_h2, writes=[ut])
                    si = fc % 2
                    B.op(ACT, lambda e: e.activation(out=sgt[si], in_=gb[:, :], func=AF.Silu),
                         reads=[gt], writes=[sgtok[si], gt])
                    B.op(DVE, lambda e: e.tensor_tensor(out=hid[:, fc, :], in0=sgt[si], in1=ub[:, :], op=ALU.mult),
                         reads=[sgtok[si], ut], writes=[t_hid, ut])
                    if hook is not None:
                        hook(fc)

            def DOWN(tb):
                for fc in range(NFC):
                    i = fc % NWD
                    B.dma(SP, wds[i], wdb[fc, :, :], writes=[wdtok[i]])
                    for tt in range(4):
                        for ob in range(2):
                            bi = tt * 2 + ob
                            B.op(PE, lambda e: e.matmul(
                                psum[bi][:, :], lhsT=hid[:, fc, tt * 128:(tt + 1) * 128],
                                rhs=wds[i][:, ob * 512:(ob + 1) * 512], start=(fc == 0), stop=(fc == NFC - 1)),
                                reads=[wdtok[i], t_hid], writes=[ptok[bi]])

            def FINAL_evac(tb):
                xb = x1s[tb % 2]
                for tt in range(4):
                    tk = t_x1[tb % 2][tt]
                    for ob in range(2):
                        bi = tt * 2 + ob
                        B.op(DVE, lambda e: e.tensor_tensor(
                            out=ftmp, in0=psum[bi][:, :], in1=gfb[:, ob * 512:(ob + 1) * 512], op=ALU.mult),
                            reads=[ptok[bi], t_g], writes=[t_ft, ptok[bi]])
                        B.op(DVE, lambda e: e.tensor_tensor(
                            out=xb[:, tt, ob * 512:(ob + 1) * 512], in0=ftmp,
                            in1=xb[:, tt, ob * 512:(ob + 1) * 512], op=ALU.add),
                            reads=[t_ft], writes=[tk])

            def FINAL_norm(tb, tt):
                xb = x1s[tb % 2]
                tk = t_x1[tb % 2][tt]
                T = tb * 4 + tt
                i = T % 2
                tsc = rms_rstd(ACT, xb[:, tt, :], 1024, 6 + i, [tk], None)
                B.op(DVE, lambda e: e.scalar_tensor_tensor(
                    out=xb[:, tt, :], in0=xb[:, tt, :], scalar=rstd[:, 6 + i:7 + i], in1=nfb, op0=ALU.mult, op1=ALU.mult),
                    reads=[tsc, t_g], writes=[tk])
                B.dma(POOL, y_d[n, T * 128:(T + 1) * 128, :], xb[:, tt, :], reads=[tk])

            pstate["i"] = 0
            if NB5:
                OPA(0)
                OPB(0)
            for tb in range(NB5):
                if tb == 0:
                    GU(tb)
                else:
                    GU(tb, hook=lambda fc, tb=tb: FINAL_norm(tb - 1, (fc - 1) // 2) if (fc % 2 == 1 and fc < 8) else None)
                if tb + 1 < NB5:
                    OPA(tb + 1)
                DOWN(tb)
                FINAL_evac(tb)
                if tb + 1 < NB5:
                    OPB(tb + 1)
                else:
                    for tt in range(4):
                        FINAL_norm(tb, tt)
        B.finish()
    return nc


_bglu_holder = {}


def _layout_inputs(inp):
    f = np.float32
    out = {}
    xs = np.concatenate([inp["x_prompt"], inp["x_sample"]], axis=0)
    cs = np.concatenate([inp["c_prompt"], inp["c_sample"]], axis=0)
    shared = {}
    shared["w_ada"] = np.ascontiguousarray(inp["w_ada"][0], dtype=f)
    shared["b_ada"] = np.ascontiguousarray(inp["b_ada"][0:1], dtype=f)
    shared["nmix"] = np.ascontiguousarray(inp["norm_mix"][0].reshape(8, 128).T, dtype=f)
    shared["nffn"] = np.ascontiguousarray(inp["norm_ffn"][0].reshape(8, 128).T, dtype=f)
    shared["nfin"] = np.ascontiguousarray(np.broadcast_to(inp["norm_final"][None, :], (128, D)), dtype=f)
    shared["nssm"] = np.ascontiguousarray(inp["norm_ssm_out"][0].reshape(4, 128).T, dtype=f)
    shared["natt"] = np.ascontiguousarray(inp["norm_attn_out"][0].reshape(4, 128).T, dtype=f)
    shared["bglu"] = np.ascontiguousarray(np.broadcast_to(inp["b_glu"][0][None, :], (128, 512)), dtype=f)
    shared["w_in"] = np.ascontiguousarray(inp["w_in"][0], dtype=f)
    shared["w_glu"] = np.ascontiguousarray(inp["w_glu"][0], dtype=f)
    shared["w_out"] = np.ascontiguousarray(inp["w_out"][0], dtype=f)
    shared["w_g"] = np.ascontiguousarray(inp["w_ffn_gate"][0], dtype=f)
    shared["w_u"] = np.ascontiguousarray(inp["w_ffn_up"][0], dtype=f)
    shared["w_d"] = np.ascontiguousarray(inp["w_ffn_down"][0], dtype=f)

    def dpg(a):
        return np.ascontiguousarray(a.transpose(0, 2, 1).reshape(128, 32), dtype=f)

    shared["are"] = dpg(inp["ssm_a_re"][0])
    shared["aim"] = dpg(inp["ssm_a_im"][0])
    shared["ldt"] = dpg(np.broadcast_to(inp["ssm_log_dt"][0][:, :, None], (2, 32, 64)))
    shared["bre"] = np.ascontiguousarray(inp["ssm_b_re"][0].transpose(0, 2, 1, 3).reshape(128, 32, 16), dtype=f)
    shared["bim"] = np.ascontiguousarray(inp["ssm_b_im"][0].transpose(0, 2, 1, 3).reshape(128, 32, 16), dtype=f)
    shared["cre"] = np.ascontiguousarray(inp["ssm_c_re"][0].transpose(0, 3, 1, 2).reshape(128, 32, 16), dtype=f)
    shared["cim"] = np.ascontiguousarray(inp["ssm_c_im"][0].transpose(0, 3, 1, 2).reshape(128, 32, 16), dtype=f)
    dvec = inp["ssm_d"][0].reshape(32, 16)
    dd = np.zeros((16, 32, 16), f)
    for h in range(16):
        dd[h, :, h] = dvec[:, h]
    shared["dd"] = dd
    rpb = inp["na_rpb"][0]
    b = np.arange(2)[:, None, None, None]
    kc = np.arange(64)[None, :, None, None]
    e = np.arange(16)[None, None, :, None]
    qc = np.arange(64)[None, None, None, :]
    ri = np.clip(e + b, 0, 14) + 0 * kc + 0 * qc
    ci = np.clip(kc - qc + 15, 0, 30) + 0 * e + 0 * b
    t2raw = rpb[:, ri, ci].reshape(8, 128, 16 * 64)
    shared["t2raw"] = np.ascontiguousarray(t2raw, dtype=f)
    cstart = np.clip(qc - 8, 0, 48)
    valid = (kc >= cstart) & (kc < cstart + 16) & ((e + b) <= 14)
    m2 = np.where(valid, 0.0, NEG).astype(f) + np.zeros((2, 64, 16, 64), f)
    shared["m2"] = np.ascontiguousarray(m2.reshape(128, 16 * 64))
    shared["identf"] = np.eye(128, dtype=f)
    in_maps = []
    for core in range(8):
        m = dict(shared)
        m["x"] = np.ascontiguousarray(xs[core * 3:(core + 1) * 3], dtype=f)
        cc = cs[core * 3:(core + 1) * 3]
        cT = np.zeros((128, 8, 4), f)
        cT[:, :, 0:3] = cc.reshape(3, 8, 128).transpose(2, 1, 0)
        m["cT"] = cT
        in_maps.append(m)
    return in_maps


def kernel(**inputs):
    inp = {k: np.asarray(v) for k, v in inputs.items()}
    in_maps = _layout_inputs(inp)
    nc = build_nc()
    res = run_bass_kernel_spmd(nc, in_maps, core_ids=list(range(8)))
    ys = np.concatenate([np.asarray(r["y"]).reshape(NSEQ, L, D) for r in res.results], axis=0).astype(np.float32)
    return ys[0:8], ys[8:24]
```
